# Optimizing a Trainium2 kernel written in Bass

```python
import math
import jax, jax.numpy as jnp
from jax import lax
import numpy as np

D_MODEL = 1024
BATCH = 8
SEQ = 2048
DEPTH = 2

PE_DIM = 256
EPS = 1e-6
NEG = -1e30
F32 = jnp.float32

CHUNK = 128
RET_HEADS = 4
RET_DK = 128
RET_DV = 256
ML_HEADS = 4
ML_DK = 128
ML_DV = 256
ML_CONV = 4
ML_F_BIAS_LO = 3.0
ML_F_BIAS_HI = 6.0
AB_SIZES = (RET_HEADS * RET_DK, RET_HEADS * RET_DK, RET_HEADS * RET_DV, RET_HEADS * RET_DV,
            ML_HEADS * ML_DK, ML_HEADS * ML_DK, ML_HEADS * ML_DV, ML_HEADS, ML_HEADS,
            ML_HEADS * ML_DV, ML_HEADS * ML_DV)
AB_IN = sum(AB_SIZES)
AB_MIX = RET_HEADS * RET_DV + ML_HEADS * ML_DV

NSA_HEADS = 16
NSA_GROUPS = 4
NSA_HPG = NSA_HEADS // NSA_GROUPS
NSA_DH = 64
CMP_STRIDE = 16
CMP_LEN = 2 * CMP_STRIDE
CMP_HIDDEN = 128
SEL_BLOCK = 64
SEL_TOPN = 4
WINDOW = 512
Q_BLOCK = 64
NSA_W = NSA_HEADS * NSA_DH
NSA_KV = NSA_GROUPS * NSA_DH
S5_GROUPS = 32
S5_GROUP_CH = 16
S5_STATE = 64
S5_WIDTH = S5_GROUPS * S5_GROUP_CH
CD_SIZES = (NSA_W, NSA_KV, NSA_KV, NSA_KV, NSA_KV, NSA_KV, NSA_KV, 3 * NSA_HEADS, NSA_W,
            S5_WIDTH, S5_WIDTH)
CD_IN = sum(CD_SIZES)
CD_MIX = NSA_W + S5_WIDTH

kernel_name = "hybrid_ret_mlstm_nsa_s5_sandwich"


def _split(a, sizes):
    offs = np.cumsum(sizes)[:-1].tolist()
    return jnp.split(a, offs, axis=-1)


def rmsnorm(x, g):
    xf = x.astype(F32)
    y = xf * lax.rsqrt(jnp.mean(xf * xf, -1, keepdims=True) + EPS)
    return (y * g.astype(F32)).astype(x.dtype)


def head_norm(y, g):
    yc = y - y.mean(-1, keepdims=True)
    yn = yc * lax.rsqrt(jnp.mean(yc * yc, -1, keepdims=True) + EPS)
    b, s, h, d = y.shape
    return yn.reshape(b, s, h * d) * g.astype(F32)


def causal_dwconv(x, w, b):
    k, c = w.shape
    out = lax.conv_general_dilated(x, w[:, None, :], window_strides=(1,), padding=[(k - 1, 0)],
                                   dimension_numbers=('NWC', 'WIO', 'NWC'), feature_group_count=c)
    return out + b


def alibi_slopes(n):
    return 2.0 ** (-8.0 * jnp.arange(1, n + 1, dtype=F32) / n)


def retention(q, k, v):
    b_, s_, h, dk = q.shape
    dv = v.shape[-1]
    L = CHUNK
    nc = s_ // L
    log_g = jnp.log1p(-(2.0 ** (-5.0 - jnp.arange(h, dtype=F32))))
    qc = (q * dk ** -0.5).reshape(b_, nc, L, h, dk)
    kc = k.reshape(b_, nc, L, h, dk)
    vc = v.reshape(b_, nc, L, h, dv)
    pos = jnp.arange(L, dtype=F32)
    diff = pos[:, None] - pos[None, :]
    decay = jnp.where(diff >= 0, jnp.exp(jnp.maximum(diff, 0.0)[None] * log_g[:, None, None]), 0.0)
    sc = jnp.einsum('bcihd,bcjhd->bchij', qc, kc) * decay
    y_intra = jnp.einsum('bchij,bcjhe->bcihe', sc, vc)
    zeta = jnp.exp((L - 1 - pos)[:, None] * log_g)
    kv = jnp.einsum('bcjhd,bcjhe->bchde', kc * zeta[:, :, None], vc)
    chunk_decay = jnp.exp(L * log_g)[:, None, None]

    def step(r, kv_c):
        return r * chunk_decay + kv_c, r

    _, r_prev = lax.scan(step, jnp.zeros((b_, h, dk, dv), F32), jnp.moveaxis(kv, 1, 0))
    r_prev = jnp.moveaxis(r_prev, 0, 1)
    xi = jnp.exp((pos + 1.0)[:, None] * log_g)
    y_cross = jnp.einsum('bcihd,bchde->bcihe', qc * xi[:, :, None], r_prev)
    return (y_intra + y_cross).reshape(b_, s_, h, dv)


def mlstm(q, k, v, i_pre, f_pre, o_pre):
    b_, s_, h, dk = q.shape
    dv = v.shape[-1]
    L = CHUNK
    nc = s_ // L
    qc = q.reshape(b_, nc, L, h, dk)
    kc = (k * dk ** -0.5).reshape(b_, nc, L, h, dk)
    vc = v.reshape(b_, nc, L, h, dv)
    logf = jax.nn.log_sigmoid(f_pre).reshape(b_, nc, L, h).transpose(0, 1, 3, 2)
    ig = i_pre.reshape(b_, nc, L, h).transpose(0, 1, 3, 2)
    bcum = jnp.cumsum(logf, axis=-1)
    b_last = bcum[..., -1]
    a = b_last[..., None] - bcum + ig
    mu = a.max(-1)
    w = jnp.exp(a - mu[..., None])
    kw = kc * w.transpose(0, 1, 3, 2)[..., None]
    kv = jnp.einsum('bcshd,bcshe->bchde', kw, vc)
    ksum = kw.sum(2)

    def step(carry, inp):
        c_st, n_st, m_st = carry
        kv_c, ks_c, mu_c, bl_c = inp
        m_new = jnp.maximum(bl_c + m_st, mu_c)
        sp = jnp.exp(bl_c + m_st - m_new)
        sc = jnp.exp(mu_c - m_new)
        c_new = sp[..., None, None] * c_st + sc[..., None, None] * kv_c
        n_new = sp[..., None] * n_st + sc[..., None] * ks_c
        return (c_new, n_new, m_new), (c_st, n_st, m_st)

    init = (jnp.zeros((b_, h, dk, dv), F32), jnp.zeros((b_, h, dk), F32), jnp.full((b_, h), NEG, F32))
    _, (c_prev, n_prev, m_prev) = lax.scan(
        step, init, (jnp.moveaxis(kv, 1, 0), jnp.moveaxis(ksum, 1, 0), jnp.moveaxis(mu, 1, 0), jnp.moveaxis(b_last, 1, 0)))
    c_prev = jnp.moveaxis(c_prev, 0, 1)
    n_prev = jnp.moveaxis(n_prev, 0, 1)
    m_prev = jnp.moveaxis(m_prev, 0, 1)
    causal = jnp.tril(jnp.ones((L, L), bool))
    log_d = bcum[..., :, None] - bcum[..., None, :] + ig[..., None, :]
    log_d = jnp.where(causal, log_d, -jnp.inf)
    inter = bcum + m_prev[..., None]
    m_t = jnp.maximum(inter, log_d.max(-1))
    dmat = jnp.exp(log_d - m_t[..., None])
    w_inter = jnp.exp(inter - m_t).transpose(0, 1, 3, 2)
    s = jnp.einsum('bcthd,bcshd->bchts', qc, kc) * dmat
    num = jnp.einsum('bchts,bcshe->bcthe', s, vc) + w_inter[..., None] * jnp.einsum('bcthd,bchde->bcthe', qc, c_prev)
    den = s.sum(-1).transpose(0, 1, 3, 2) + w_inter * jnp.einsum('bcthd,bchd->bcth', qc, n_prev)
    den = jnp.maximum(jnp.abs(den), jnp.exp(-m_t.transpose(0, 1, 3, 2)))
    hcell = (num / den[..., None]).reshape(b_, s_, h, dv)
    return jax.nn.sigmoid(o_pre) * hcell


def compress(kx, pos, w1, w2):
    b_, s_, g, dh = kx.shape
    ch = kx.reshape(b_, s_ // CMP_STRIDE, CMP_STRIDE, g, dh)
    blocks = jnp.concatenate([ch[:, :-1], ch[:, 1:]], axis=2) + pos[None, None, :, None, :]
    nb = blocks.shape[1]
    flat = blocks.transpose(0, 1, 3, 2, 4).reshape(b_, nb, g, CMP_LEN * dh)
    return jax.nn.gelu(flat @ w1) @ w2


def nsa(q, k_cmp, v_cmp, k_slc, v_slc, k_win, v_win, gate_pre, pos_k, pos_v, w1_k, w2_k, w1_v, w2_v):
    b_, s_, h, dh = q.shape
    g, hpg = NSA_GROUPS, NSA_HPG
    t = jnp.arange(s_)
    slopes = alibi_slopes(h).reshape(g, hpg)
    qg = (q * dh ** -0.5).reshape(b_, s_, g, hpg, dh)
    kc = compress(k_cmp, pos_k, w1_k, w2_k)
    vc = compress(v_cmp, pos_v, w1_v, w2_v)
    nb = kc.shape[1]
    cstart = jnp.arange(nb) * CMP_STRIDE
    dist_c = (t[:, None] - (cstart + CMP_LEN - 1)[None, :]).astype(F32)
    s_c = jnp.einsum('btgkd,bjgd->bgktj', qg, kc) - slopes[:, :, None, None] * dist_c
    s_c = jnp.where(dist_c >= 0, s_c, NEG)
    p_c = jax.nn.softmax(s_c, axis=-1)
    p_c = jnp.where((t >= CMP_LEN - 1)[:, None], p_c, 0.0)
    o_c = jnp.einsum('bgktj,bjgd->btgkd', p_c, vc)
    nsel = s_ // SEL_BLOCK
    topn = min(SEL_TOPN, nsel)
    sel = jnp.arange(nsel)
    overlap = ((cstart[:, None] < (sel[None, :] + 1) * SEL_BLOCK)
               & (cstart[:, None] + CMP_LEN > sel[None, :] * SEL_BLOCK)).astype(F32)
    imp = jnp.einsum('bgktj,jn->bgtn', p_c, overlap)
    valid = sel[None, :] * SEL_BLOCK <= t[:, None]
    forced = (sel[None, :] == 0) | (sel[None, :] == (t // SEL_BLOCK)[:, None])
    score = jnp.where(forced, jnp.inf, jnp.where(valid, imp, -1.0))
    _, idx = lax.top_k(score, topn)
    ks_blk = k_slc.reshape(b_, nsel, SEL_BLOCK, g, dh).transpose(0, 3, 1, 2, 4)
    vs_blk = v_slc.reshape(b_, nsel, SEL_BLOCK, g, dh).transpose(0, 3, 1, 2, 4)
    kw_pad = jnp.pad(k_win, ((0, 0), (WINDOW, 0), (0, 0), (0, 0)))
    vw_pad = jnp.pad(v_win, ((0, 0), (WINDOW, 0), (0, 0), (0, 0)))
    nq = s_ // Q_BLOCK
    q_blocks = qg.reshape(b_, nq, Q_BLOCK, g, hpg, dh).transpose(1, 0, 2, 3, 4, 5)
    idx_blocks = idx.reshape(b_, g, nq, Q_BLOCK, topn).transpose(2, 0, 1, 3, 4)
    bi = jnp.arange(b_)[:, None, None, None]
    gi = jnp.arange(g)[None, :, None, None]
    span = WINDOW + Q_BLOCK

    def block(args):
        qb, ib, nblk = args
        t0 = nblk * Q_BLOCK
        tq = t0 + jnp.arange(Q_BLOCK)
        k_sel = ks_blk[bi, gi, ib]
        v_sel = vs_blk[bi, gi, ib]
        kpos = ib[..., None] * SEL_BLOCK + jnp.arange(SEL_BLOCK)
        dist = (tq[None, None, :, None, None] - kpos).astype(F32)
        s = jnp.einsum('bqgkd,bgqnsd->bgkqns', qb, k_sel) - slopes[None, :, :, None, None, None] * dist[:, :, None]
        s = jnp.where(dist[:, :, None] >= 0, s, NEG)
        p = jax.nn.softmax(s.reshape(b_, g, hpg, Q_BLOCK, topn * SEL_BLOCK), axis=-1).reshape(s.shape)
        o_s = jnp.einsum('bgkqns,bgqnsd->bqgkd', p, v_sel)
        kwb = lax.dynamic_slice_in_dim(kw_pad, t0, span, axis=1)
        vwb = lax.dynamic_slice_in_dim(vw_pad, t0, span, axis=1)
        kpos_w = t0 - WINDOW + jnp.arange(span)
        dist_w = tq[:, None] - kpos_w[None, :]
        valid_w = (dist_w >= 0) & (dist_w < WINDOW) & (kpos_w[None, :] >= 0)
        s_w = jnp.einsum('bqgkd,bsgd->bgkqs', qb, kwb) - slopes[:, :, None, None] * dist_w.astype(F32)
        s_w = jnp.where(valid_w, s_w, NEG)
        p_w = jax.nn.softmax(s_w, axis=-1)
        o_w = jnp.einsum('bgkqs,bsgd->bqgkd', p_w, vwb)
        return o_s, o_w

    o_s, o_w = lax.map(block, (q_blocks, idx_blocks, jnp.arange(nq)))
    o_s = o_s.transpose(1, 0, 2, 3, 4, 5).reshape(b_, s_, g, hpg, dh)
    o_w = o_w.transpose(1, 0, 2, 3, 4, 5).reshape(b_, s_, g, hpg, dh)
    gt = jax.nn.sigmoid(gate_pre).reshape(b_, s_, g, hpg, 3)
    o = gt[..., 0:1] * o_c + gt[..., 1:2] * o_s + gt[..., 2:3] * o_w
    return o.reshape(b_, s_, h * dh)


def s5(u, a_re, a_im, log_dt, b_re, b_im, c_re, c_im, d_skip, w_glu):
    b_, s_, wdt = u.shape
    g, c, pst = S5_GROUPS, S5_GROUP_CH, S5_STATE
    dt = jnp.exp(log_dt)[:, None]
    er = jnp.exp(a_re * dt)
    abar_re = er * jnp.cos(a_im * dt)
    abar_im = er * jnp.sin(a_im * dt)
    lam2 = a_re * a_re + a_im * a_im
    cr = ((abar_re - 1.0) * a_re + abar_im * a_im) / lam2
    ci = (abar_im * a_re - (abar_re - 1.0) * a_im) / lam2
    bb_re = cr[..., None] * b_re - ci[..., None] * b_im
    bb_im = cr[..., None] * b_im + ci[..., None] * b_re
    ug = u.reshape(b_, s_, g, c)
    bu_re = jnp.einsum('bsgc,gpc->bsgp', ug, bb_re)
    bu_im = jnp.einsum('bsgc,gpc->bsgp', ug, bb_im)
    ar = jnp.broadcast_to(abar_re, (1, s_, g, pst))
    ai = jnp.broadcast_to(abar_im, (1, s_, g, pst))

    def combine(e1, e2):
        a1r, a1i, b1r, b1i = e1
        a2r, a2i, b2r, b2i = e2
        return (a2r * a1r - a2i * a1i, a2r * a1i + a2i * a1r,
                a2r * b1r - a2i * b1i + b2r, a2r * b1i + a2i * b1r + b2i)

    _, _, xr, xi = lax.associative_scan(combine, (ar, ai, bu_re, bu_im), axis=1)
    y = jnp.einsum('bsgp,gcp->bsgc', xr, c_re) - jnp.einsum('bsgp,gcp->bsgc', xi, c_im)
    y = jax.nn.gelu(y.reshape(b_, s_, wdt) + d_skip * u)
    z1, z2 = jnp.split(y @ w_glu, 2, axis=-1)
    return z1 * jax.nn.sigmoid(z2)


def ab_mixer(u, w_in, ret_norm, conv_w, conv_b, gate_b, ml_norm, w_out):
    b_, s_, _ = u.shape
    proj = (u @ w_in).astype(F32)
    rq, rk, rv, rg, mq, mk, mv, mi, mf, mo, mz = _split(proj, AB_SIZES)
    hd = lambda a, n: a.reshape(b_, s_, n, -1)
    y_ret = retention(hd(rq, RET_HEADS), hd(rk, RET_HEADS), hd(rv, RET_HEADS))
    y_ret = head_norm(y_ret, ret_norm) * jax.nn.silu(rg)
    mqk = jax.nn.silu(causal_dwconv(jnp.concatenate([mq, mk], -1), conv_w.astype(F32), conv_b.astype(F32)))
    mq, mk = jnp.split(mqk, 2, axis=-1)
    gb = gate_b.astype(F32)
    y_ml = mlstm(hd(mq, ML_HEADS), hd(mk, ML_HEADS), hd(mv, ML_HEADS),
                 mi + gb[:ML_HEADS], mf + gb[ML_HEADS:], hd(mo, ML_HEADS))
    y_ml = head_norm(y_ml, ml_norm) * jax.nn.silu(mz)
    mix = jnp.concatenate([y_ret, y_ml], -1).astype(u.dtype)
    return mix @ w_out


def cd_mixer(u, w_in, pos_k, pos_v, w1_k, w2_k, w1_v, w2_v, a_re, a_im, log_dt,
             b_re, b_im, c_re, c_im, d_skip, w_glu, w_out):
    b_, s_, _ = u.shape
    proj = (u @ w_in).astype(F32)
    nq, kc, vc, ks, vs, kw, vw, ng, nz, su, sz = _split(proj, CD_SIZES)
    kvh = lambda a: a.reshape(b_, s_, NSA_GROUPS, NSA_DH)
    y_nsa = nsa(nq.reshape(b_, s_, NSA_HEADS, NSA_DH), kvh(kc), kvh(vc), kvh(ks), kvh(vs), kvh(kw), kvh(vw), ng,
                pos_k.astype(F32), pos_v.astype(F32), w1_k.astype(F32), w2_k.astype(F32),
                w1_v.astype(F32), w2_v.astype(F32))
    y_nsa = y_nsa * jax.nn.silu(nz)
    y_s5 = s5(su, a_re.astype(F32), a_im.astype(F32), log_dt.astype(F32), b_re.astype(F32), b_im.astype(F32),
              c_re.astype(F32), c_im.astype(F32), d_skip.astype(F32), w_glu.astype(F32))
    y_s5 = y_s5 * jax.nn.silu(sz)
    mix = jnp.concatenate([y_nsa, y_s5], -1).astype(u.dtype)
    return mix @ w_out


def setup_inputs(seed: int = 0) -> dict:
    key = jax.random.key(seed)
    keys = list(jax.random.split(key, 40))
    n_even = (DEPTH + 1) // 2
    n_odd = DEPTH // 2

    def nrm(shape, scale):
        return jax.random.normal(keys.pop(), shape, F32) * scale

    def gain(shape):
        return 1.0 + nrm(shape, 0.02)

    d = D_MODEL
    inp = {}
    inp['x'] = nrm((BATCH, SEQ, d), 1.0)
    inp['p'] = nrm((DEPTH, BATCH, SEQ, PE_DIM), 1.0)
    inp['norm_pre'] = gain((DEPTH, d))
    inp['norm_post'] = gain((DEPTH, d))
    inp['pe_proj'] = nrm((DEPTH, PE_DIM, d), PE_DIM ** -0.5)
    inp['pe_gate'] = nrm((DEPTH, d, d), d ** -0.5)
    inp['ab_w_in'] = nrm((n_even, d, AB_IN), d ** -0.5)
    inp['ret_norm'] = gain((n_even, RET_HEADS * RET_DV))
    inp['ml_conv_w'] = nrm((n_even, ML_CONV, 2 * ML_HEADS * ML_DK), ML_CONV ** -0.5)
    inp['ml_conv_b'] = nrm((n_even, 2 * ML_HEADS * ML_DK), 0.02)
    f_bias = jnp.linspace(ML_F_BIAS_LO, ML_F_BIAS_HI, ML_HEADS, dtype=F32)
    inp['ml_gate_b'] = jnp.concatenate([nrm((n_even, ML_HEADS), 0.1),
                                        f_bias[None] + nrm((n_even, ML_HEADS), 0.1)], -1)
    inp['ml_norm'] = gain((n_even, ML_HEADS * ML_DV))
    inp['ab_w_out'] = nrm((n_even, AB_MIX, d), AB_MIX ** -0.5)
    inp['cd_w_in'] = nrm((n_odd, d, CD_IN), d ** -0.5)
    inp['cmp_pos_k'] = nrm((n_odd, CMP_LEN, NSA_DH), 0.02)
    inp['cmp_pos_v'] = nrm((n_odd, CMP_LEN, NSA_DH), 0.02)
    inp['cmp_w1_k'] = nrm((n_odd, CMP_LEN * NSA_DH, CMP_HIDDEN), (CMP_LEN * NSA_DH) ** -0.5)
    inp['cmp_w2_k'] = nrm((n_odd, CMP_HIDDEN, NSA_DH), CMP_HIDDEN ** -0.5)
    inp['cmp_w1_v'] = nrm((n_odd, CMP_LEN * NSA_DH, CMP_HIDDEN), (CMP_LEN * NSA_DH) ** -0.5)
    inp['cmp_w2_v'] = nrm((n_odd, CMP_HIDDEN, NSA_DH), CMP_HIDDEN ** -0.5)
    inp['s5_a_re'] = -0.5 + nrm((n_odd, S5_GROUPS, S5_STATE), 0.01)
    inp['s5_a_im'] = math.pi * jnp.arange(S5_STATE, dtype=F32) + nrm((n_odd, S5_GROUPS, S5_STATE), 0.01)
    inp['s5_log_dt'] = jax.random.uniform(keys.pop(), (n_odd, S5_GROUPS), F32, math.log(1e-3), math.log(1e-1))
    inp['s5_b_re'] = nrm((n_odd, S5_GROUPS, S5_STATE, S5_GROUP_CH), (2 * S5_GROUP_CH) ** -0.5)
    inp['s5_b_im'] = nrm((n_odd, S5_GROUPS, S5_STATE, S5_GROUP_CH), (2 * S5_GROUP_CH) ** -0.5)
    inp['s5_c_re'] = nrm((n_odd, S5_GROUPS, S5_GROUP_CH, S5_STATE), S5_STATE ** -0.5)
    inp['s5_c_im'] = nrm((n_odd, S5_GROUPS, S5_GROUP_CH, S5_STATE), S5_STATE ** -0.5)
    inp['s5_d'] = nrm((n_odd, S5_WIDTH), 1.0)
    inp['s5_w_glu'] = nrm((n_odd, S5_WIDTH, 2 * S5_WIDTH), S5_WIDTH ** -0.5)
    inp['cd_w_out'] = nrm((n_odd, CD_MIX, d), CD_MIX ** -0.5)
    return inp


def reference(x, p, norm_pre, norm_post, pe_proj, pe_gate,
              ab_w_in, ret_norm, ml_conv_w, ml_conv_b, ml_gate_b, ml_norm, ab_w_out,
              cd_w_in, cmp_pos_k, cmp_pos_v, cmp_w1_k, cmp_w2_k, cmp_w1_v, cmp_w2_v,
              s5_a_re, s5_a_im, s5_log_dt, s5_b_re, s5_b_im, s5_c_re, s5_c_im, s5_d, s5_w_glu, cd_w_out):
    h = x
    for i in range(DEPTH):
        u = rmsnorm(h, norm_pre[i])
        j = i // 2
        if i % 2 == 0:
            y = ab_mixer(u, ab_w_in[j], ret_norm[j], ml_conv_w[j], ml_conv_b[j], ml_gate_b[j], ml_norm[j], ab_w_out[j])
        else:
            y = cd_mixer(u, cd_w_in[j], cmp_pos_k[j], cmp_pos_v[j], cmp_w1_k[j], cmp_w2_k[j], cmp_w1_v[j], cmp_w2_v[j],
                         s5_a_re[j], s5_a_im[j], s5_log_dt[j], s5_b_re[j], s5_b_im[j], s5_c_re[j], s5_c_im[j],
                         s5_d[j], s5_w_glu[j], cd_w_out[j])
        h = h + rmsnorm(y.astype(h.dtype), norm_post[i])
        h = h + jax.nn.sigmoid(h @ pe_gate[i]) * (p[i] @ pe_proj[i])
    return h
```

```python
import math
from contextlib import ExitStack
import numpy as np
import ml_dtypes
import concourse.bass as bass
import concourse.mybir as mybir
from concourse.bass_utils import run_bass_kernel_spmd

F32 = mybir.dt.float32
BF16 = mybir.dt.bfloat16
ALU = mybir.AluOpType
AF = mybir.ActivationFunctionType
AX = mybir.AxisListType

SEM_ROLL = 12000
NDS = 24
S = 2048
D = 1024
NT = 16
EPS = 1e-6
NEG = -1e30


PSUM_NAMES = set()
STRICT_WAR = True


def _region(ap):
    t = ap.tensor
    shp = list(t.shape)
    dims = ap.ap
    off = int(ap.offset)
    space = str(ap.space).upper()
    if not ('SB' in space or 'PSUM' in space):
        lo = off
        hi = off + 1
        for st, c in dims:
            hi += abs(st) * (c - 1)
        return (t.name, 0, 1, lo, hi)
    if 'PSUM' in space:
        PSUM_NAMES.add(t.name)
    rowlen = 1
    for s in shp[1:]:
        rowlen *= s
    p0 = off // rowlen
    lo = off % rowlen
    pst, pc = dims[0]
    p1 = p0 + (1 if pst == 0 else pc)
    hi = lo + 1
    for st, c in dims[1:]:
        hi += abs(st) * (c - 1)
    return (t.name, p0, p1, lo, hi)


class KB:
    def __init__(self, nc, stack):
        self.nc = nc
        self.stack = stack
        self.E = dict(pe=nc.tensor, dve=nc.vector, act=nc.scalar, pool=nc.gpsimd, sp=nc.sync)
        self.sem = {}
        self.cnt = {}
        self.nsem = 0
        for e in self.E:
            self._newsem(e)
        self.known = {e: {} for e in self.E}
        self.acc = {}
        self.dsem = [stack.enter_context(nc.semaphore(f"dq{i}")) for i in range(NDS)]
        self.dcnt = [0] * NDS
        self.dnext = 0
        self.ninst = {e: 0 for e in self.E}
        self.nwait = {e: 0 for e in self.E}
        self.uid = 0
        self.log = []

    def _newsem(self, e):
        self.nsem += 1
        self.sem[e] = self.stack.enter_context(self.nc.semaphore(f"pg_{e}_{self.nsem}"))
        self.cnt[e] = 0

    def sb(self, name, shape, dt=F32, st=None):
        self.uid += 1
        return (st or self.stack).enter_context(self.nc.sbuf_tensor(f"{name}_{self.uid}", list(shape), dt))

    def ps(self, name, shape, dt=F32, st=None):
        self.uid += 1
        return (st or self.stack).enter_context(self.nc.psum_tensor(f"{name}_{self.uid}", list(shape), dt))

    def _wait(self, eng, sem, val):
        kn = self.known[eng]
        key = sem.name
        if kn.get(key, 0) >= val:
            return
        self.E[eng].wait_ge(sem, val)
        self.log.append((eng, 'WAIT', key, val))
        kn[key] = val
        self.nwait[eng] += 1

    def barrier(self):
        toks = [(self.sem[e], self.cnt[e]) for e in self.E if self.cnt[e] > 0]
        toks += [(self.dsem[i], self.dcnt[i]) for i in range(NDS) if self.dcnt[i] > 0]
        for e in self.E:
            for s, v in toks:
                if s is self.sem[e]:
                    continue
                self._wait(e, s, v)
        self.acc = {}

    def issue(self, eng, fn, reads, writes, dma=False):
        deps = {}
        accs = []
        for ap in reads:
            if ap is None or isinstance(ap, (int, float)):
                continue
            accs.append((_region(ap), False))
        for ap in writes:
            accs.append((_region(ap), True))
        src_eng = 'dma' if dma else eng
        for (name, p0, p1, lo, hi), w in accs:
            d = self.acc.get(name)
            if not d:
                continue
            isps = name in PSUM_NAMES
            for key, (s, v) in d.items():
                e2, w2, q0, q1, l2, h2 = key
                if isps and e2 != src_eng:
                    sk = s.name
                    if sk not in deps or deps[sk][1] < v:
                        deps[sk] = (s, v)
                    continue
                if not (w or w2):
                    continue
                if q1 <= p0 or p1 <= q0 or h2 <= lo or hi <= l2:
                    continue
                if e2 == src_eng and e2 != 'dma':
                    if e2 == 'pe':
                        continue
                    if not w2 and not STRICT_WAR:
                        continue
                sk = s.name
                if sk not in deps or deps[sk][1] < v:
                    deps[sk] = (s, v)
        for sk, (s, v) in deps.items():
            self._wait(eng, s, v)
        if dma:
            slot = self.dnext
            self.dnext = (self.dnext + 1) % NDS
            if self.dcnt[slot] > 0:
                self._wait(eng, self.dsem[slot], self.dcnt[slot])
            inst = fn()
            self.dcnt[slot] += 16
            inst.then_inc(self.dsem[slot], 16)
            tok = (self.dsem[slot], self.dcnt[slot])
        else:
            if self.cnt[eng] >= SEM_ROLL:
                self._newsem(eng)
            inst = fn()
            self.cnt[eng] += 1
            inst.then_inc(self.sem[eng], 1)
            tok = (self.sem[eng], self.cnt[eng])
        self.ninst[eng] += 1
        self.log.append((eng, 'INST', tok[0].name, tok[1], [(a[0][0], a[1]) for a in accs]))
        for (name, p0, p1, lo, hi), w in accs:
            d = self.acc.setdefault(name, {})
            if w:
                dead = [k for k in d if k[2] >= p0 and k[3] <= p1 and k[4] >= lo and k[5] <= hi]
                for k in dead:
                    del d[k]
            d[(src_eng, w, p0, p1, lo, hi)] = tok
        return tok

    def dma(self, out, in_, eng='sp', **kw):
        return self.issue(eng, lambda: self.E[eng].dma_start(out=out, in_=in_, **kw), [in_], [out], dma=True)

    def mm(self, out, lhsT, rhs, start=True, stop=True, **kw):
        return self.issue('pe', lambda: self.nc.tensor.matmul(out, lhsT, rhs, start=start, stop=stop, **kw),
                          [lhsT, rhs], [out])

    def tr(self, out, in_, ident):
        return self.issue('pe', lambda: self.nc.tensor.transpose(out, in_, ident), [in_, ident], [out])

    def act(self, out, in_, func, bias=0.0, scale=1.0, accum_out=None):
        rd = [in_]
        kw = {}
        if not isinstance(bias, (int, float)):
            rd.append(bias)
        if not isinstance(scale, (int, float)):
            rd.append(scale)
        wr = [out]
        if accum_out is not None:
            wr.append(accum_out)
            kw['accum_out'] = accum_out
        return self.issue('act', lambda: self.nc.scalar.activation(out, in_, func, bias=bias, scale=scale, **kw),
                          rd, wr)

    def tt(self, eng, out, in0, in1, op):
        return self.issue(eng, lambda: self.E[eng].tensor_tensor(out, in0, in1, op), [in0, in1], [out])

    def ts(self, eng, out, in0, s1, s2, op0, op1=None, accum_out=None):
        rd = [in0]
        if not isinstance(s1, (int, float)):
            rd.append(s1)
        if s2 is not None and not isinstance(s2, (int, float)):
            rd.append(s2)
        wr = [out]
        kw = {}
        if accum_out is not None:
            wr.append(accum_out)
            kw['accum_out'] = accum_out
        if op1 is None:
            return self.issue(eng, lambda: self.E[eng].tensor_scalar(out, in0, s1, None, op0, **kw), rd, wr)
        return self.issue(eng, lambda: self.E[eng].tensor_scalar(out, in0, s1, s2, op0, op1, **kw), rd, wr)

    def stt(self, eng, out, in0, scalar, in1, op0, op1):
        rd = [in0, in1]
        if not isinstance(scalar, (int, float)):
            rd.append(scalar)
        return self.issue(eng, lambda: self.E[eng].scalar_tensor_tensor(out, in0, scalar, in1, op0, op1), rd, [out])

    def red(self, eng, out, in_, op, axis=AX.X, **kw):
        return self.issue(eng, lambda: self.E[eng].tensor_reduce(out, in_, axis, op, **kw), [in_], [out])

    def cp(self, eng, out, in_):
        if eng == 'act':
            return self.issue('act', lambda: self.nc.scalar.copy(out, in_), [in_], [out])
        return self.issue(eng, lambda: self.E[eng].tensor_copy(out, in_), [in_], [out])

    def scan(self, out, d0, d1, init, op0, op1):
        rd = [d0, d1]
        if not isinstance(init, (int, float)):
            rd.append(init)
        return self.issue('dve', lambda: self.nc.vector.tensor_tensor_scan(out, d0, d1, init, op0, op1), rd, [out])

    def memset(self, eng, ap, val):
        return self.issue(eng, lambda: self.E[eng].memset(ap, val), [], [ap])

    def max8(self, out, in_):
        return self.issue('dve', lambda: self.nc.vector.max(out, in_), [in_], [out])

    def recip(self, out, in_):
        return self.issue('dve', lambda: self.nc.vector.reciprocal(out, in_), [in_], [out])

    def bnstats(self, out, in_):
        return self.issue('dve', lambda: self.nc.vector.bn_stats(out, in_), [in_], [out])

    def bnaggr(self, out, in_):
        return self.issue('dve', lambda: self.nc.vector.bn_aggr(out, in_), [in_], [out])

    def finish(self, toks):
        for s, v in toks:
            self._wait('sp', s, v)


RET_G = [1.0 - 2.0 ** (-5.0 - h) for h in range(4)]


def host_consts_l0():
    c = {}
    c['ident'] = np.eye(128, dtype=np.float32)
    i = np.arange(128)
    c['triu'] = (i[:, None] <= i[None, :]).astype(np.float32)
    c['negmask'] = np.where(i[None, :] <= i[:, None], 0.0, NEG).astype(np.float32)
    dec = np.zeros((128, 4, 128), np.float32)
    xi = np.zeros((128, 4, 128), np.float32)
    zeta = np.zeros((128, 4), np.float32)
    for h in range(4):
        lg = np.log1p(-(2.0 ** (-5.0 - h)))
        diff = (i[None, :] - i[:, None]).astype(np.float64)
        dec[:, h, :] = np.where(diff >= 0, np.exp(np.maximum(diff, 0) * lg), 0.0) * 128 ** -0.5
        xi[:, h, :] = (np.exp((i + 1.0) * lg) * 128 ** -0.5)[None, :]
        zeta[:, h] = np.exp((127 - i) * lg)
    c['decT'] = dec
    c['xi'] = xi
    c['zeta'] = zeta
    return c


class Ctx:
    pass


RR_ENG = ('pool', 'act', 'dve')


def load_w(k, C, dst, src, r0, nkc, c0, ncols, eng='pool'):
    k.dma(dst[:, 0:nkc, 0:ncols], src[r0:r0 + nkc * 128, c0:c0 + ncols].rearrange("(kc p) n -> p kc n", p=128), eng='pool')


def run_interleaved(gens, stagger=0):
    gens = list(gens)
    if len(gens) > 1:
        for _ in range(stagger):
            try:
                next(gens[0])
            except StopIteration:
                gens.pop(0)
                break
    while gens:
        for g in list(gens):
            try:
                next(g)
            except StopIteration:
                gens.remove(g)


def norm_to_uT(k, C, src_dram, gT, uT):
    def tile_gen(T, i):
        xt = C.xt[i]
        k.dma(xt[:], src_dram[T * 128:(T + 1) * 128, :])
        sq = C.junks[i]
        ss = C.sm[:, 4 * i:4 * i + 1]
        k.act(sq[:], xt[:], AF.Square, accum_out=ss)
        yield
        rs = C.sm[:, 4 * i + 1:4 * i + 2]
        k.ts('dve', rs, ss, 1.0 / D, EPS, ALU.mult, ALU.add)
        yield
        k.tt('pool', rs, rs, C.neghalf[:], ALU.pow)
        yield
        xn = C.xns[i]
        k.ts('dve', xn[:], xt[:], rs, None, ALU.mult)
        yield
        pt = C.pT[i]
        for c in range(8):
            k.tr(pt[:, c * 128:(c + 1) * 128], xn[:, c * 128:(c + 1) * 128], C.identb[:])
        yield
        k.tt('dve', uT[:, :, T * 128:(T + 1) * 128], pt[:].rearrange("p (c t) -> p c t", c=8),
             gT[:].unsqueeze(2).to_broadcast([128, 8, 128]), ALU.mult)
        yield
    for T0 in range(0, NT, 2):
        run_interleaved([tile_gen(T0 + i, i) for i in range(2)])


def proj_fm(k, C, W, wc0, uT, evac):
    for g in range(4):
        ps = C.pa[C.pai % 2]
        C.pai += 1
        for kc in range(8):
            k.mm(ps[:], W[:, kc, wc0:wc0 + 128], uT[:, kc, g * 512:(g + 1) * 512], start=(kc == 0), stop=(kc == 7))
        evac(g, ps)


def proj_tm(k, C, W, wc0, ncols, uT, T, ps):
    for kc in range(8):
        k.mm(ps[:, 0:ncols], uT[:, kc, T * 128:(T + 1) * 128], W[:, kc, wc0:wc0 + ncols], start=(kc == 0), stop=(kc == 7))


def headnorm_gate(k, C, ysb, gg, out_bf):
    st6 = C.sm[:, 8:14]
    mv = C.sm[:, 14:16]
    k.bnstats(st6, ysb)
    k.bnaggr(mv, st6)
    rstd = C.sm[:, 16:17]
    k.ts('dve', rstd, mv[:, 1:2], EPS, None, ALU.add)
    k.act(rstd, rstd, AF.Sqrt)
    k.recip(rstd, rstd)
    k.ts('dve', ysb, ysb, mv[:, 0:1], rstd, ALU.subtract, ALU.mult)
    k.tt('dve', out_bf, ysb, gg, ALU.mult)


def mix_to_T(k, C, mix_bf, mixT, fc0, T):
    pt = C.pT[C.pti % 2]
    C.pti += 1
    for j in range(2):
        k.tr(pt[:, j * 128:(j + 1) * 128], mix_bf[:, j * 128:(j + 1) * 128], C.identb[:])
    k.cp('act', mixT[:, fc0:fc0 + 2, T * 128:(T + 1) * 128], pt[:, 0:256].rearrange("p (c t) -> p c t", c=2))


def outproj_accum(k, C, mixT, nfc, w_out, r0, yacc, first):
    Wo = C.Wo
    load_w(k, C, Wo, w_out, r0, nfc, 0, 512)
    load_w(k, C, C.Wo2, w_out, r0, nfc, 512, 512)
    for T in range(NT):
        for half, Wt in ((0, Wo), (1, C.Wo2)):
            ps = C.pa[C.pai % 2]
            C.pai += 1
            for fc in range(nfc):
                k.mm(ps[:], mixT[:, fc, T * 128:(T + 1) * 128], Wt[:, fc, 0:512], start=(fc == 0), stop=(fc == nfc - 1))
            dst = yacc[:, T, half * 512:(half + 1) * 512]
            if first:
                k.cp('act', dst, ps[:])
            else:
                k.tt('dve', dst, dst, ps[:], ALU.add)


def post_block(k, C, sb3, PSL, res_dram, p_dram, gpost_b, Wg, Wp, out_dram, gen_y):
    toks = []

    class Sl:
        pass
    slots = []
    for i in range(2):
        s_ = Sl()
        s_.i = i
        s_.ps = PSL[i]
        s_.xt = sb3(f"pxt{i}", [128, 1024])
        s_.ptile = sb3(f"ppt{i}", [128, 256])
        s_.junk = sb3(f"pjunk{i}", [128, 1024])
        s_.hb = sb3(f"phb{i}", [128, 1024], BF16)
        s_.hT = sb3(f"phT{i}", [128, 8, 128], BF16)
        s_.pb = sb3(f"ppb{i}", [128, 256], BF16)
        s_.ppT = sb3(f"pppT{i}", [128, 2, 128], BF16)
        s_.ytile = sb3(f"pyt{i}", [128, 1024])
        s_.sm = sb3(f"psm{i}", [128, 8])
        slots.append(s_)

    def tile_gen(T, s_):
        xt = s_.xt
        k.dma(xt[:], res_dram[T * 128:(T + 1) * 128, :])
        k.dma(s_.ptile[:], p_dram[T * 128:(T + 1) * 128, :])
        yield from gen_y(T, s_)
        y = s_.ytile[:]
        ss = s_.sm[:, 0:1]
        k.act(s_.junk[:], y, AF.Square, accum_out=ss)
        yield
        rs = s_.sm[:, 1:2]
        k.ts('dve', rs, ss, 1.0 / D, EPS, ALU.mult, ALU.add)
        k.tt('pool', rs, rs, C.neghalf[:], ALU.pow)
        yield
        k.stt('dve', y, y, rs, gpost_b[:], ALU.mult, ALU.mult)
        yield
        k.tt('pool', xt[:], xt[:], y, ALU.add)
        k.cp('act', s_.hb[:], xt[:])
        k.cp('act', s_.pb[:], s_.ptile[:])
        yield
        pt = s_.ps['pT']
        for c in range(8):
            k.tr(pt[:, c * 128:(c + 1) * 128], s_.hb[:, c * 128:(c + 1) * 128], C.identb[:])
        yield
        k.cp('dve', s_.hT[:], pt[:].rearrange("p (c t) -> p c t", c=8))
        yield
        for c in range(2):
            k.tr(pt[:, c * 128:(c + 1) * 128], s_.pb[:, c * 128:(c + 1) * 128], C.identb[:])
        yield
        k.cp('dve', s_.ppT[:], pt[:, 0:256].rearrange("p (c t) -> p c t", c=2))
        yield
        for half in range(2):
            psg = s_.ps['pg']
            psp = s_.ps['pp']
            for kc in range(8):
                k.mm(psg[:], s_.hT[:, kc, :], Wg[:, kc, half * 512:(half + 1) * 512], start=(kc == 0), stop=(kc == 7))
            for kc in range(2):
                k.mm(psp[:], s_.ppT[:, kc, :], Wp[:, kc, half * 512:(half + 1) * 512], start=(kc == 0), stop=(kc == 1))
            yield
            sg = s_.junk[:, 0:512]
            k.act(sg, psg[:], AF.Tanh, scale=0.5)
            yield
            k.stt('dve', sg, sg, 1.0, psp[:], ALU.add, ALU.mult)
            yield
            k.stt('dve', xt[:, half * 512:(half + 1) * 512], sg, 0.5, xt[:, half * 512:(half + 1) * 512], ALU.mult, ALU.add)
            yield
        toks.append(k.dma(out_dram[T * 128:(T + 1) * 128, :], xt[:]))
    for T0 in range(0, NT, 2):
        run_interleaved([tile_gen(T0 + i, slots[i]) for i in range(2)], stagger=0)
    return toks


L0_STOP = 99
L0_NG = 2
STAG_RET = 0
STAG_ML = 0


def layer0(k, nc, A, x_dram, out_dram, mixd):
    with ExitStack() as st:
        C = Ctx()
        C.wsi = 0
        C.pai = 0
        C.pti = 0
        sb = lambda n, s, d=F32: k.sb(n, s, d, st)
        ps = lambda n, s, d=F32: k.ps(n, s, d, st)
        PS = []
        for i in range(2):
            PS.append(dict(pa=ps(f"pa{i}", [128, 512]), pb=ps(f"pb{i}", [128, 512]), pc=ps(f"pc{i}", [128, 512]),
                           pT=ps(f"pT{i}", [128, 1024], BF16)))
        C.pa = [PS[0]['pa'], PS[1]['pa']]
        C.pT = [PS[0]['pT'], PS[1]['pT']]
        identf = sb("identf", [128, 128])
        C.identf = identf
        C.identb = sb("identb", [128, 128], BF16)
        triu = sb("triu", [128, 128])
        negmask = sb("negmask", [128, 128])
        decT = sb("decT", [128, 4, 128])
        xi = sb("xi", [128, 4, 128])
        zeta = sb("zeta", [128, 4])
        onesf = sb("onesf", [128, 128])
        k.dma(identf[:], A['c_ident'][:, :])
        k.dma(triu[:], A['c_triu'][:, :])
        k.dma(negmask[:], A['c_negmask'][:, :])
        k.dma(decT[:], A['c_decT'][:, :, :])
        k.dma(xi[:], A['c_xi'][:, :, :])
        k.dma(zeta[:], A['c_zeta'][:, :])
        k.cp('dve', C.identb[:], identf[:])
        k.memset('pool', onesf[:], 1.0)
        C.neghalf = sb("neghalf", [128, 1])
        k.memset('pool', C.neghalf[:], -0.5)
        gpreT = sb("gpreT", [128, 8])
        k.dma(gpreT[:], A['gpre0T'][:, :])
        convw = sb("convw", [128, 8, 4])
        convb = sb("convb", [128, 8])
        gateb = sb("gateb", [128, 8])
        k.dma(convw[:], A['convw'][:, :, :])
        k.dma(convb[:], A['convb'][:, :])
        k.dma(gateb[:], A['gateb'][:].partition_broadcast(128))
        C.xt = [sb("xt0", [128, 1024]), sb("xt1", [128, 1024])]
        C.junk = sb("junk", [128, 1024])
        C.xn = sb("xn", [128, 1024], BF16)
        C.junks = [C.junk, sb("junk2", [128, 1024])]
        C.xns = [C.xn, sb("xn2", [128, 1024], BF16)]
        C.sm = sb("sm", [128, 64])
        w_in = A['w_in0']
        with ExitStack() as st2:
            sb2 = lambda n, s, d=F32: k.sb(n, s, d, st2)
            uT = sb2("uT", [128, 8, S], BF16)
            ifg = sb2("ifg", [128, NT, 8])
            logf = sb2("logf", [128, NT, 4])
            bcum = sb2("bcum", [128, NT, 4])
            Gtok = sb2("Gtok", [128, NT, 4])
            blast = sb2("blast", [128, NT, 4])
            mst = sb2("mst", [128, 4])
            maxG_all = sb2("maxG_all", [128, NT, 4])
            nmaxG_all = sb2("nmaxG_all", [128, NT, 4])
            mu_all = sb2("mu_all", [128, NT, 4])
            bm_all = sb2("bm_all", [128, NT, 4])
            m_all = sb2("m_all", [128, NT + 1, 4])
            sp_all = sb2("sp_all", [128, NT, 4])
            sc_all = sb2("sc_all", [128, NT, 4])

            class Slot:
                pass
            slots = []
            for i in range(2):
                B = Slot()
                B.i = i
                B.ps = PS[i]
                B.W = sb2(f"W{i}", [128, 8, 256], BF16)
                B.W2 = sb2(f"W2{i}", [128, 8, 256], BF16)
                B.W3 = sb2(f"W3{i}", [128, 8, 256], BF16)
                B.W4 = sb2(f"W4{i}", [128, 8, 256], BF16)
                B.qT = sb2(f"qT{i}", [128, S], BF16)
                B.qxT = sb2(f"qxT{i}", [128, S], BF16)
                B.kT = sb2(f"kT{i}", [128, S], BF16)
                B.ktok = sb2(f"ktok{i}", [128, NT, 128], BF16)
                B.vaug = sb2(f"vaug{i}", [128, NT, 260], BF16)
                B.vz = sb2(f"vz{i}", [128, 260], BF16)
                B.gnb = sb2(f"gnb{i}", [128, 256])
                B.gg = sb2(f"gg{i}", [128, 256])
                B.scm = sb2(f"scm{i}", [128, 128], BF16)
                B.r32 = sb2(f"r32{i}", [128, 260])
                B.rbf = sb2(f"rbf{i}", [128, 260], BF16)
                B.ysb = sb2(f"ysb{i}", [128, 260])
                B.tmp257 = sb2(f"tmp257{i}", [128, 260])
                B.mixb = sb2(f"mixb{i}", [128, 256], BF16)
                B.mixst = [sb2(f"mixst{i}a", [128, 2, 128], BF16), sb2(f"mixst{i}b", [128, 2, 128], BF16)]
                B.sigo = sb2(f"sigo{i}", [128, 256])
                B.diagG = sb2(f"diagG{i}", [128, 128])
                B.logD = sb2(f"logD{i}", [128, 128])
                B.dmat = sb2(f"dmat{i}", [128, 128])
                B.Sbf = sb2(f"Sbf{i}", [128, 128], BF16)
                B.p1 = [(B.diagG, B.logD, B.dmat, B.Sbf),
                        (sb2(f"diagG{i}b", [128, 128]), sb2(f"logD{i}b", [128, 128]), sb2(f"dmat{i}b", [128, 128]),
                         sb2(f"Sbf{i}b", [128, 128], BF16))]
                B.ST = sb2(f"ST{i}", [128, NT, 128], BF16)
                B.sc16 = sb2(f"sc16{i}", [128, 4, NT])
                B.STb = sb2(f"STb{i}", [128, 128], BF16)
                B.cin = sb2(f"cin{i}", [128, 520])
                B.cacc = sb2(f"cacc{i}", [128, 512])
                B.csig = sb2(f"csig{i}", [128, 512])
                B.sm = sb2(f"sms{i}", [128, 64])
                B.wsi = 0
                B.mxi = 0
                slots.append(B)

            def loadw(B, dst, c0, ncols):
                k.dma(dst[:, :, 0:ncols], w_in[:, c0:c0 + ncols].rearrange("(kc p) n -> p kc n", p=128), eng='pool')

            def projfm(B, Wt, evac):
                pl = [B.ps['pa'], B.ps['pc']]
                for g in range(4):
                    p = pl[g % 2]
                    for kc in range(8):
                        k.mm(p[:], Wt[:, kc, 0:128], uT[:, kc, g * 512:(g + 1) * 512], start=(kc == 0), stop=(kc == 7))
                    evac(g, p)
                    yield

            def projtm(B, Wt, ncols, T, p, c0=0):
                for kc in range(8):
                    k.mm(p[:, c0:c0 + ncols], uT[:, kc, T * 128:(T + 1) * 128], Wt[:, kc, 0:ncols], start=(kc == 0), stop=(kc == 7))

            def hnorm(B, ysb, gg, out_bf):
                sm_ = B.sm
                st6 = sm_[:, 8:14]
                mv = sm_[:, 14:16]
                k.bnstats(st6, ysb)
                k.bnaggr(mv, st6)
                rstd = sm_[:, 16:17]
                k.ts('dve', rstd, mv[:, 1:2], EPS, None, ALU.add)
                k.act(rstd, rstd, AF.Sqrt)
                k.recip(rstd, rstd)
                k.ts('dve', ysb, ysb, mv[:, 0:1], rstd, ALU.subtract, ALU.mult)
                k.tt('dve', out_bf, ysb, gg, ALU.mult)

            def mix_out(B, fc0, T):
                pt = B.ps['pT']
                for j in range(2):
                    k.tr(pt[:, 512 + j * 128:512 + (j + 1) * 128], B.mixb[:, j * 128:(j + 1) * 128], C.identb[:])
                ms = B.mixst[B.mxi % 2]
                B.mxi += 1
                k.cp('act', ms[:], pt[:, 512:768].rearrange("p (c t) -> p c t", c=2))
                k.dma(mixd[fc0:fc0 + 2, :, T * 128:(T + 1) * 128].rearrange("c p t -> p c t"), ms[:])

            def ktok_from_kT(B):
                pt = B.ps['pT']
                for T in range(NT):
                    k.tr(pt[:, (T % 4) * 128:(T % 4 + 1) * 128], B.kT[:, T * 128:(T + 1) * 128], C.identb[:])
                    k.cp('dve', B.ktok[:, T, :], pt[:, (T % 4) * 128:(T % 4 + 1) * 128])
                    if T % 4 == 3:
                        yield

            def v_proj(B, Wt):
                pl = [B.ps['pa'], B.ps['pc']]
                for T in range(NT):
                    p = pl[T % 2]
                    projtm(B, Wt, 256, T, p)
                    k.cp('act', B.vaug[:, T, 0:256], p[:, 0:256])
                    if T % 2 == 1:
                        yield

            def ret_head(h, B):
                pa, pb, pc = B.ps['pa'], B.ps['pb'], B.ps['pc']
                qT, qxT, kT, vaug = B.qT, B.qxT, B.kT, B.vaug
                loadw(B, B.W, h * 128, 128)
                loadw(B, B.W2, 512 + h * 128, 128)
                loadw(B, B.W3, 1024 + h * 256, 256)
                loadw(B, B.W4, 2048 + h * 256, 256)
                yield
                yield from projfm(B, B.W, lambda g, p: k.cp('act', qT[:, g * 512:(g + 1) * 512], p[:]))
                k.tt('dve', qxT[:].rearrange("p (c i) -> p c i", c=NT), qT[:].rearrange("p (c i) -> p c i", c=NT),
                     xi[:, h, :].unsqueeze(1).to_broadcast([128, NT, 128]), ALU.mult)
                yield from projfm(B, B.W2, lambda g, p: k.cp('act', kT[:, g * 512:(g + 1) * 512], p[:]))
                yield from ktok_from_kT(B)
                yield from v_proj(B, B.W3)
                k.dma(B.gnb[:], A['ret_norm'][h * 256:(h + 1) * 256].partition_broadcast(128))
                k.ts('pool', B.gnb[:], B.gnb[:], 0.5, None, ALU.mult)
                k.memset('pool', B.r32[:], 0.0)
                for c in range(NT):
                    projtm(B, B.W4, 256, c, pc)
                    k.act(B.sigo[:], pc[:, 0:256], AF.Tanh, scale=0.5)
                    k.stt('dve', B.sigo[:], B.sigo[:], 1.0, pc[:, 0:256], ALU.add, ALU.mult)
                    k.tt('pool', B.gg[:], B.sigo[:], B.gnb[:], ALU.mult)
                    yield
                    k.mm(pb[:, 0:128], kT[:, c * 128:(c + 1) * 128], qT[:, c * 128:(c + 1) * 128])
                    k.tt('dve', B.scm[:], pb[:, 0:128], decT[:, h, :], ALU.mult)
                    yield
                    k.mm(pa[:, 0:256], B.scm[:], vaug[:, c, 0:256], start=True, stop=(c == 0))
                    if c > 0:
                        k.mm(pa[:, 0:256], qxT[:, c * 128:(c + 1) * 128], B.rbf[:, 0:256], start=False, stop=True)
                    if c < NT - 1:
                        k.ts('dve', B.vz[:, 0:256], vaug[:, c, 0:256], zeta[:, h:h + 1], None, ALU.mult)
                        k.mm(pb[:, 256:512], B.ktok[:, c, :], B.vz[:, 0:256])
                    yield
                    if c < NT - 1:
                        k.stt('dve', B.r32[:, 0:256], B.r32[:, 0:256], float(RET_G[h] ** 128), pb[:, 256:512], ALU.mult, ALU.add)
                        k.cp('act', B.rbf[:, 0:256], B.r32[:, 0:256])
                    k.cp('act', B.ysb[:, 0:256], pa[:, 0:256])
                    yield
                    sm_ = B.sm
                    k.bnstats(sm_[:, 8:14], B.ysb[:, 0:256])
                    k.bnaggr(sm_[:, 14:16], sm_[:, 8:14])
                    yield
                    rstd = sm_[:, 16:17]
                    k.ts('dve', rstd, sm_[:, 15:16], EPS, None, ALU.add)
                    k.tt('pool', rstd, rstd, C.neghalf[:], ALU.pow)
                    yield
                    k.ts('dve', B.ysb[:, 0:256], B.ysb[:, 0:256], sm_[:, 14:15], rstd, ALU.subtract, ALU.mult)
                    yield
                    k.tt('dve', B.mixb[:], B.ysb[:, 0:256], B.gg[:], ALU.mult)
                    mix_out(B, h * 2, c)
                    yield

            MQ, MK, MV, MI, MF, MO, MZ = 3072, 3584, 4096, 5120, 5124, 5128, 6152

            def ml_head(h, B):
                pa, pb, pc = B.ps['pa'], B.ps['pb'], B.ps['pc']
                qT, kT, vaug = B.qT, B.kT, B.vaug
                sm_ = B.sm
                cin, cacc, csig = B.cin, B.cacc, B.csig
                for which, col0, dstT in ((0, MQ + h * 128, qT), (1, MK + h * 128, kT)):
                    loadw(B, B.W, col0, 128)
                    blk = which * 4 + h
                    k.memset('pool', cin[:, 0:3], 0.0)
                    yield

                    def evac(g, p, blk=blk, dstT=dstT, which=which):
                        k.cp('act', cin[:, 3:515], p[:])
                        k.ts('dve', cacc[:], cin[:, 3:515], convw[:, blk, 3:4], convb[:, blk:blk + 1], ALU.mult, ALU.add)
                        for j in range(3):
                            k.stt('dve', cacc[:], cin[:, j:j + 512], convw[:, blk, j:j + 1], cacc[:], ALU.mult, ALU.add)
                        k.cp('pool', cin[:, 0:3], cin[:, 512:515])
                        k.act(csig[:], cacc[:], AF.Tanh, scale=0.5)
                        sc = 0.5 if which == 0 else 0.5 * 128 ** -0.5
                        k.stt('dve', csig[:], csig[:], 1.0, cacc[:], ALU.add, ALU.mult)
                        k.ts('pool', dstT[:, g * 512:(g + 1) * 512], csig[:], sc, None, ALU.mult)
                    yield from projfm(B, B.W, evac)
                yield from ktok_from_kT(B)
                loadw(B, B.W, MV + h * 256, 256)
                yield
                yield from v_proj(B, B.W)
                k.memset('pool', vaug[:, :, 256:257], 1.0)
                loadw(B, B.W, MO + h * 256, 256)
                loadw(B, B.W2, MZ + h * 256, 256)
                k.dma(B.gnb[:], A['ml_norm'][h * 256:(h + 1) * 256].partition_broadcast(128))
                k.ts('pool', B.gnb[:], B.gnb[:], 0.5, None, ALU.mult)
                k.memset('pool', B.r32[:], 0.0)
                yield
                r32, rbf, ysb, tmp257 = B.r32, B.rbf, B.ysb, B.tmp257
                for c in range(NT):
                    mprev = mst[:, h:h + 1]
                    bc_t = bcum[:, c, h:h + 1]
                    projtm(B, B.W, 256, c, pc)
                    projtm(B, B.W2, 256, c, pc, c0=256)
                    k.act(B.sigo[:], pc[:, 0:256], AF.Tanh, scale=0.5)
                    k.act(B.gg[:], pc[:, 256:512], AF.Tanh, scale=0.5)
                    k.ts('dve', B.sigo[:], B.sigo[:], 0.5, 0.5, ALU.mult, ALU.add)
                    k.stt('dve', B.gg[:], B.gg[:], 1.0, pc[:, 256:512], ALU.add, ALU.mult)
                    k.tt('pool', B.gg[:], B.gg[:], B.gnb[:], ALU.mult)
                    yield
                    k.ts('dve', B.diagG[:], identf[:], Gtok[:, c, h:h + 1], None, ALU.mult)
                    k.mm(pb[:, 0:128], onesf[:], B.diagG[:])
                    yield
                    k.stt('dve', B.logD[:], pb[:, 0:128], bc_t, negmask[:], ALU.add, ALU.add)
                    mx = sm_[:, 20:21]
                    maxG = sm_[:, 21:22]
                    k.red('dve', maxG, pb[:, 0:128], ALU.max)
                    k.red('dve', mx, B.logD[:], ALU.max)
                    yield
                    inter = sm_[:, 22:23]
                    k.tt('dve', inter, bc_t, mprev, ALU.add)
                    negm = sm_[:, 23:24]
                    k.tt('dve', negm, inter, mx, ALU.max)
                    yield
                    k.ts('dve', negm, negm, -1.0, None, ALU.mult)
                    k.mm(pb[:, 128:256], qT[:, c * 128:(c + 1) * 128], kT[:, c * 128:(c + 1) * 128])
                    yield
                    k.act(B.dmat[:], B.logD[:], AF.Exp, bias=negm)
                    yield
                    k.tt('dve', B.Sbf[:], pb[:, 128:256], B.dmat[:], ALU.mult)
                    yield
                    pt = B.ps['pT']
                    k.tr(pt[:, 0:128], B.Sbf[:], C.identb[:])
                    yield
                    k.cp('act', B.STb[:], pt[:, 0:128])
                    yield
                    k.mm(pa[:, 0:257], B.STb[:], vaug[:, c, 0:257])
                    if c > 0:
                        k.mm(pc[:, 0:257], qT[:, c * 128:(c + 1) * 128], rbf[:, 0:257])
                        wint = sm_[:, 24:25]
                        k.act(wint, inter, AF.Exp, bias=negm)
                        yield
                        k.act(tmp257[:, 0:257], pc[:, 0:257], AF.Copy, scale=wint)
                        yield
                        k.tt('dve', ysb[:, 0:257], tmp257[:, 0:257], pa[:, 0:257], ALU.add)
                    else:
                        yield
                        k.cp('act', ysb[:, 0:257], pa[:, 0:257])
                    enm = sm_[:, 25:26]
                    k.act(enm, negm, AF.Exp)
                    yield
                    den = sm_[:, 26:27]
                    k.act(den, ysb[:, 256:257], AF.Abs)
                    yield
                    k.tt('dve', den, den, enm, ALU.max)
                    yield
                    k.recip(den, den)
                    yield
                    k.stt('dve', ysb[:, 0:256], ysb[:, 0:256], den, B.sigo[:], ALU.mult, ALU.mult)
                    yield
                    if c < NT - 1:
                        nmaxG = sm_[:, 27:28]
                        k.ts('dve', nmaxG, maxG, -1.0, None, ALU.mult)
                        wj = sm_[:, 28:29]
                        k.act(wj, Gtok[:, c, h:h + 1], AF.Exp, bias=nmaxG)
                        yield
                        k.ts('dve', B.vz[:, 0:257], vaug[:, c, 0:257], wj, None, ALU.mult)
                        yield
                        k.mm(pa[:, 0:257], B.ktok[:, c, :], B.vz[:, 0:257])
                        bl = blast[:, c, h:h + 1]
                        mu = sm_[:, 29:30]
                        k.tt('dve', mu, bl, maxG, ALU.add)
                        bm = sm_[:, 30:31]
                        k.tt('dve', bm, bl, mprev, ALU.add)
                        yield
                        nmn = sm_[:, 31:32]
                        k.tt('dve', nmn, bm, mu, ALU.max)
                        yield
                        k.cp('dve', mprev, nmn)
                        k.ts('dve', nmn, nmn, -1.0, None, ALU.mult)
                        yield
                        sp = sm_[:, 32:33]
                        scl = sm_[:, 33:34]
                        k.act(sp, bm, AF.Exp, bias=nmn)
                        k.act(scl, mu, AF.Exp, bias=nmn)
                        yield
                        k.act(tmp257[:, 0:257], pa[:, 0:257], AF.Copy, scale=scl)
                        yield
                        k.stt('dve', r32[:, 0:257], r32[:, 0:257], sp, tmp257[:, 0:257], ALU.mult, ALU.add)
                        yield
                        k.cp('act', rbf[:, 0:257], r32[:, 0:257])
                        yield
                    k.bnstats(sm_[:, 8:14], ysb[:, 0:256])
                    k.bnaggr(sm_[:, 14:16], sm_[:, 8:14])
                    yield
                    rstd = sm_[:, 16:17]
                    k.ts('dve', rstd, sm_[:, 15:16], EPS, None, ALU.add)
                    k.tt('pool', rstd, rstd, C.neghalf[:], ALU.pow)
                    yield
                    k.ts('dve', ysb[:, 0:256], ysb[:, 0:256], sm_[:, 14:15], rstd, ALU.subtract, ALU.mult)
                    yield
                    k.tt('dve', B.mixb[:], ysb[:, 0:256], B.gg[:], ALU.mult)
                    mix_out(B, 8 + h * 2, c)
                    yield

            def ml_head2(h, B):
                pa, pb, pc = B.ps['pa'], B.ps['pb'], B.ps['pc']
                qT, kT, vaug = B.qT, B.kT, B.vaug
                sm_ = B.sm
                cin, cacc, csig = B.cin, B.cacc, B.csig
                loadw(B, B.W, MQ + h * 128, 128)
                loadw(B, B.W2, MK + h * 128, 128)
                loadw(B, B.W3, MV + h * 256, 256)
                loadw(B, B.W4, MO + h * 256, 256)
                for which, col0, dstT in ((0, MQ + h * 128, qT), (1, MK + h * 128, kT)):
                    Wqk = B.W if which == 0 else B.W2
                    blk = which * 4 + h
                    k.memset('pool', cin[:, 0:3], 0.0)
                    yield

                    def evac(g, p, blk=blk, dstT=dstT, which=which):
                        k.cp('act', cin[:, 3:515], p[:])
                        k.ts('dve', cacc[:], cin[:, 3:515], convw[:, blk, 3:4], convb[:, blk:blk + 1], ALU.mult, ALU.add)
                        for j in range(3):
                            k.stt('dve', cacc[:], cin[:, j:j + 512], convw[:, blk, j:j + 1], cacc[:], ALU.mult, ALU.add)
                        k.cp('pool', cin[:, 0:3], cin[:, 512:515])
                        k.act(csig[:], cacc[:], AF.Tanh, scale=0.5)
                        sc = 0.5 if which == 0 else 0.5 * 128 ** -0.5
                        k.stt('dve', csig[:], csig[:], 1.0, cacc[:], ALU.add, ALU.mult)
                        k.ts('pool', dstT[:, g * 512:(g + 1) * 512], csig[:], sc, None, ALU.mult)
                    yield from projfm(B, Wqk, evac)
                yield from ktok_from_kT(B)
                loadw(B, B.W, MZ + h * 256, 256)
                yield
                yield from v_proj(B, B.W3)
                k.memset('pool', vaug[:, :, 256:257], 1.0)
                k.dma(B.gnb[:], A['ml_norm'][h * 256:(h + 1) * 256].partition_broadcast(128))
                k.ts('pool', B.gnb[:], B.gnb[:], 0.5, None, ALU.mult)
                k.memset('pool', B.r32[:], 0.0)
                yield
                inter_all, negm_all, wint_all, enm_all = B.sc16[:, 0, :], B.sc16[:, 1, :], B.sc16[:, 2, :], B.sc16[:, 3, :]
                k.tt('dve', inter_all, bcum[:, :, h], m_all[:, 0:NT, h], ALU.add)
                pt = B.ps['pT']
                for c in range(NT):
                    P = pb if c % 2 == 0 else pa
                    dG, lD, dM, Sb = B.p1[c % 2]
                    k.ts('pool', dG[:], identf[:], Gtok[:, c, h:h + 1], None, ALU.mult)
                    k.mm(P[:, 0:128], onesf[:], dG[:])
                    k.mm(P[:, 128:256], qT[:, c * 128:(c + 1) * 128], kT[:, c * 128:(c + 1) * 128])
                    yield
                    k.stt('dve', lD[:], P[:, 0:128], bcum[:, c, h:h + 1], negmask[:], ALU.add, ALU.add)
                    yield
                    k.red('dve', negm_all[:, c:c + 1], lD[:], ALU.max)
                    yield
                    k.tt('dve', negm_all[:, c:c + 1], negm_all[:, c:c + 1], inter_all[:, c:c + 1], ALU.max)
                    yield
                    k.ts('dve', negm_all[:, c:c + 1], negm_all[:, c:c + 1], -1.0, None, ALU.mult)
                    yield
                    k.act(dM[:], lD[:], AF.Exp, bias=negm_all[:, c:c + 1])
                    yield
                    k.tt('dve', Sb[:], P[:, 128:256], dM[:], ALU.mult)
                    yield
                    k.tr(pt[:, (c % 4) * 128:(c % 4 + 1) * 128], Sb[:], C.identb[:])
                    yield
                    k.cp('act', B.ST[:, c, :], pt[:, (c % 4) * 128:(c % 4 + 1) * 128])
                    yield
                k.tt('dve', wint_all, inter_all, negm_all, ALU.add)
                k.act(wint_all, wint_all, AF.Exp)
                k.act(enm_all, negm_all, AF.Exp)
                yield
                r32, rbf, ysb, tmp257 = B.r32, B.rbf, B.ysb, B.tmp257
                for c in range(NT):
                    projtm(B, B.W4, 256, c, pc)
                    projtm(B, B.W, 256, c, pc, c0=256)
                    k.act(B.sigo[:], pc[:, 0:256], AF.Tanh, scale=0.5)
                    k.act(B.gg[:], pc[:, 256:512], AF.Tanh, scale=0.5)
                    yield
                    k.ts('dve', B.sigo[:], B.sigo[:], 0.5, 0.5, ALU.mult, ALU.add)
                    k.stt('dve', B.gg[:], B.gg[:], 1.0, pc[:, 256:512], ALU.add, ALU.mult)
                    yield
                    k.tt('pool', B.gg[:], B.gg[:], B.gnb[:], ALU.mult)
                    k.mm(pa[:, 0:257], B.ST[:, c, :], vaug[:, c, 0:257])
                    if c > 0:
                        k.mm(pb[:, 0:257], qT[:, c * 128:(c + 1) * 128], rbf[:, 0:257])
                        yield
                        k.act(tmp257[:, 0:257], pb[:, 0:257], AF.Copy, scale=wint_all[:, c:c + 1])
                        yield
                        k.tt('dve', ysb[:, 0:257], tmp257[:, 0:257], pa[:, 0:257], ALU.add)
                    else:
                        yield
                        k.cp('act', ysb[:, 0:257], pa[:, 0:257])
                    yield
                    den = sm_[:, 26:27]
                    k.act(den, ysb[:, 256:257], AF.Abs)
                    yield
                    k.tt('dve', den, den, enm_all[:, c:c + 1], ALU.max)
                    yield
                    k.recip(den, den)
                    yield
                    k.stt('dve', ysb[:, 0:256], ysb[:, 0:256], den, B.sigo[:], ALU.mult, ALU.mult)
                    yield
                    if c < NT - 1:
                        wj = sm_[:, 28:29]
                        k.act(wj, Gtok[:, c, h:h + 1], AF.Exp, bias=nmaxG_all[:, c, h:h + 1])
                        yield
                        k.ts('dve', B.vz[:, 0:257], vaug[:, c, 0:257], wj, None, ALU.mult)
                        yield
                        k.mm(pc[:, 0:257], B.ktok[:, c, :], B.vz[:, 0:257])
                        yield
                        k.act(tmp257[:, 0:257], pc[:, 0:257], AF.Copy, scale=sc_all[:, c, h:h + 1])
                        yield
                        k.stt('dve', r32[:, 0:257], r32[:, 0:257], sp_all[:, c, h:h + 1], tmp257[:, 0:257], ALU.mult, ALU.add)
                        yield
                        k.cp('act', rbf[:, 0:257], r32[:, 0:257])
                        yield
                    k.bnstats(sm_[:, 8:14], ysb[:, 0:256])
                    k.bnaggr(sm_[:, 14:16], sm_[:, 8:14])
                    yield
                    rstd = sm_[:, 16:17]
                    k.ts('dve', rstd, sm_[:, 15:16], EPS, None, ALU.add)
                    k.tt('pool', rstd, rstd, C.neghalf[:], ALU.pow)
                    yield
                    k.ts('dve', ysb[:, 0:256], ysb[:, 0:256], sm_[:, 14:15], rstd, ALU.subtract, ALU.mult)
                    yield
                    k.tt('dve', B.mixb[:], ysb[:, 0:256], B.gg[:], ALU.mult)
                    mix_out(B, 8 + h * 2, c)
                    yield

            print("L0 mixer sbuf remaining", k.nc.sbuf_bytes_remaining, flush=True)
            norm_to_uT(k, C, x_dram, gpreT, uT)
            for pi, pair in enumerate(((0, 1), (2, 3))):
                if L0_STOP <= 1 + pi:
                    break
                run_interleaved([ret_head(h, slots[i]) for i, h in enumerate(pair)][:L0_NG], stagger=STAG_RET)
            B0 = slots[0]
            loadw(B0, B0.W, MI, 8)
            for T in range(NT):
                p = C.pa[T % 2]
                projtm(B0, B0.W, 8, T, p)
                k.tt('dve', ifg[:, T, :], p[:, 0:8], gateb[:], ALU.add)
            k.act(logf[:], ifg[:, :, 4:8], AF.Exp, scale=-1.0)
            k.act(logf[:], logf[:], AF.Ln, bias=1.0)
            k.ts('dve', logf[:], logf[:], -1.0, None, ALU.mult)
            pbc = C.pa[0]
            for T in range(NT):
                k.mm(pbc[:, T * 4:(T + 1) * 4], triu[:], logf[:, T, :])
            k.cp('dve', bcum[:].rearrange("p t h -> p (t h)"), pbc[:, 0:64])
            k.tt('dve', Gtok[:], ifg[:, :, 0:4], bcum[:], ALU.subtract)
            pbl = C.pa[1]
            k.mm(pbl[:, 0:64], onesf[:], logf[:].rearrange("p t h -> p (t h)"))
            k.cp('dve', blast[:].rearrange("p t h -> p (t h)"), pbl[:, 0:64])
            k.memset('pool', mst[:], NEG)
            dgt = [slots[0].p1[0][0], slots[1].p1[0][0]]
            pgt = [PS[0]['pb'], PS[1]['pb']]
            for c in range(NT):
                for hh in range(4):
                    i_ = (c * 4 + hh) % 2
                    k.ts('pool', dgt[i_][:], identf[:], Gtok[:, c, hh:hh + 1], None, ALU.mult)
                    k.mm(pgt[i_][:, 0:128], onesf[:], dgt[i_][:])
                    k.red('dve', maxG_all[:, c, hh:hh + 1], pgt[i_][:, 0:128], ALU.max)
            k.tt('dve', mu_all[:], blast[:], maxG_all[:], ALU.add)
            k.ts('dve', nmaxG_all[:], maxG_all[:], -1.0, None, ALU.mult)
            k.memset('pool', m_all[:, 0, :], NEG)
            for c in range(NT):
                k.tt('dve', bm_all[:, c, :], blast[:, c, :], m_all[:, c, :], ALU.add)
                k.tt('dve', m_all[:, c + 1, :], bm_all[:, c, :], mu_all[:, c, :], ALU.max)
            k.tt('dve', sp_all[:], bm_all[:], m_all[:, 1:NT + 1, :], ALU.subtract)
            k.act(sp_all[:], sp_all[:], AF.Exp)
            k.tt('dve', sc_all[:], mu_all[:], m_all[:, 1:NT + 1, :], ALU.subtract)
            k.act(sc_all[:], sc_all[:], AF.Exp)
            for pi, pair in enumerate(((0, 1), (2, 3))):
                if L0_STOP <= 4 + pi:
                    break
                run_interleaved([ml_head2(h, slots[i]) for i, h in enumerate(pair)][:L0_NG], stagger=STAG_ML)
        k.barrier()
        with ExitStack() as st3:
            sb3 = lambda n, s, d=F32: k.sb(n, s, d, st3)
            gpost_b = sb3("gpostb", [128, 1024])
            k.dma(gpost_b[:], A['gpost0'][:].partition_broadcast(128))
            Wg = sb3("Wg", [128, 8, 1024], BF16)
            Wp = sb3("Wp", [128, 2, 1024], BF16)
            Wo = sb3("Wo0", [128, 16, 1024], BF16)
            for half in range(2):
                load_w(k, C, Wo[:, :, half * 512:(half + 1) * 512], A['w_out0'], 0, 16, half * 512, 512, eng='rr')
            for half in range(2):
                load_w(k, C, Wg[:, :, half * 512:(half + 1) * 512], A['pe_gate0'], 0, 8, half * 512, 512, eng='rr')
                load_w(k, C, Wp[:, :, half * 512:(half + 1) * 512], A['pe_proj0'], 0, 2, half * 512, 512, eng='rr')
            mt = [sb3("mt0", [128, 16, 128], BF16), sb3("mt1", [128, 16, 128], BF16)]
            PSL = [dict(pT=PS[i]['pT'], pg=PS[i]['pa'], pp=PS[i]['pc'], py=PS[i]['pb']) for i in range(2)]

            def gen_y(T, s_):
                m = mt[s_.i]
                k.dma(m[:], mixd[:, :, T * 128:(T + 1) * 128].rearrange("c p t -> p c t"))
                p = s_.ps['py']
                for half in range(2):
                    for fc in range(16):
                        k.mm(p[:], m[:, fc, :], Wo[:, fc, half * 512:(half + 1) * 512], start=(fc == 0), stop=(fc == 15))
                    yield
                    k.cp('act', s_.ytile[:, half * 512:(half + 1) * 512], p[:])
                    yield
            toks = post_block(k, C, sb3, PSL, x_dram, A['p0'], gpost_b, Wg, Wp, out_dram, gen_y)
        k.barrier()
    return toks


def l0_inputs(inp, b):
    d = {}
    d['x'] = np.ascontiguousarray(inp['x'][b])
    d['p0'] = np.ascontiguousarray(inp['p'][0, b])
    d['gpre0T'] = np.ascontiguousarray(inp['norm_pre'][0].reshape(8, 128).T)
    d['gpost0'] = np.ascontiguousarray(inp['norm_post'][0].reshape(1024))
    d['w_in0'] = np.ascontiguousarray(inp['ab_w_in'][0])
    d['ret_norm'] = np.ascontiguousarray(inp['ret_norm'][0].reshape(1024))
    d['ml_norm'] = np.ascontiguousarray(inp['ml_norm'][0].reshape(1024))
    d['convw'] = np.ascontiguousarray(inp['ml_conv_w'][0].T.reshape(8, 128, 4).transpose(1, 0, 2))
    d['convb'] = np.ascontiguousarray(inp['ml_conv_b'][0].reshape(8, 128).T)
    d['gateb'] = np.ascontiguousarray(inp['ml_gate_b'][0].reshape(8))
    d['w_out0'] = np.ascontiguousarray(inp['ab_w_out'][0])
    d['pe_gate0'] = np.ascontiguousarray(inp['pe_gate'][0])
    d['pe_proj0'] = np.ascontiguousarray(inp['pe_proj'][0])
    for kk, v in host_consts_l0().items():
        d['c_' + kk] = v
    return d


def declare(nc, d, skip=()):
    A = {}
    for kk, v in d.items():
        if kk in skip:
            continue
        dt = F32 if v.dtype == np.float32 else BF16
        A[kk] = nc.dram_tensor(kk, list(v.shape), dt, kind="ExternalInput").ap()
    return A


def build_l0(d):
    nc = bass.Bass("TRN2", target_bir_lowering=False)
    A = declare(nc, d)
    out = nc.dram_tensor("out", [S, D], F32, kind="ExternalOutput").ap()
    with ExitStack() as stack:
        k = KB(nc, stack)
        mixd = nc.dram_tensor("mix0T", [16, 128, S], BF16, kind="Internal").ap()
        toks = layer0(k, nc, A, A['x'], out, mixd)
        k.finish(toks)
        print("L0 inst", k.ninst, "waits", k.nwait, flush=True)
    return nc


SLOPES = [2.0 ** (-(h + 1) / 2.0) for h in range(16)]
MASKV = -60000.0
CQ, CKC, CVC, CKS, CVS, CKW, CVW, CG, CNZ, CSU, CSZ = 0, 1024, 1280, 1536, 1792, 2048, 2304, 2560, 2608, 3632, 4144


def _bf(x):
    return np.float64(np.float32(x).astype(ml_dtypes.bfloat16).astype(np.float32))


def _split3(x):
    hi = _bf(x)
    mid = _bf(x - hi)
    lo = _bf(x - hi - mid)
    return hi, mid, lo


def host_consts_l1():
    c = {}
    c['ident'] = np.eye(128, dtype=np.float32)
    kc = np.zeros((4, 32, 2048), np.float32)
    qc = np.zeros((4, 32, 2048), np.float32)
    t = np.arange(2048)
    for g in range(4):
        for j in range(4):
            pcs = _split3(SLOPES[4 * g + j])
            for r in range(3):
                kc[g, 6 * j + r, :] = -8.0 * 128.0 * pcs[r]
                kc[g, 6 * j + 3 + r, :] = -8.0 * pcs[r]
    for j in range(4):
        for r in range(3):
            qc[j, 6 * j + r, :] = t // 128
            qc[j, 6 * j + 3 + r, :] = t % 128
    c['kconst'] = kc
    c['qconst'] = qc
    c['E'] = (t[None, :] // 64 == np.arange(32)[:, None]).astype(np.float32)
    p = np.arange(128)
    c['pos'] = (128.0 * np.arange(16)[None, :] + p[:, None]).astype(np.float32)
    c['posc'] = (16.0 * p + 31.0).astype(np.float32).reshape(128, 1)
    c['cm'] = np.where(p[:, None] <= p[None, :], 0.0, MASKV).astype(np.float32)
    c['am'] = np.where(p[:, None] > p[None, :], 0.0, MASKV).astype(np.float32)
    cmpm = np.where((16 * p[:, None] + 31) <= t[None, :], 0.0, MASKV).astype(np.float32)
    cmpm[127, :] = MASKV
    c['cmpmask'] = cmpm
    d0 = (p[:, None] - 16.0 * p[None, :] - 31.0).astype(np.float32)
    d0[:, 127] = -1e6
    c['dist0'] = d0
    ct = np.zeros((128, 16, 32), np.float32)
    for T in range(16):
        tt = T * 128 + p
        n = np.arange(32)
        forced = (n[None, :] == 0) | (n[None, :] == (tt // 64)[:, None])
        valid = (n[None, :] * 64) <= tt[:, None]
        ct[:, T, :] = np.where(forced, 1e9, np.where(valid, 0.0, -1.0))
    c['ct'] = ct
    c['rowvalid'] = (p >= 31).astype(np.float32).reshape(128, 1)
    c['iota'] = np.broadcast_to(np.arange(512, dtype=np.float32)[None, :], (128, 512)).copy()
    c['halfmask'] = (p[:, None] // 64 == np.arange(2)[None, :]).astype(np.float32)
    return c


def proj_fm64(k, C, W, uT, evac):
    for g in range(4):
        ps = C.pa[C.pai % 2]
        C.pai += 1
        for kc in range(8):
            k.mm(ps[0:64, :], W[:, kc, 0:64], uT[:, kc, g * 512:(g + 1) * 512], start=(kc == 0), stop=(kc == 7))
        evac(g, ps)


def stage_rows(k, C, dst, src, p0, p1, ncols, eng='pool'):
    k.dma(dst[p0:p1, 0:ncols], src[:, 0:ncols], eng='pool')


def norm2max(k, C, srcT, ncols, dst):
    npieces = (ncols + 511) // 512
    scr = C.nmx[C.nmi % 2]
    C.nmi += 1
    for i, c0 in enumerate(range(0, ncols, 512)):
        n = min(512, ncols - c0)
        sq = C.sqbs[C.sqi % 2]
        ps = C.pa[C.sqi % 2]
        C.sqi += 1
        k.act(sq[0:64, 0:n], srcT[0:64, c0:c0 + n], AF.Square)
        k.mm(ps[:, 0:n], C.onesb[0:64, :], sq[0:64, 0:n])
        k.red('dve', scr[:, i:i + 1], ps[:, 0:n], ALU.max)
    if npieces == 1:
        k.cp('dve', dst, scr[:, 0:1])
    else:
        k.red('dve', dst, scr[:, 0:npieces], ALU.max)


L1_STOP = 99
L1_VAR = 0
L1_S5 = True


def layer1(k, nc, A, h_dram, out_dram):
    with ExitStack() as st:
        C = Ctx()
        C.wsi = 0
        C.pai = 0
        C.pti = 0
        sb = lambda n, s, d=F32: k.sb(n, s, d, st)
        ps = lambda n, s, d=F32: k.ps(n, s, d, st)
        C.pa = [ps("pa0", [128, 512]), ps("pa1", [128, 512])]
        C.psc = [ps("psc0", [128, 512]), ps("psc1", [128, 512])]
        C.po = [ps("po0", [128, 512]), ps("po1", [128, 512])]
        C.pm = C.pa[1]
        C.pT = [ps("pT0", [128, 1024], BF16), ps("pT1", [128, 1024], BF16)]
        identf = sb("identf", [128, 128])
        C.identf = identf
        C.identb = sb("identb", [128, 128], BF16)
        k.dma(identf[:], A['d_ident'][:, :])
        k.cp('dve', C.identb[:], identf[:])
        onesf = sb("onesf", [128, 128])
        k.memset('pool', onesf[:], 1.0)
        C.onesb = sb("onesb", [128, 128], BF16)
        k.memset('pool', C.onesb[:], 1.0)
        C.neghalf = sb("neghalf", [128, 1])
        k.memset('pool', C.neghalf[:], -0.5)
        gpreT = sb("gpreT", [128, 8])
        k.dma(gpreT[:], A['gpre1T'][:, :])
        C.xt = [sb("xt0", [128, 1024]), sb("xt1", [128, 1024])]
        C.junk = sb("junk", [128, 1024])
        C.xn = sb("xn", [128, 1024], BF16)
        C.junks = [C.junk, sb("junk2", [128, 1024])]
        C.xns = [C.xn, sb("xn2", [128, 1024], BF16)]
        C.sm = sb("sm", [128, 64])
        sm = C.sm
        mixT = sb("mixT", [128, 12, S], BF16)
        w_in = A['w_in1']
        with ExitStack() as stu:
            uT = k.sb("uT", [128, 8, S], BF16, stu)
            norm_to_uT(k, C, h_dram, gpreT, uT)
            with ExitStack() as st2:
                sb2 = lambda n, s, d=F32: k.sb(n, s, d, st2)
                Wrot = [sb2(f"Wr{i_}", [128, 8, 128], BF16) for i_ in range(3)]
                wri = [0]

                def nextW():
                    wri[0] += 1
                    return Wrot[wri[0] % 3]
                W = Wrot[0]
                qaug = [sb2(f"qaug{j}", [128, S], BF16) for j in range(4)]
                kaug_s = sb2("kaug_s", [128, S], BF16)
                kaug_w = sb2("kaug_w", [128, S], BF16)
                kaug_c = sb2("kaug_c", [128, 128], BF16)
                kc8 = sb2("kc8", [128, 128], BF16)
                vaug_s = sb2("vaug_s", [128, NT, 66], BF16)
                vaug_w = sb2("vaug_w", [128, NT, 66], BF16)
                vcaug = sb2("vcaug", [128, 66], BF16)
                kcmpT = sb2("kcmpT", [128, S], BF16)
                vcmpT = sb2("vcmpT", [128, S], BF16)
                w1k = sb2("w1k", [128, 32, 128], BF16)
                w1v = sb2("w1v", [128, 32, 128], BF16)
                w2k = sb2("w2k", [128, 64], BF16)
                w2v = sb2("w2v", [128, 64], BF16)
                posk = sb2("posk", [128, 32], BF16)
                posv = sb2("posv", [128, 32], BF16)
                gates = sb2("gates", [128, NT, 48])
                ynsa = sb2("ynsa", [128, NT, 128])
                PT = [sb2(f"PT{i}", [128, 512], BF16) for i in range(3)]
                C.sqb = sb2("sqb", [128, 512], BF16)
                C.sqbs = [C.sqb, sb2("sqb2", [128, 512], BF16)]
                C.nmx = [sb2("nmx0", [128, 4]), sb2("nmx1", [128, 4])]
                C.nmi = 0
                C.sqi = 0
                cmb = sb2("cmb", [128, 128], BF16)
                amb = sb2("amb", [128, 128], BF16)
                cmpmask = sb2("cmpmask", [128, S], BF16)
                dist0 = sb2("dist0", [128, 128])
                ctab = sb2("ctab", [128, NT, 32])
                pos = sb2("pos", [128, 16])
                posc = sb2("posc", [128, 1])
                rowvalid = sb2("rowvalid", [128, 1])
                bias_all = sb2("bias_all", [128, 16])
                brow = sb2("brow", [128, 128], BF16)
                nsl = sb2("nsl", [128, 4, 128])
                slt = sb2("slt", [128, 4, 128])
                sc4 = sb2("sc4", [128, 4, 128])
                e4 = sb2("e4", [128, 4, 128])

                class SelSlot:
                    pass
                selslots = []
                for i_ in range(2):
                    s_ = SelSlot()
                    s_.dm = sb2(f"dm{i_}", [128, 128])
                    s_.dmk = sb2(f"dmk{i_}", [128, 128])
                    s_.slt = sb2(f"slt{i_}", [128, 4, 128]) if i_ else slt
                    s_.sc4 = sb2(f"sc4{i_}", [128, 4, 128]) if i_ else sc4
                    s_.e4 = sb2(f"e4{i_}", [128, 4, 128]) if i_ else e4
                    s_.pg = sb2(f"pg{i_}", [128, 128])
                    s_.imp = sb2(f"imp{i_}", [128, 32])
                    s_.top8 = sb2(f"top8{i_}", [128, 8])
                    s_.selb = sb2(f"selb{i_}", [128, 128], BF16)
                    s_.selT = sb2(f"selT{i_}", [128, 128], BF16)
                    s_.sm = sb2(f"ssm{i_}", [128, 8])
                    s_.pq = C.psc[i_]
                    s_.pT = C.pT[i_]
                    k.memset('pool', s_.selb[:], 0.0)
                    selslots.append(s_)
                Mall = sb2("Mall", [128, 4, 3])
                qn2 = sb2("qn2", [128, 8])
                bias_all3 = sb2("bias_all3", [128, 12, 16])
                hbf = sb2("hbf", [128, 128], BF16)
                otmp = sb2("otmp", [128, 4, 64])
                nzs = sb2("nzs", [128, 128])
                mixb = sb2("mixb", [128, 128], BF16)
                print("NSA sbuf remaining", k.nc.sbuf_bytes_remaining, flush=True)
                k.dma(dist0[:], A['d_dist0'][:, :])
                k.dma(ctab[:], A['d_ct'][:, :, :])
                k.dma(pos[:], A['d_pos'][:, :])
                k.dma(posc[:], A['d_posc'][:, :])
                k.dma(rowvalid[:], A['d_rowvalid'][:, :])
                k.dma(cmb[:], A['d_cm'][:, :], eng='pool')
                k.dma(amb[:], A['d_am'][:, :], eng='pool')
                stage_rows(k, C, cmpmask, A['d_cmpmask'], 0, 128, S, eng='rr')
                for j in range(4):
                    stage_rows(k, C, qaug[j], A['d_qconst'][j], 64, 96, S, eng='rr')
                    k.memset('pool', qaug[j][96:128, :], 0.0)
                stage_rows(k, C, kaug_s, A['d_E'], 96, 128, S, eng='rr')
                k.memset('pool', kaug_w[96:128, :], 0.0)
                k.memset('pool', kaug_c[:], 0.0)
                k.memset('pool', kc8[:], 0.0)
                k.memset('pool', vaug_s[:, :, 64:65], 1.0)
                k.memset('pool', vaug_w[:, :, 64:65], 1.0)
                k.memset('pool', vcaug[:], 0.0)
                k.memset('pool', vcaug[0:127, 64:65], 1.0)
                for (w1, src) in ((w1k, A['w1k']), (w1v, A['w1v'])):
                    for l0 in range(0, 32, 8):
                        k.dma(w1[0:64, l0:l0 + 8, :], src[:, l0:l0 + 8, :], eng='pool')
                for (w2, src) in ((w2k, A['w2k']), (w2v, A['w2v'])):
                    k.dma(w2[:], src[:, :], eng='pool')
                for (pp, src) in ((posk, A['poskT']), (posv, A['posvT'])):
                    k.dma(pp[0:64, :], src[:, :], eng='pool')
                load_w(k, C, W, w_in, 0, 8, CG, 48)
                for T in range(NT):
                    p = C.pa[C.pai % 2]
                    C.pai += 1
                    proj_tm(k, C, W, 0, 48, uT, T, p)
                    k.act(gates[:, T, :], p[:, 0:48], AF.Sigmoid)

                for g in range(4):
                    if L1_STOP <= 1 or (L1_STOP <= 6 and g > 0):
                        break
                    stage_rows(k, C, kaug_s, A['d_kconst'][g], 64, 96, S)
                    stage_rows(k, C, kaug_w, A['d_kconst'][g], 64, 96, S)
                    stage_rows(k, C, kaug_c, A['d_kconst'][g][:, 0:128], 64, 96, 128)
                    for col0, dst in ((CKS + g * 64, kaug_s), (CKW + g * 64, kaug_w), (CKC + g * 64, kcmpT), (CVC + g * 64, vcmpT)):
                        W = nextW()
                        load_w(k, C, W, w_in, 0, 8, col0, 64)
                        proj_fm64(k, C, W, uT, lambda gg_, p, dst=dst: k.cp('act', dst[0:64, gg_ * 512:(gg_ + 1) * 512], p[0:64, :]))
                    for col0, dst in ((CVS + g * 64, vaug_s), (CVW + g * 64, vaug_w)):
                        W = nextW()
                        load_w(k, C, W, w_in, 0, 8, col0, 64)
                        for T in range(NT):
                            p = C.pa[C.pai % 2]
                            C.pai += 1
                            proj_tm(k, C, W, 0, 64, uT, T, p)
                            k.cp('act', dst[:, T, 0:64], p[:, 0:64])
                    if L1_STOP <= 2:
                        break
                    for which, (srcT, w1, w2, pp) in enumerate(((kcmpT, w1k, w2k, posk), (vcmpT, w1v, w2v, posv))):
                        ph = C.pm
                        pbrow = C.pa[C.pai % 2]
                        C.pai += 1
                        for l in range(32):
                            k.mm(pbrow[0:1, 0:128], pp[0:64, l:l + 1], w1[0:64, l, :], start=(l == 0), stop=(l == 31))
                        k.cp('act', brow[0:1, :], pbrow[0:1, 0:128])
                        if L1_STOP <= 2.2:
                            break
                        v3 = srcT[0:64, :].rearrange("p (n l) -> p n l", l=16)
                        for l in range(32):
                            rhs = v3[:, 0:127, l] if l < 16 else v3[:, 1:128, l - 16]
                            k.mm(ph[:, 0:127], w1[0:64, l, :], rhs, start=(l == 0), stop=False)
                        k.mm(ph[:, 0:127], brow[0:1, :], C.onesb[0:1, 0:127], start=False, stop=True)
                        if L1_STOP <= 2.4:
                            break
                        k.act(hbf[:, 0:127], ph[:, 0:127], AF.Gelu_apprx_tanh)
                        if L1_STOP <= 2.8:
                            break
                        if which == 0:
                            pk2 = C.pa[C.pai % 2]
                            C.pai += 1
                            k.mm(pk2[0:64, 0:127], w2[:], hbf[:, 0:127])
                            k.cp('act', kaug_c[0:64, 0:127], pk2[0:64, 0:127])
                            k.ts('pool', kc8[0:64, 0:127], kaug_c[0:64, 0:127], 0.125, None, ALU.mult)
                        else:
                            pk2 = C.pa[C.pai % 2]
                            C.pai += 1
                            k.mm(pk2[0:127, 0:64], hbf[:, 0:127], w2[:])
                            k.cp('act', vcaug[0:127, 0:64], pk2[0:127, 0:64])
                    if L1_STOP <= 3:
                        break
                    for j in range(4):
                        h = 4 * g + j
                        W = nextW()
                        load_w(k, C, W, w_in, 0, 8, CQ + h * 64, 64)
                        proj_fm64(k, C, W, uT, lambda gg_, p, j=j: k.cp('act', qaug[j][0:64, gg_ * 512:(gg_ + 1) * 512], p[0:64, :]))
                    if L1_STOP <= 4:
                        break
                    for j in range(4):
                        k.memset('pool', nsl[:, j, :], -SLOPES[4 * g + j])
                    def sel_gen(T, s_):
                        dm, dmk, slt, sc4, e4, pg, imp, top8, selb, selT, sm_ = (s_.dm, s_.dmk, s_.slt, s_.sc4, s_.e4, s_.pg,
                                                                              s_.imp, s_.top8, s_.selb, s_.selT, s_.sm)
                        k.ts('dve', dm[:], dist0[:], float(128 * T), None, ALU.add)
                        k.ts('dve', dmk[:], dm[:], 0.0, 1e32, ALU.is_lt, ALU.mult)
                        yield
                        k.tt('dve', dm[:], dm[:], dmk[:], ALU.add)
                        pq_ = s_.pq
                        pq3 = pq_[:].rearrange("p (h j) -> p h j", h=4)
                        for j in range(4):
                            k.mm(pq3[:, j, :], qaug[j][0:64, T * 128:(T + 1) * 128], kc8[0:64, :])
                        yield
                        k.tt('pool', slt[:], nsl[:], dm[:].unsqueeze(1).to_broadcast([128, 4, 128]), ALU.mult)
                        yield
                        k.tt('dve', sc4[:], pq3, slt[:], ALU.add)
                        yield
                        k.red('dve', sm_[:, 0:4], sc4[:], ALU.max)
                        yield
                        k.tt('dve', sc4[:], sc4[:], sm_[:, 0:4].unsqueeze(2).to_broadcast([128, 4, 128]), ALU.subtract)
                        yield
                        k.act(e4[:], sc4[:], AF.Exp)
                        yield
                        k.red('dve', sm_[:, 4:8], e4[:], ALU.add)
                        yield
                        k.recip(sm_[:, 4:8], sm_[:, 4:8])
                        yield
                        if T == 0:
                            k.tt('dve', sm_[:, 4:8], sm_[:, 4:8], rowvalid[:].to_broadcast([128, 4]), ALU.mult)
                            yield
                        k.tt('pool', e4[:], e4[:], sm_[:, 4:8].unsqueeze(2).to_broadcast([128, 4, 128]), ALU.mult)
                        yield
                        k.red('dve', pg[:], e4[:].rearrange("p h j -> p j h"), ALU.add)
                        yield
                        pg3 = pg[:].rearrange("p (n f) -> p n f", f=4)
                        k.red('dve', imp[:], pg3, ALU.add)
                        yield
                        k.tt('dve', imp[:, 1:32], imp[:, 1:32], pg3[:, 0:31, 3], ALU.add)
                        yield
                        k.tt('dve', imp[:], imp[:], ctab[:, T, :], ALU.add)
                        yield
                        k.max8(top8[:], imp[:])
                        yield
                        k.ts('dve', imp[:], imp[:], top8[:, 3:4], None, ALU.is_ge)
                        yield
                        k.ts('dve', selb[:, 96:128], imp[:], 1.0, -MASKV, ALU.subtract, ALU.mult)
                        yield
                        pt = s_.pT
                        k.tr(pt[:, 0:128], selb[:], C.identb[:])
                        yield
                        k.cp('act', selT[96:128, :], pt[96:128, 0:128])
                        yield
                        for j in range(4):
                            k.cp('pool', qaug[j][96:128, T * 128:(T + 1) * 128], selT[96:128, :])
                        yield
                    for T0 in range(0, NT, 2):
                        run_interleaved([sel_gen(T0 + i, selslots[i]) for i in range(2)], stagger=0)
                    if L1_STOP <= 5:
                        break
                    norm2max(k, C, kaug_c, 127, qn2[:, 4:5])
                    norm2max(k, C, kaug_s, S, qn2[:, 5:6])
                    norm2max(k, C, kaug_w, S, qn2[:, 6:7])
                    for j in range(4):
                        norm2max(k, C, qaug[j], S, qn2[:, j:j + 1])
                    for br in range(3):
                        k.ts('dve', Mall[:, :, br], qn2[:, 0:4], qn2[:, 4 + br:5 + br], None, ALU.mult)
                    k.tt('pool', Mall[:].rearrange("p a b -> p (a b)"), Mall[:].rearrange("p a b -> p (a b)"),
                         C.neghalf[:].to_broadcast([128, 12]), ALU.pow)
                    k.recip(Mall[:].rearrange("p a b -> p (a b)"), Mall[:].rearrange("p a b -> p (a b)"))
                    k.ts('dve', Mall[:], Mall[:], 1.02 / 8.0, None, ALU.mult)
                    for j in range(4):
                        for br in range(3):
                            src = posc[:] if br == 0 else pos[:]
                            dstb = bias_all3[:, j * 3 + br, 0:1] if br == 0 else bias_all3[:, j * 3 + br, :]
                            k.ts('dve', dstb, src, SLOPES[4 * g + j], Mall[:, j, br:br + 1], ALU.mult, ALU.subtract)
                    for j in range(4):
                        h = 4 * g + j
                        for br in range(3):
                            bias_all = bias_all3[:, j * 3 + br, :]
                            kaug = (kaug_c, kaug_s, kaug_w)[br]
                            gcol = h * 3 + br
                            vaug_br = (None, vaug_s, vaug_w)[br]
                            items = []
                            for Q in range(4):
                                its = []
                                if br == 0:
                                    nk = min(127, 32 * (Q + 1))
                                    its.append([Q, 0, nk, 512 * Q, 512 * Q + 512, 'cmp'])
                                elif br == 1:
                                    for S_ in range(0, 4 * Q + 4):
                                        its.append([Q, S_, 128, max(512 * Q, 128 * S_), 512 * Q + 512, 'slc'])
                                else:
                                    for S_ in range(max(0, 4 * Q - 4), 4 * Q + 4):
                                        t0 = 128 * max(S_, 4 * Q)
                                        t1 = 128 * (min(S_ + 4, 4 * Q + 3) + 1)
                                        its.append([Q, S_, 128, t0, t1, 'win'])
                                nmm = sum((it[4] - it[3]) // 128 for it in its)
                                for ii, it in enumerate(its):
                                    it.append(ii == len(its) - 1)
                                    it.append(nmm)
                                items += its
                            immc = [0, 0, 0, 0]

                            def emit_S(it, idx):
                                Q, S_, nk, t0, t1, kind, lastq, nmm = it
                                n = t1 - t0
                                psc = psc3[idx % 3]
                                if kind == 'cmp':
                                    k.mm(psc[0:nk, 0:n], kaug[:, 0:nk], qaug[j][:, t0:t1], start=True, stop=False)
                                    k.mm(psc[0:nk, 0:n], C.identb[0:nk, 0:nk], cmpmask[0:nk, t0:t1], start=False, stop=True)
                                else:
                                    masks = []
                                    if 128 * S_ >= 512 * Q:
                                        masks.append((0, cmb))
                                    if kind == 'win' and S_ + 4 <= 4 * Q + 3:
                                        masks.append((n - 128, amb))
                                    k.mm(psc[:, 0:n], kaug[:, S_ * 128:(S_ + 1) * 128], qaug[j][:, t0:t1], start=True, stop=(len(masks) == 0))
                                    for mi, (c0, mt) in enumerate(masks):
                                        k.mm(psc[:, c0:c0 + 128], C.identb[:], mt[:], start=False, stop=(mi == len(masks) - 1))
                                return psc

                            def emit_E(it, idx, psc):
                                Q, S_, nk, t0, t1, kind, lastq, nmm = it
                                n = t1 - t0
                                bias = bias_all[0:nk, 0:1] if kind == 'cmp' else bias_all[:, S_:S_ + 1]
                                pt_ = PT[idx % 3]
                                k.act(pt_[0:nk, 0:n], psc[0:nk, 0:n], AF.Exp, bias=bias, scale=0.125)
                                return pt_

                            def emit_PV(it, pt_):
                                Q, S_, nk, t0, t1, kind, lastq, nmm = it
                                n = t1 - t0
                                po = C.po[Q % 2]
                                po3 = po[:].rearrange("p (s c) -> p s c", s=4)
                                for sub in range(n // 128):
                                    Tq = t0 // 128 + sub - 4 * Q
                                    rhs = vcaug[0:nk, 0:65] if kind == 'cmp' else vaug_br[:, S_, 0:65]
                                    k.mm(po3[:, Tq, 0:65], pt_[0:nk, sub * 128:(sub + 1) * 128], rhs,
                                         start=(immc[Q] == 0), stop=(immc[Q] == nmm - 1), skip_group_check=True)
                                    immc[Q] += 1
                                if lastq:
                                    rd = sm[:, 50:54]
                                    k.ts('dve', rd, po3[:, :, 64], 1e-30, None, ALU.max)
                                    k.recip(rd, rd)
                                    k.tt('dve', rd, rd, gates[:, 4 * Q:4 * Q + 4, gcol], ALU.mult)
                                    ydst = ynsa[:, 4 * Q:4 * Q + 4, (j % 2) * 64:(j % 2) * 64 + 64]
                                    if br == 0:
                                        k.tt('dve', ydst, po3[:, :, 0:64], rd.unsqueeze(2).to_broadcast([128, 4, 64]), ALU.mult)
                                    else:
                                        k.tt('dve', otmp[:], po3[:, :, 0:64], rd.unsqueeze(2).to_broadcast([128, 4, 64]), ALU.mult)
                                        k.tt('pool', ydst, ydst, otmp[:], ALU.add)
                            psc3 = [C.psc[0], C.psc[1], C.pa[0]]
                            nit = len(items)
                            pscs = {}
                            for idx in range(min(2, nit)):
                                pscs[idx] = emit_S(items[idx], idx)
                            for idx, it in enumerate(items):
                                pt_ = emit_E(it, idx, pscs.pop(idx))
                                if idx + 2 < nit:
                                    pscs[idx + 2] = emit_S(items[idx + 2], idx + 2)
                                emit_PV(it, pt_)
                        if j % 2 == 1:
                            m = h // 2
                            W = nextW()
                            load_w(k, C, W, w_in, 0, 8, CNZ + m * 128, 128)
                            for T in range(NT):
                                p = C.pa[C.pai % 2]
                                C.pai += 1
                                proj_tm(k, C, W, 0, 128, uT, T, p)
                                k.act(nzs[:], p[:, 0:128], AF.Tanh, scale=0.5)
                                k.stt('dve', nzs[:], nzs[:], 1.0, p[:, 0:128], ALU.add, ALU.mult)
                                k.stt('dve', mixb[:], nzs[:], 0.5, ynsa[:, T, :], ALU.mult, ALU.mult)
                                pt = C.pT[C.pti % 2]
                                C.pti += 1
                                k.tr(pt[:, 0:128], mixb[:], C.identb[:])
                                k.cp('act', mixT[:, m, T * 128:(T + 1) * 128], pt[:, 0:128])
            k.barrier()
            with ExitStack() as st2:
                sb2 = lambda n, s, d=F32: k.sb(n, s, d, st2)
                if L1_S5:
                    s5_block(k, C, A, sb2, uT, mixT, identf)
            k.barrier()
        k.barrier()
        with ExitStack() as st3:
            sb3 = lambda n, s, d=F32: k.sb(n, s, d, st3)
            gpost_b = sb3("gpostb", [128, 1024])
            k.dma(gpost_b[:], A['gpost1'][:].partition_broadcast(128))
            Wg = sb3("Wg", [128, 8, 1024], BF16)
            Wp = sb3("Wp", [128, 2, 1024], BF16)
            Wo = sb3("Wo1", [128, 12, 1024], BF16)
            for half in range(2):
                load_w(k, C, Wo[:, :, half * 512:(half + 1) * 512], A['w_out1'], 0, 12, half * 512, 512, eng='rr')
            for half in range(2):
                load_w(k, C, Wg[:, :, half * 512:(half + 1) * 512], A['pe_gate1'], 0, 8, half * 512, 512, eng='rr')
                load_w(k, C, Wp[:, :, half * 512:(half + 1) * 512], A['pe_proj1'], 0, 2, half * 512, 512, eng='rr')
            PSL = [dict(pT=C.pT[i], pg=C.pa[i], pp=C.psc[i], py=C.po[i]) for i in range(2)]

            def gen_y(T, s_):
                p = s_.ps['py']
                for half in range(2):
                    for fc in range(12):
                        k.mm(p[:], mixT[:, fc, T * 128:(T + 1) * 128], Wo[:, fc, half * 512:(half + 1) * 512],
                             start=(fc == 0), stop=(fc == 11))
                    yield
                    k.cp('act', s_.ytile[:, half * 512:(half + 1) * 512], p[:])
                    yield
            toks = post_block(k, C, sb3, PSL, h_dram, A['p1'], gpost_b, Wg, Wp, out_dram, gen_y)
        k.barrier()
    return toks


def s5_block(k, C, A, sb2, uT, mixT, identf):
    sm = C.sm
    w_in = A['w_in1']
    TWO_PI = 2.0 * math.pi
    I32 = mybir.dt.int32
    W = sb2("W5", [128, 8, 512], BF16)
    suT = sb2("suT", [128, 4, S], BF16)
    ys5 = sb2("ys5", [128, NT, 512])
    rr = sb2("rr", [128, 16])
    thr_ = sb2("thr", [128, 16])
    rots = sb2("rots", [128, 16])
    rotc = sb2("rotc", [128, 16])
    BDT = [sb2("BDTr", [128, 4, 128], BF16), sb2("BDTi", [128, 4, 128], BF16)]
    BDTz = [sb2("BDTzr", [128, 4, 128], BF16), sb2("BDTzi", [128, 4, 128], BF16)]
    CBD = [sb2("CBDr", [128, 16, 2, 16], BF16), sb2("CBDi", [128, 16, 2, 16], BF16)]
    diagd = sb2("diagd", [128, 4, 128], BF16)
    iota = sb2("iota", [128, 512])
    k.dma(iota[:], A['d_iota'][:, :])

    def sincos(dst_sin, dst_cos, src, n, ang, tf, ti):
        for dst, off in ((dst_sin, 0.0), (dst_cos, math.pi / 2)):
            k.ts('dve', ang[:, 0:n], src, off, 1.0 / TWO_PI, ALU.add, ALU.mult)
            k.cp('dve', ti[:, 0:n], ang[:, 0:n])
            k.cp('dve', tf[:, 0:n], ti[:, 0:n])
            k.tt('dve', ang[:, 0:n], ang[:, 0:n], tf[:, 0:n], ALU.subtract)
            k.ts('dve', ang[:, 0:n], ang[:, 0:n], -0.49999, 0.49999, ALU.max, ALU.min)
            k.act(dst, ang[:, 0:n], AF.Sin, scale=TWO_PI)

    with ExitStack() as sts:
        sbs = lambda n, s_, d=F32: k.sb(n, s_, d, sts)
        are = sbs("are", [128, 16])
        aim = sbs("aim", [128, 16])
        dt_ = sbs("dt", [128, 16])
        k.dma(are[:], A['s5_are'][:, :])
        k.dma(aim[:], A['s5_aim'][:, :])
        k.dma(dt_[:], A['s5_logdt'][:, :])
        k.act(dt_[:], dt_[:], AF.Exp)
        th = sbs("th", [128, 16])
        k.tt('dve', rr[:], are[:], dt_[:], ALU.mult)
        k.act(rr[:], rr[:], AF.Exp)
        k.tt('dve', th[:], aim[:], dt_[:], ALU.mult)
        ti = sbs("ti", [128, 16], I32)
        tf = sbs("tf", [128, 16])
        ang = sbs("ang", [128, 16])
        sn = sbs("sn", [128, 16])
        cs = sbs("cs", [128, 16])
        sincos(sn[:], cs[:], th[:], 16, ang, tf, ti)
        k.ts('dve', ang[:, 0:16], th[:], 1.0 / TWO_PI, None, ALU.mult)
        k.cp('dve', ti[:, 0:16], ang[:, 0:16])
        k.cp('dve', tf[:, 0:16], ti[:, 0:16])
        k.stt('dve', thr_[:], tf[:, 0:16], -TWO_PI, th[:], ALU.mult, ALU.add)
        th512 = sbs("th512", [128, 16])
        k.ts('dve', th512[:], thr_[:], 512.0, None, ALU.mult)
        sincos(rots[:], rotc[:], th512[:], 16, ang, tf, ti)
        abr = sbs("abr", [128, 16])
        abi = sbs("abi", [128, 16])
        k.tt('dve', abr[:], rr[:], cs[:], ALU.mult)
        k.tt('dve', abi[:], rr[:], sn[:], ALU.mult)
        lam2 = sbs("lam2", [128, 16])
        t1 = sbs("t1", [128, 16])
        t2 = sbs("t2", [128, 16])
        k.tt('dve', lam2[:], are[:], are[:], ALU.mult)
        k.tt('dve', t1[:], aim[:], aim[:], ALU.mult)
        k.tt('dve', lam2[:], lam2[:], t1[:], ALU.add)
        k.recip(lam2[:], lam2[:])
        am1 = sbs("am1", [128, 16])
        k.ts('dve', am1[:], abr[:], -1.0, None, ALU.add)
        cr = sbs("cr", [128, 16])
        ci = sbs("ci", [128, 16])
        k.tt('dve', t1[:], am1[:], are[:], ALU.mult)
        k.tt('dve', t2[:], abi[:], aim[:], ALU.mult)
        k.tt('dve', cr[:], t1[:], t2[:], ALU.add)
        k.tt('dve', cr[:], cr[:], lam2[:], ALU.mult)
        k.tt('dve', t1[:], abi[:], are[:], ALU.mult)
        k.tt('dve', t2[:], am1[:], aim[:], ALU.mult)
        k.tt('dve', ci[:], t1[:], t2[:], ALU.subtract)
        k.tt('dve', ci[:], ci[:], lam2[:], ALU.mult)
        bre = sbs("bre", [128, 16, 16])
        bim = sbs("bim", [128, 16, 16])
        k.dma(bre[:], A['s5_bre'][:, :, :])
        k.dma(bim[:], A['s5_bim'][:, :, :])
        bbr = sbs("bbr", [128, 16, 16])
        bbi = sbs("bbi", [128, 16, 16])
        tb = sbs("tb", [128, 16, 16])
        crb = cr[:].unsqueeze(2).to_broadcast([128, 16, 16])
        cib = ci[:].unsqueeze(2).to_broadcast([128, 16, 16])
        k.tt('dve', bbr[:], bre[:], crb, ALU.mult)
        k.tt('dve', tb[:], bim[:], cib, ALU.mult)
        k.tt('dve', bbr[:], bbr[:], tb[:], ALU.subtract)
        k.tt('dve', bbi[:], bim[:], crb, ALU.mult)
        k.tt('dve', tb[:], bre[:], cib, ALU.mult)
        k.tt('dve', bbi[:], bbi[:], tb[:], ALU.add)
        halfmask = sbs("halfmask", [128, 2])
        k.dma(halfmask[:], A['d_halfmask'][:, :])
        BD = sbs("BD", [128, 16, 2, 16])
        k.memset('pool', BDTz[0][:], 0.0)
        k.memset('pool', BDTz[1][:], 0.0)
        for ri, src in enumerate((bbr, bbi)):
            for g2 in range(2):
                k.ts('dve', BD[:, :, g2, :], src[:], halfmask[:, g2:g2 + 1], None, ALU.mult)
            for fc in range(4):
                pt = C.pm
                k.tr(pt[:, 0:128], BD[:, 4 * fc:4 * fc + 4, :, :].rearrange("p a b c -> p (a b c)"), identf[:])
                k.cp('act', BDT[ri][:, fc, :], pt[:, 0:128])
                k.cp('act', BDTz[ri][96:128, fc, :], pt[96:128, 0:128])
        cre = sbs("cre", [128, 16, 16])
        cim = sbs("cim", [128, 16, 16])
        k.dma(cre[:], A['s5_cre'][:, :, :])
        k.dma(cim[:], A['s5_cim'][:, :, :])
        for g2 in range(2):
            k.ts('dve', CBD[0][:, :, g2, :], cre[:], halfmask[:, g2:g2 + 1], None, ALU.mult)
            k.ts('dve', CBD[1][:, :, g2, :], cim[:], halfmask[:, g2:g2 + 1], -1.0, ALU.mult, ALU.mult)
        dsk = sbs("dsk", [128, 4])
        k.dma(dsk[:], A['s5_dT'][:, :])
        for fc in range(4):
            k.ts('dve', diagd[:, fc, :], identf[:], dsk[:, fc:fc + 1], None, ALU.mult)
    k.barrier()
    load_w(k, C, W, w_in, 0, 8, CSU, 512)
    for fc in range(4):
        proj_fm(k, C, W, fc * 128, uT, lambda g, p, fc=fc: k.cp('act', suT[:, fc, g * 512:(g + 1) * 512], p[:]))
    with ExitStack() as stl:
        sbl = lambda n, s_, d=F32: k.sb(n, s_, d, stl)

        class Sl:
            pass
        slots = []
        for i in range(2):
            s_ = Sl()
            s_.cosT = sbl(f"cosT{i}", [128, 512])
            s_.sinT = sbl(f"sinT{i}", [128, 512])
            s_.zr = sbl(f"zr{i}", [128, 512])
            s_.zi = sbl(f"zi{i}", [128, 512])
            s_.wr = sbl(f"wr{i}", [128, 512])
            s_.wi = sbl(f"wi{i}", [128, 512])
            s_.ta = sbl(f"ta{i}", [128, 512])
            s_.tb = sbl(f"tb{i}", [128, 512])
            s_.ti = sbl(f"ti{i}", [128, 512], I32)
            s_.xr = sbl(f"xr{i}", [128, 512], BF16)
            s_.xi = sbl(f"xi{i}", [128, 512], BF16)
            s_.ini = sbl(f"ini{i}", [128, 4])
            s_.pbr = C.pa[i]
            s_.pbi = C.psc[i]
            s_.pyy = C.po[i]
            slots.append(s_)
        print("S5 sbuf remaining", k.nc.sbuf_bytes_remaining, flush=True)

        def pair_gen(j, s_):
            fc = j // 4
            pb0 = 32 * (j % 4)
            cosT, sinT, zr, zi, wr, wi, ta, tb, xr, xi_, ini = (s_.cosT, s_.sinT, s_.zr, s_.zi, s_.wr, s_.wi, s_.ta,
                                                             s_.tb, s_.xr, s_.xi, s_.ini)
            k.ts('dve', zr[:], iota[:], thr_[:, j:j + 1], None, ALU.mult)
            sincos(sinT[:], cosT[:], zr[:], 512, ta, tb, s_.ti)
            yield
            pyy3 = s_.pyy[:].rearrange("p (t c) -> p t c", t=NT)
            pbr, pbi = s_.pbr, s_.pbi
            for n in range(4):
                if pb0 < 96:
                    k.mm(pbr[:], BDT[0][pb0:pb0 + 32, fc, :], suT[pb0:pb0 + 32, fc, n * 512:(n + 1) * 512])
                    k.mm(pbi[:], BDT[1][pb0:pb0 + 32, fc, :], suT[pb0:pb0 + 32, fc, n * 512:(n + 1) * 512])
                else:
                    k.mm(pbr[:], BDTz[0][64:128, fc, :], suT[64:128, fc, n * 512:(n + 1) * 512])
                    k.mm(pbi[:], BDTz[1][64:128, fc, :], suT[64:128, fc, n * 512:(n + 1) * 512])
                yield
                k.tt('dve', zr[:], pbr[:], cosT[:], ALU.mult)
                k.tt('dve', ta[:], pbi[:], sinT[:], ALU.mult)
                yield
                k.tt('dve', zr[:], zr[:], ta[:], ALU.add)
                k.tt('dve', zi[:], pbi[:], cosT[:], ALU.mult)
                yield
                k.tt('dve', tb[:], pbr[:], sinT[:], ALU.mult)
                yield
                k.tt('pool', zi[:], zi[:], tb[:], ALU.subtract)
                if n == 0:
                    ir, ii = 0.0, 0.0
                else:
                    k.tt('dve', ini[:, 0:1], wr[:, 511:512], rotc[:, j:j + 1], ALU.mult)
                    k.tt('dve', ini[:, 1:2], wi[:, 511:512], rots[:, j:j + 1], ALU.mult)
                    k.tt('dve', ini[:, 2:3], wr[:, 511:512], rots[:, j:j + 1], ALU.mult)
                    k.tt('dve', ini[:, 3:4], wi[:, 511:512], rotc[:, j:j + 1], ALU.mult)
                    yield
                    k.tt('dve', ini[:, 0:1], ini[:, 0:1], ini[:, 1:2], ALU.subtract)
                    k.tt('dve', ini[:, 2:3], ini[:, 2:3], ini[:, 3:4], ALU.add)
                    ir, ii = ini[:, 0:1], ini[:, 2:3]
                yield
                rb = rr[:, j:j + 1].to_broadcast([128, 512])
                k.scan(wr[:], rb, zr[:], ir, ALU.mult, ALU.add)
                yield
                k.scan(wi[:], rb, zi[:], ii, ALU.mult, ALU.add)
                k.tt('pool', ta[:], wr[:], cosT[:], ALU.mult)
                yield
                k.tt('pool', tb[:], wi[:], sinT[:], ALU.mult)
                yield
                k.tt('pool', xr[:], ta[:], tb[:], ALU.subtract)
                yield
                k.tt('pool', ta[:], wr[:], sinT[:], ALU.mult)
                yield
                k.tt('pool', tb[:], wi[:], cosT[:], ALU.mult)
                yield
                k.tt('pool', xi_[:], ta[:], tb[:], ALU.add)
                yield
                for sub in range(4):
                    T = 4 * n + sub
                    k.mm(pyy3[:, T, :], xr[:, sub * 128:(sub + 1) * 128], CBD[0][:, j, :, :].rearrange("p a b -> p (a b)"),
                         start=True, stop=False)
                    k.mm(pyy3[:, T, :], xi_[:, sub * 128:(sub + 1) * 128], CBD[1][:, j, :, :].rearrange("p a b -> p (a b)"),
                         start=False, stop=True)
                yield
            k.cp('act', ys5[:, :, 32 * j:32 * j + 32], pyy3)
            yield
        for j0 in range(0, 16, 2):
            run_interleaved([pair_gen(j0 + i, slots[i]) for i in range(2)], stagger=9)
    k.barrier()
    Wz = W
    load_w(k, C, Wz, w_in, 0, 8, CSZ, 512)
    W = sb2("W5a", [128, 4, 512], BF16)
    load_w(k, C, W, A['w_glu'], 0, 4, 0, 512)
    W2 = sb2("W5b", [128, 4, 512], BF16)
    load_w(k, C, W2, A['w_glu'], 0, 4, 512, 512)
    class FS:
        pass
    fsl = []
    for i in range(2):
        f_ = FS()
        f_.yg = sb2(f"yg{i}", [128, 512], BF16)
        f_.ygT = sb2(f"ygT{i}", [128, 4, 128], BF16)
        f_.mixb = sb2(f"mixb5{i}", [128, 512], BF16)
        f_.za = sb2(f"za5{i}", [128, 512])
        f_.zb = sb2(f"zb5{i}", [128, 512])
        f_.pd, f_.pz1, f_.pz2, f_.pT = C.pa[i], C.psc[i], C.po[i], C.pT[i]
        fsl.append(f_)

    def fin_gen(T, f_):
        pd = f_.pd
        for fc in range(4):
            k.mm(pd[:, fc * 128:(fc + 1) * 128], suT[:, fc, T * 128:(T + 1) * 128], diagd[:, fc, :])
        yield
        k.tt('dve', ys5[:, T, :], ys5[:, T, :], pd[:], ALU.add)
        yield
        k.act(f_.yg[:], ys5[:, T, :], AF.Gelu_apprx_tanh)
        yield
        pt = f_.pT
        for fc in range(4):
            k.tr(pt[:, fc * 128:(fc + 1) * 128], f_.yg[:, fc * 128:(fc + 1) * 128], C.identb[:])
        yield
        k.cp('dve', f_.ygT[:], pt[:, 0:512].rearrange("p (c t) -> p c t", c=4))
        yield
        for fc in range(4):
            k.mm(f_.pz1[:], f_.ygT[:, fc, :], W[:, fc, 0:512], start=(fc == 0), stop=(fc == 3))
        for fc in range(4):
            k.mm(f_.pz2[:], f_.ygT[:, fc, :], W2[:, fc, 0:512], start=(fc == 0), stop=(fc == 3))
        proj_tm(k, C, Wz, 0, 512, uT, T, pd)
        yield
        k.act(f_.za[:], f_.pz2[:], AF.Tanh, scale=0.5)
        yield
        k.stt('dve', f_.za[:], f_.za[:], 1.0, f_.pz1[:], ALU.add, ALU.mult)
        k.act(f_.zb[:], pd[:], AF.Tanh, scale=0.5)
        yield
        k.stt('dve', f_.zb[:], f_.zb[:], 1.0, pd[:], ALU.add, ALU.mult)
        yield
        k.stt('dve', f_.mixb[:], f_.za[:], 0.25, f_.zb[:], ALU.mult, ALU.mult)
        yield
        for fc in range(4):
            k.tr(pt[:, 512 + fc * 128:512 + (fc + 1) * 128], f_.mixb[:, fc * 128:(fc + 1) * 128], C.identb[:])
        yield
        k.cp('act', mixT[:, 8:12, T * 128:(T + 1) * 128], pt[:, 512:1024].rearrange("p (c t) -> p c t", c=4))
        yield
    for T0 in range(0, NT, 2):
        run_interleaved([fin_gen(T0 + i, fsl[i]) for i in range(2)])


def l1_inputs(inp, b, h_in=None):
    d = {}
    if h_in is not None:
        d['h_in'] = np.ascontiguousarray(h_in)
    d['p1'] = np.ascontiguousarray(inp['p'][1, b])
    d['gpre1T'] = np.ascontiguousarray(inp['norm_pre'][1].reshape(8, 128).T)
    d['gpost1'] = np.ascontiguousarray(inp['norm_post'][1].reshape(1024))
    d['w_in1'] = np.ascontiguousarray(inp['cd_w_in'][0])
    d['w1k'] = np.ascontiguousarray(inp['cmp_w1_k'][0].reshape(32, 64, 128).transpose(1, 0, 2))
    d['w1v'] = np.ascontiguousarray(inp['cmp_w1_v'][0].reshape(32, 64, 128).transpose(1, 0, 2))
    d['w2k'] = np.ascontiguousarray(inp['cmp_w2_k'][0])
    d['w2v'] = np.ascontiguousarray(inp['cmp_w2_v'][0])
    d['poskT'] = np.ascontiguousarray(inp['cmp_pos_k'][0].T)
    d['posvT'] = np.ascontiguousarray(inp['cmp_pos_v'][0].T)

    def gp(a):
        return np.ascontiguousarray(a.reshape(16, 2, 64).transpose(1, 2, 0).reshape(128, 16))
    d['s5_are'] = gp(inp['s5_a_re'][0])
    d['s5_aim'] = gp(inp['s5_a_im'][0])
    d['s5_logdt'] = gp(np.repeat(inp['s5_log_dt'][0][:, None], 64, axis=1))
    gb = lambda a: np.ascontiguousarray(a.reshape(16, 2, 64, 16).transpose(1, 2, 0, 3).reshape(128, 16, 16))
    d['s5_bre'] = gb(inp['s5_b_re'][0])
    d['s5_bim'] = gb(inp['s5_b_im'][0])
    gc = lambda a: np.ascontiguousarray(a.reshape(16, 2, 16, 64).transpose(1, 3, 0, 2).reshape(128, 16, 16))
    d['s5_cre'] = gc(inp['s5_c_re'][0])
    d['s5_cim'] = gc(inp['s5_c_im'][0])
    d['s5_dT'] = np.ascontiguousarray(inp['s5_d'][0].reshape(4, 128).T)
    d['w_glu'] = np.ascontiguousarray(inp['s5_w_glu'][0])
    d['w_out1'] = np.ascontiguousarray(inp['cd_w_out'][0])
    d['pe_gate1'] = np.ascontiguousarray(inp['pe_gate'][1])
    d['pe_proj1'] = np.ascontiguousarray(inp['pe_proj'][1])
    for kk, v in host_consts_l1().items():
        d['d_' + kk] = v
    return d


def build_l1(d):
    nc = bass.Bass("TRN2", target_bir_lowering=False)
    A = declare(nc, d)
    out = nc.dram_tensor("out", [S, D], F32, kind="ExternalOutput").ap()
    with ExitStack() as stack:
        k = KB(nc, stack)
        toks = layer1(k, nc, A, A['h_in'], out)
        k.finish(toks)
        print("L1 inst", k.ninst, "waits", k.nwait, flush=True)
        global LASTLOG
        LASTLOG = k.log
    return nc


FUSED = True


def build_fused(d):
    nc = bass.Bass("TRN2", target_bir_lowering=False)
    A = declare(nc, d)
    out = nc.dram_tensor("out", [S, D], F32, kind="ExternalOutput").ap()
    h1 = nc.dram_tensor("h1_scratch", [S, D], F32, kind="Internal").ap()
    with ExitStack() as stack:
        k = KB(nc, stack)
        mixd = nc.dram_tensor("mix0T", [16, 128, S], BF16, kind="Internal").ap()
        layer0(k, nc, A, A['x'], h1, mixd)
        k.barrier()
        toks = layer1(k, nc, A, h1, out)
        k.finish(toks)
    return nc


def kernel(**inp):
    inp = {kk: np.asarray(v) for kk, v in inp.items()}
    cores = list(range(8))
    if FUSED:
        ds = []
        for b in cores:
            d = l0_inputs(inp, b)
            d.update(l1_inputs(inp, b))
            ds.append(d)
        nc = build_fused(ds[0])
        res = run_bass_kernel_spmd(nc, ds, core_ids=cores)
        return np.stack([res.results[b]["out"] for b in cores], 0).astype(np.float32)
    d0 = [l0_inputs(inp, b) for b in cores]
    nc0 = build_l0(d0[0])
    res0 = run_bass_kernel_spmd(nc0, d0, core_ids=cores)
    d1 = [l1_inputs(inp, b, res0.results[b]["out"]) for b in cores]
    nc1 = build_l1(d1[0])
    res1 = run_bass_kernel_spmd(nc1, d1, core_ids=cores)
    return np.stack([res1.results[b]["out"] for b in cores], 0).astype(np.float32)
```

```python
import math
from contextlib import ExitStack
import numpy as np
import ml_dtypes
import concourse.bass as bass
import concourse.mybir as mybir
from concourse.bass_utils import run_bass_kernel_spmd

F32 = mybir.dt.float32
BF16 = mybir.dt.bfloat16
ALU = mybir.AluOpType
AF = mybir.ActivationFunctionType
AX = mybir.AxisListType

SEM_ROLL = 12000
NDS = 24
S = 2048
D = 1024
NT = 16
EPS = 1e-6
NEG = -1e30


PSUM_NAMES = set()
STRICT_WAR = True


def _region(ap):
    t = ap.tensor
    shp = list(t.shape)
    dims = ap.ap
    off = int(ap.offset)
    space = str(ap.space).upper()
    if not ('SB' in space or 'PSUM' in space):
        lo = off
        hi = off + 1
        for st, c in dims:
            hi += abs(st) * (c - 1)
        return (t.name, 0, 1, lo, hi)
    if 'PSUM' in space:
        PSUM_NAMES.add(t.name)
    rowlen = 1
    for s in shp[1:]:
        rowlen *= s
    p0 = off // rowlen
    lo = off % rowlen
    pst, pc = dims[0]
    p1 = p0 + (1 if pst == 0 else pc)
    hi = lo + 1
    for st, c in dims[1:]:
        hi += abs(st) * (c - 1)
    return (t.name, p0, p1, lo, hi)


class KB:
    def __init__(self, nc, stack):
        self.nc = nc
        self.stack = stack
        self.E = dict(pe=nc.tensor, dve=nc.vector, act=nc.scalar, pool=nc.gpsimd, sp=nc.sync)
        self.sem = {}
        self.cnt = {}
        self.nsem = 0
        for e in self.E:
            self._newsem(e)
        self.known = {e: {} for e in self.E}
        self.acc = {}
        self.dsem = [stack.enter_context(nc.semaphore(f"dq{i}")) for i in range(NDS)]
        self.dcnt = [0] * NDS
        self.dnext = 0
        self.ninst = {e: 0 for e in self.E}
        self.nwait = {e: 0 for e in self.E}
        self.uid = 0
        self.log = []

    def _newsem(self, e):
        self.nsem += 1
        self.sem[e] = self.stack.enter_context(self.nc.semaphore(f"pg_{e}_{self.nsem}"))
        self.cnt[e] = 0

    def sb(self, name, shape, dt=F32, st=None):
        self.uid += 1
        return (st or self.stack).enter_context(self.nc.sbuf_tensor(f"{name}_{self.uid}", list(shape), dt))

    def ps(self, name, shape, dt=F32, st=None):
        self.uid += 1
        return (st or self.stack).enter_context(self.nc.psum_tensor(f"{name}_{self.uid}", list(shape), dt))

    def _wait(self, eng, sem, val):
        kn = self.known[eng]
        key = sem.name
        if kn.get(key, 0) >= val:
            return
        self.E[eng].wait_ge(sem, val)
        self.log.append((eng, 'WAIT', key, val))
        kn[key] = val
        self.nwait[eng] += 1

    def barrier(self):
        toks = [(self.sem[e], self.cnt[e]) for e in self.E if self.cnt[e] > 0]
        toks += [(self.dsem[i], self.dcnt[i]) for i in range(NDS) if self.dcnt[i] > 0]
        for e in self.E:
            for s, v in toks:
                if s is self.sem[e]:
                    continue
                self._wait(e, s, v)
        self.acc = {}

    def issue(self, eng, fn, reads, writes, dma=False):
        deps = {}
        accs = []
        for ap in reads:
            if ap is None or isinstance(ap, (int, float)):
                continue
            accs.append((_region(ap), False))
        for ap in writes:
            accs.append((_region(ap), True))
        src_eng = 'dma' if dma else eng
        for (name, p0, p1, lo, hi), w in accs:
            d = self.acc.get(name)
            if not d:
                continue
            isps = name in PSUM_NAMES
            for key, (s, v) in d.items():
                e2, w2, q0, q1, l2, h2 = key
                if isps and e2 != src_eng:
                    sk = s.name
                    if sk not in deps or deps[sk][1] < v:
                        deps[sk] = (s, v)
                    continue
                if not (w or w2):
                    continue
                if q1 <= p0 or p1 <= q0 or h2 <= lo or hi <= l2:
                    continue
                if e2 == src_eng and e2 != 'dma':
                    if e2 == 'pe':
                        continue
                    if not w2 and not STRICT_WAR:
                        continue
                sk = s.name
                if sk not in deps or deps[sk][1] < v:
                    deps[sk] = (s, v)
        for sk, (s, v) in deps.items():
            self._wait(eng, s, v)
        if dma:
            slot = self.dnext
            self.dnext = (self.dnext + 1) % NDS
            if self.dcnt[slot] > 0:
                self._wait(eng, self.dsem[slot], self.dcnt[slot])
            inst = fn()
            self.dcnt[slot] += 16
            inst.then_inc(self.dsem[slot], 16)
            tok = (self.dsem[slot], self.dcnt[slot])
        else:
            if self.cnt[eng] >= SEM_ROLL:
                self._newsem(eng)
            inst = fn()
            self.cnt[eng] += 1
            inst.then_inc(self.sem[eng], 1)
            tok = (self.sem[eng], self.cnt[eng])
        self.ninst[eng] += 1
        self.log.append((eng, 'INST', tok[0].name, tok[1], [(a[0][0], a[1]) for a in accs]))
        for (name, p0, p1, lo, hi), w in accs:
            d = self.acc.setdefault(name, {})
            if w:
                dead = [k for k in d if k[2] >= p0 and k[3] <= p1 and k[4] >= lo and k[5] <= hi]
                for k in dead:
                    del d[k]
            d[(src_eng, w, p0, p1, lo, hi)] = tok
        return tok

    def dma(self, out, in_, eng='sp', **kw):
        return self.issue(eng, lambda: self.E[eng].dma_start(out=out, in_=in_, **kw), [in_], [out], dma=True)

    def mm(self, out, lhsT, rhs, start=True, stop=True, **kw):
        return self.issue('pe', lambda: self.nc.tensor.matmul(out, lhsT, rhs, start=start, stop=stop, **kw),
                          [lhsT, rhs], [out])

    def tr(self, out, in_, ident):
        return self.issue('pe', lambda: self.nc.tensor.transpose(out, in_, ident), [in_, ident], [out])

    def act(self, out, in_, func, bias=0.0, scale=1.0, accum_out=None):
        rd = [in_]
        kw = {}
        if not isinstance(bias, (int, float)):
            rd.append(bias)
        if not isinstance(scale, (int, float)):
            rd.append(scale)
        wr = [out]
        if accum_out is not None:
            wr.append(accum_out)
            kw['accum_out'] = accum_out
        return self.issue('act', lambda: self.nc.scalar.activation(out, in_, func, bias=bias, scale=scale, **kw),
                          rd, wr)

    def tt(self, eng, out, in0, in1, op):
        return self.issue(eng, lambda: self.E[eng].tensor_tensor(out, in0, in1, op), [in0, in1], [out])

    def ts(self, eng, out, in0, s1, s2, op0, op1=None, accum_out=None):
        rd = [in0]
        if not isinstance(s1, (int, float)):
            rd.append(s1)
        if s2 is not None and not isinstance(s2, (int, float)):
            rd.append(s2)
        wr = [out]
        kw = {}
        if accum_out is not None:
            wr.append(accum_out)
            kw['accum_out'] = accum_out
        if op1 is None:
            return self.issue(eng, lambda: self.E[eng].tensor_scalar(out, in0, s1, None, op0, **kw), rd, wr)
        return self.issue(eng, lambda: self.E[eng].tensor_scalar(out, in0, s1, s2, op0, op1, **kw), rd, wr)

    def stt(self, eng, out, in0, scalar, in1, op0, op1):
        rd = [in0, in1]
        if not isinstance(scalar, (int, float)):
            rd.append(scalar)
        return self.issue(eng, lambda: self.E[eng].scalar_tensor_tensor(out, in0, scalar, in1, op0, op1), rd, [out])

    def red(self, eng, out, in_, op, axis=AX.X, **kw):
        return self.issue(eng, lambda: self.E[eng].tensor_reduce(out, in_, axis, op, **kw), [in_], [out])

    def cp(self, eng, out, in_):
        if eng == 'act':
            return self.issue('act', lambda: self.nc.scalar.copy(out, in_), [in_], [out])
        return self.issue(eng, lambda: self.E[eng].tensor_copy(out, in_), [in_], [out])

    def scan(self, out, d0, d1, init, op0, op1):
        rd = [d0, d1]
        if not isinstance(init, (int, float)):
            rd.append(init)
        return self.issue('dve', lambda: self.nc.vector.tensor_tensor_scan(out, d0, d1, init, op0, op1), rd, [out])

    def memset(self, eng, ap, val):
        return self.issue(eng, lambda: self.E[eng].memset(ap, val), [], [ap])

    def max8(self, out, in_):
        return self.issue('dve', lambda: self.nc.vector.max(out, in_), [in_], [out])

    def recip(self, out, in_):
        return self.issue('dve', lambda: self.nc.vector.reciprocal(out, in_), [in_], [out])

    def bnstats(self, out, in_):
        return self.issue('dve', lambda: self.nc.vector.bn_stats(out, in_), [in_], [out])

    def bnaggr(self, out, in_):
        return self.issue('dve', lambda: self.nc.vector.bn_aggr(out, in_), [in_], [out])

    def finish(self, toks):
        for s, v in toks:
            self._wait('sp', s, v)


RET_G = [1.0 - 2.0 ** (-5.0 - h) for h in range(4)]


def host_consts_l0():
    c = {}
    c['ident'] = np.eye(128, dtype=np.float32)
    i = np.arange(128)
    c['triu'] = (i[:, None] <= i[None, :]).astype(np.float32)
    c['negmask'] = np.where(i[None, :] <= i[:, None], 0.0, NEG).astype(np.float32)
    dec = np.zeros((128, 4, 128), np.float32)
    xi = np.zeros((128, 4, 128), np.float32)
    zeta = np.zeros((128, 4), np.float32)
    for h in range(4):
        lg = np.log1p(-(2.0 ** (-5.0 - h)))
        diff = (i[None, :] - i[:, None]).astype(np.float64)
        dec[:, h, :] = np.where(diff >= 0, np.exp(np.maximum(diff, 0) * lg), 0.0) * 128 ** -0.5
        xi[:, h, :] = (np.exp((i + 1.0) * lg) * 128 ** -0.5)[None, :]
        zeta[:, h] = np.exp((127 - i) * lg)
    c['decT'] = dec
    c['xi'] = xi
    c['zeta'] = zeta
    return c


class Ctx:
    pass


RR_ENG = ('pool', 'act', 'dve')


def load_w(k, C, dst, src, r0, nkc, c0, ncols, eng='pool'):
    k.dma(dst[:, 0:nkc, 0:ncols], src[r0:r0 + nkc * 128, c0:c0 + ncols].rearrange("(kc p) n -> p kc n", p=128), eng='pool')


def run_interleaved(gens, stagger=0):
    gens = list(gens)
    if len(gens) > 1:
        for _ in range(stagger):
            try:
                next(gens[0])
            except StopIteration:
                gens.pop(0)
                break
    while gens:
        for g in list(gens):
            try:
                next(g)
            except StopIteration:
                gens.remove(g)


def norm_to_uT(k, C, src_dram, gT, uT):
    def tile_gen(T, i):
        xt = C.xt[i]
        k.dma(xt[:], src_dram[T * 128:(T + 1) * 128, :])
        sq = C.junks[i]
        ss = C.sm[:, 4 * i:4 * i + 1]
        k.act(sq[:], xt[:], AF.Square, accum_out=ss)
        yield
        rs = C.sm[:, 4 * i + 1:4 * i + 2]
        k.ts('dve', rs, ss, 1.0 / D, EPS, ALU.mult, ALU.add)
        yield
        k.tt('pool', rs, rs, C.neghalf[:], ALU.pow)
        yield
        xn = C.xns[i]
        k.ts('dve', xn[:], xt[:], rs, None, ALU.mult)
        yield
        pt = C.pT[i]
        for c in range(8):
            k.tr(pt[:, c * 128:(c + 1) * 128], xn[:, c * 128:(c + 1) * 128], C.identb[:])
        yield
        k.tt('dve', uT[:, :, T * 128:(T + 1) * 128], pt[:].rearrange("p (c t) -> p c t", c=8),
             gT[:].unsqueeze(2).to_broadcast([128, 8, 128]), ALU.mult)
        yield
    for T0 in range(0, NT, 2):
        run_interleaved([tile_gen(T0 + i, i) for i in range(2)])


def proj_fm(k, C, W, wc0, uT, evac):
    for g in range(4):
        ps = C.pa[C.pai % 2]
        C.pai += 1
        for kc in range(8):
            k.mm(ps[:], W[:, kc, wc0:wc0 + 128], uT[:, kc, g * 512:(g + 1) * 512], start=(kc == 0), stop=(kc == 7))
        evac(g, ps)


def proj_tm(k, C, W, wc0, ncols, uT, T, ps):
    for kc in range(8):
        k.mm(ps[:, 0:ncols], uT[:, kc, T * 128:(T + 1) * 128], W[:, kc, wc0:wc0 + ncols], start=(kc == 0), stop=(kc == 7))


def headnorm_gate(k, C, ysb, gg, out_bf):
    st6 = C.sm[:, 8:14]
    mv = C.sm[:, 14:16]
    k.bnstats(st6, ysb)
    k.bnaggr(mv, st6)
    rstd = C.sm[:, 16:17]
    k.ts('dve', rstd, mv[:, 1:2], EPS, None, ALU.add)
    k.act(rstd, rstd, AF.Sqrt)
    k.recip(rstd, rstd)
    k.ts('dve', ysb, ysb, mv[:, 0:1], rstd, ALU.subtract, ALU.mult)
    k.tt('dve', out_bf, ysb, gg, ALU.mult)


def mix_to_T(k, C, mix_bf, mixT, fc0, T):
    pt = C.pT[C.pti % 2]
    C.pti += 1
    for j in range(2):
        k.tr(pt[:, j * 128:(j + 1) * 128], mix_bf[:, j * 128:(j + 1) * 128], C.identb[:])
    k.cp('act', mixT[:, fc0:fc0 + 2, T * 128:(T + 1) * 128], pt[:, 0:256].rearrange("p (c t) -> p c t", c=2))


def outproj_accum(k, C, mixT, nfc, w_out, r0, yacc, first):
    Wo = C.Wo
    load_w(k, C, Wo, w_out, r0, nfc, 0, 512)
    load_w(k, C, C.Wo2, w_out, r0, nfc, 512, 512)
    for T in range(NT):
        for half, Wt in ((0, Wo), (1, C.Wo2)):
            ps = C.pa[C.pai % 2]
            C.pai += 1
            for fc in range(nfc):
                k.mm(ps[:], mixT[:, fc, T * 128:(T + 1) * 128], Wt[:, fc, 0:512], start=(fc == 0), stop=(fc == nfc - 1))
            dst = yacc[:, T, half * 512:(half + 1) * 512]
            if first:
                k.cp('act', dst, ps[:])
            else:
                k.tt('dve', dst, dst, ps[:], ALU.add)


def post_block(k, C, sb3, PSL, res_dram, p_dram, gpost_b, Wg, Wp, out_dram, gen_y):
    toks = []

    class Sl:
        pass
    slots = []
    for i in range(2):
        s_ = Sl()
        s_.i = i
        s_.ps = PSL[i]
        s_.xt = sb3(f"pxt{i}", [128, 1024])
        s_.ptile = sb3(f"ppt{i}", [128, 256])
        s_.junk = sb3(f"pjunk{i}", [128, 1024])
        s_.hb = sb3(f"phb{i}", [128, 1024], BF16)
        s_.hT = sb3(f"phT{i}", [128, 8, 128], BF16)
        s_.pb = sb3(f"ppb{i}", [128, 256], BF16)
        s_.ppT = sb3(f"pppT{i}", [128, 2, 128], BF16)
        s_.ytile = sb3(f"pyt{i}", [128, 1024])
        s_.sm = sb3(f"psm{i}", [128, 8])
        slots.append(s_)

    def tile_gen(T, s_):
        xt = s_.xt
        k.dma(xt[:], res_dram[T * 128:(T + 1) * 128, :])
        k.dma(s_.ptile[:], p_dram[T * 128:(T + 1) * 128, :])
        yield from gen_y(T, s_)
        y = s_.ytile[:]
        ss = s_.sm[:, 0:1]
        k.act(s_.junk[:], y, AF.Square, accum_out=ss)
        yield
        rs = s_.sm[:, 1:2]
        k.ts('dve', rs, ss, 1.0 / D, EPS, ALU.mult, ALU.add)
        k.tt('pool', rs, rs, C.neghalf[:], ALU.pow)
        yield
        k.stt('dve', y, y, rs, gpost_b[:], ALU.mult, ALU.mult)
        yield
        k.tt('pool', xt[:], xt[:], y, ALU.add)
        k.cp('act', s_.hb[:], xt[:])
        k.cp('act', s_.pb[:], s_.ptile[:])
        yield
        pt = s_.ps['pT']
        for c in range(8):
            k.tr(pt[:, c * 128:(c + 1) * 128], s_.hb[:, c * 128:(c + 1) * 128], C.identb[:])
        yield
        k.cp('dve', s_.hT[:], pt[:].rearrange("p (c t) -> p c t", c=8))
        yield
        for c in range(2):
            k.tr(pt[:, c * 128:(c + 1) * 128], s_.pb[:, c * 128:(c + 1) * 128], C.identb[:])
        yield
        k.cp('dve', s_.ppT[:], pt[:, 0:256].rearrange("p (c t) -> p c t", c=2))
        yield
        for half in range(2):
            psg = s_.ps['pg']
            psp = s_.ps['pp']
            for kc in range(8):
                k.mm(psg[:], s_.hT[:, kc, :], Wg[:, kc, half * 512:(half + 1) * 512], start=(kc == 0), stop=(kc == 7))
            for kc in range(2):
                k.mm(psp[:], s_.ppT[:, kc, :], Wp[:, kc, half * 512:(half + 1) * 512], start=(kc == 0), stop=(kc == 1))
            yield
            sg = s_.junk[:, 0:512]
            k.act(sg, psg[:], AF.Tanh, scale=0.5)
            yield
            k.stt('dve', sg, sg, 1.0, psp[:], ALU.add, ALU.mult)
            yield
            k.stt('dve', xt[:, half * 512:(half + 1) * 512], sg, 0.5, xt[:, half * 512:(half + 1) * 512], ALU.mult, ALU.add)
            yield
        toks.append(k.dma(out_dram[T * 128:(T + 1) * 128, :], xt[:]))
    for T0 in range(0, NT, 2):
        run_interleaved([tile_gen(T0 + i, slots[i]) for i in range(2)], stagger=0)
    return toks


L0_STOP = 99
L0_NG = 2
STAG_RET = 0
STAG_ML = 0


def layer0(k, nc, A, x_dram, out_dram, mixd):
    with ExitStack() as st:
        C = Ctx()
        C.wsi = 0
        C.pai = 0
        C.pti = 0
        sb = lambda n, s, d=F32: k.sb(n, s, d, st)
        ps = lambda n, s, d=F32: k.ps(n, s, d, st)
        PS = []
        for i in range(2):
            PS.append(dict(pa=ps(f"pa{i}", [128, 512]), pb=ps(f"pb{i}", [128, 512]), pc=ps(f"pc{i}", [128, 512]),
                           pT=ps(f"pT{i}", [128, 1024], BF16)))
        C.pa = [PS[0]['pa'], PS[1]['pa']]
        C.pT = [PS[0]['pT'], PS[1]['pT']]
        identf = sb("identf", [128, 128])
        C.identf = identf
        C.identb = sb("identb", [128, 128], BF16)
        triu = sb("triu", [128, 128])
        negmask = sb("negmask", [128, 128])
        decT = sb("decT", [128, 4, 128])
        xi = sb("xi", [128, 4, 128])
        zeta = sb("zeta", [128, 4])
        onesf = sb("onesf", [128, 128])
        k.dma(identf[:], A['c_ident'][:, :])
        k.dma(triu[:], A['c_triu'][:, :])
        k.dma(negmask[:], A['c_negmask'][:, :])
        k.dma(decT[:], A['c_decT'][:, :, :])
        k.dma(xi[:], A['c_xi'][:, :, :])
        k.dma(zeta[:], A['c_zeta'][:, :])
        k.cp('dve', C.identb[:], identf[:])
        k.memset('pool', onesf[:], 1.0)
        C.neghalf = sb("neghalf", [128, 1])
        k.memset('pool', C.neghalf[:], -0.5)
        gpreT = sb("gpreT", [128, 8])
        k.dma(gpreT[:], A['gpre0T'][:, :])
        convw = sb("convw", [128, 8, 4])
        convb = sb("convb", [128, 8])
        gateb = sb("gateb", [128, 8])
        k.dma(convw[:], A['convw'][:, :, :])
        k.dma(convb[:], A['convb'][:, :])
        k.dma(gateb[:], A['gateb'][:].partition_broadcast(128))
        C.xt = [sb("xt0", [128, 1024]), sb("xt1", [128, 1024])]
        C.junk = sb("junk", [128, 1024])
        C.xn = sb("xn", [128, 1024], BF16)
        C.junks = [C.junk, sb("junk2", [128, 1024])]
        C.xns = [C.xn, sb("xn2", [128, 1024], BF16)]
        C.sm = sb("sm", [128, 64])
        w_in = A['w_in0']
        with ExitStack() as st2:
            sb2 = lambda n, s, d=F32: k.sb(n, s, d, st2)
            uT = sb2("uT", [128, 8, S], BF16)
            ifg = sb2("ifg", [128, NT, 8])
            logf = sb2("logf", [128, NT, 4])
            bcum = sb2("bcum", [128, NT, 4])
            Gtok = sb2("Gtok", [128, NT, 4])
            blast = sb2("blast", [128, NT, 4])
            mst = sb2("mst", [128, 4])
            maxG_all = sb2("maxG_all", [128, NT, 4])
            nmaxG_all = sb2("nmaxG_all", [128, NT, 4])
            mu_all = sb2("mu_all", [128, NT, 4])
            bm_all = sb2("bm_all", [128, NT, 4])
            m_all = sb2("m_all", [128, NT + 1, 4])
            sp_all = sb2("sp_all", [128, NT, 4])
            sc_all = sb2("sc_all", [128, NT, 4])

            class Slot:
                pass
            slots = []
            for i in range(2):
                B = Slot()
                B.i = i
                B.ps = PS[i]
                B.W = sb2(f"W{i}", [128, 8, 256], BF16)
                B.W2 = sb2(f"W2{i}", [128, 8, 256], BF16)
                B.W3 = sb2(f"W3{i}", [128, 8, 256], BF16)
                B.W4 = sb2(f"W4{i}", [128, 8, 256], BF16)
                B.qT = sb2(f"qT{i}", [128, S], BF16)
                B.qxT = sb2(f"qxT{i}", [128, S], BF16)
                B.kT = sb2(f"kT{i}", [128, S], BF16)
                B.ktok = sb2(f"ktok{i}", [128, NT, 128], BF16)
                B.vaug = sb2(f"vaug{i}", [128, NT, 260], BF16)
                B.vz = sb2(f"vz{i}", [128, 260], BF16)
                B.gnb = sb2(f"gnb{i}", [128, 256])
                B.gg = sb2(f"gg{i}", [128, 256])
                B.scm = sb2(f"scm{i}", [128, 128], BF16)
                B.r32 = sb2(f"r32{i}", [128, 260])
                B.rbf = sb2(f"rbf{i}", [128, 260], BF16)
                B.ysb = sb2(f"ysb{i}", [128, 260])
                B.tmp257 = sb2(f"tmp257{i}", [128, 260])
                B.mixb = sb2(f"mixb{i}", [128, 256], BF16)
                B.mixst = [sb2(f"mixst{i}a", [128, 2, 128], BF16), sb2(f"mixst{i}b", [128, 2, 128], BF16)]
                B.sigo = sb2(f"sigo{i}", [128, 256])
                B.diagG = sb2(f"diagG{i}", [128, 128])
                B.logD = sb2(f"logD{i}", [128, 128])
                B.dmat = sb2(f"dmat{i}", [128, 128])
                B.Sbf = sb2(f"Sbf{i}", [128, 128], BF16)
                B.p1 = [(B.diagG, B.logD, B.dmat, B.Sbf),
                        (sb2(f"diagG{i}b", [128, 128]), sb2(f"logD{i}b", [128, 128]), sb2(f"dmat{i}b", [128, 128]),
                         sb2(f"Sbf{i}b", [128, 128], BF16))]
                B.ST = sb2(f"ST{i}", [128, NT, 128], BF16)
                B.sc16 = sb2(f"sc16{i}", [128, 4, NT])
                B.STb = sb2(f"STb{i}", [128, 128], BF16)
                B.cin = sb2(f"cin{i}", [128, 520])
                B.cacc = sb2(f"cacc{i}", [128, 512])
                B.csig = sb2(f"csig{i}", [128, 512])
                B.sm = sb2(f"sms{i}", [128, 64])
                B.wsi = 0
                B.mxi = 0
                slots.append(B)

            def loadw(B, dst, c0, ncols):
                k.dma(dst[:, :, 0:ncols], w_in[:, c0:c0 + ncols].rearrange("(kc p) n -> p kc n", p=128), eng='pool')

            def projfm(B, Wt, evac):
                pl = [B.ps['pa'], B.ps['pc']]
                for g in range(4):
                    p = pl[g % 2]
                    for kc in range(8):
                        k.mm(p[:], Wt[:, kc, 0:128], uT[:, kc, g * 512:(g + 1) * 512], start=(kc == 0), stop=(kc == 7))
                    evac(g, p)
                    yield

            def projtm(B, Wt, ncols, T, p, c0=0):
                for kc in range(8):
                    k.mm(p[:, c0:c0 + ncols], uT[:, kc, T * 128:(T + 1) * 128], Wt[:, kc, 0:ncols], start=(kc == 0), stop=(kc == 7))

            def hnorm(B, ysb, gg, out_bf):
                sm_ = B.sm
                st6 = sm_[:, 8:14]
                mv = sm_[:, 14:16]
                k.bnstats(st6, ysb)
                k.bnaggr(mv, st6)
                rstd = sm_[:, 16:17]
                k.ts('dve', rstd, mv[:, 1:2], EPS, None, ALU.add)
                k.act(rstd, rstd, AF.Sqrt)
                k.recip(rstd, rstd)
                k.ts('dve', ysb, ysb, mv[:, 0:1], rstd, ALU.subtract, ALU.mult)
                k.tt('dve', out_bf, ysb, gg, ALU.mult)

            def mix_out(B, fc0, T):
                pt = B.ps['pT']
                for j in range(2):
                    k.tr(pt[:, 512 + j * 128:512 + (j + 1) * 128], B.mixb[:, j * 128:(j + 1) * 128], C.identb[:])
                ms = B.mixst[B.mxi % 2]
                B.mxi += 1
                k.cp('act', ms[:], pt[:, 512:768].rearrange("p (c t) -> p c t", c=2))
                k.dma(mixd[fc0:fc0 + 2, :, T * 128:(T + 1) * 128].rearrange("c p t -> p c t"), ms[:])

            def ktok_from_kT(B):
                pt = B.ps['pT']
                for T in range(NT):
                    k.tr(pt[:, (T % 4) * 128:(T % 4 + 1) * 128], B.kT[:, T * 128:(T + 1) * 128], C.identb[:])
                    k.cp('dve', B.ktok[:, T, :], pt[:, (T % 4) * 128:(T % 4 + 1) * 128])
                    if T % 4 == 3:
                        yield

            def v_proj(B, Wt):
                pl = [B.ps['pa'], B.ps['pc']]
                for T in range(NT):
                    p = pl[T % 2]
                    projtm(B, Wt, 256, T, p)
                    k.cp('act', B.vaug[:, T, 0:256], p[:, 0:256])
                    if T % 2 == 1:
                        yield

            def ret_head(h, B):
                pa, pb, pc = B.ps['pa'], B.ps['pb'], B.ps['pc']
                qT, qxT, kT, vaug = B.qT, B.qxT, B.kT, B.vaug
                loadw(B, B.W, h * 128, 128)
                loadw(B, B.W2, 512 + h * 128, 128)
                loadw(B, B.W3, 1024 + h * 256, 256)
                loadw(B, B.W4, 2048 + h * 256, 256)
                yield
                yield from projfm(B, B.W, lambda g, p: k.cp('act', qT[:, g * 512:(g + 1) * 512], p[:]))
                k.tt('dve', qxT[:].rearrange("p (c i) -> p c i", c=NT), qT[:].rearrange("p (c i) -> p c i", c=NT),
                     xi[:, h, :].unsqueeze(1).to_broadcast([128, NT, 128]), ALU.mult)
                yield from projfm(B, B.W2, lambda g, p: k.cp('act', kT[:, g * 512:(g + 1) * 512], p[:]))
                yield from ktok_from_kT(B)
                yield from v_proj(B, B.W3)
                k.dma(B.gnb[:], A['ret_norm'][h * 256:(h + 1) * 256].partition_broadcast(128))
                k.ts('pool', B.gnb[:], B.gnb[:], 0.5, None, ALU.mult)
                k.memset('pool', B.r32[:], 0.0)
                for c in range(NT):
                    projtm(B, B.W4, 256, c, pc)
                    k.act(B.sigo[:], pc[:, 0:256], AF.Tanh, scale=0.5)
                    k.stt('dve', B.sigo[:], B.sigo[:], 1.0, pc[:, 0:256], ALU.add, ALU.mult)
                    k.tt('pool', B.gg[:], B.sigo[:], B.gnb[:], ALU.mult)
                    yield
                    k.mm(pb[:, 0:128], kT[:, c * 128:(c + 1) * 128], qT[:, c * 128:(c + 1) * 128])
                    k.tt('dve', B.scm[:], pb[:, 0:128], decT[:, h, :], ALU.mult)
                    yield
                    k.mm(pa[:, 0:256], B.scm[:], vaug[:, c, 0:256], start=True, stop=(c == 0))
                    if c > 0:
                        k.mm(pa[:, 0:256], qxT[:, c * 128:(c + 1) * 128], B.rbf[:, 0:256], start=False, stop=True)
                    if c < NT - 1:
                        k.ts('dve', B.vz[:, 0:256], vaug[:, c, 0:256], zeta[:, h:h + 1], None, ALU.mult)
                        k.mm(pb[:, 256:512], B.ktok[:, c, :], B.vz[:, 0:256])
                    yield
                    if c < NT - 1:
                        k.stt('dve', B.r32[:, 0:256], B.r32[:, 0:256], float(RET_G[h] ** 128), pb[:, 256:512], ALU.mult, ALU.add)
                        k.cp('act', B.rbf[:, 0:256], B.r32[:, 0:256])
                    k.cp('act', B.ysb[:, 0:256], pa[:, 0:256])
                    yield
                    sm_ = B.sm
                    k.bnstats(sm_[:, 8:14], B.ysb[:, 0:256])
                    k.bnaggr(sm_[:, 14:16], sm_[:, 8:14])
                    yield
                    rstd = sm_[:, 16:17]
                    k.ts('dve', rstd, sm_[:, 15:16], EPS, None, ALU.add)
                    k.tt('pool', rstd, rstd, C.neghalf[:], ALU.pow)
                    yield
                    k.ts('dve', B.ysb[:, 0:256], B.ysb[:, 0:256], sm_[:, 14:15], rstd, ALU.subtract, ALU.mult)
                    yield
                    k.tt('dve', B.mixb[:], B.ysb[:, 0:256], B.gg[:], ALU.mult)
                    mix_out(B, h * 2, c)
                    yield

            MQ, MK, MV, MI, MF, MO, MZ = 3072, 3584, 4096, 5120, 5124, 5128, 6152

            def ml_head(h, B):
                pa, pb, pc = B.ps['pa'], B.ps['pb'], B.ps['pc']
                qT, kT, vaug = B.qT, B.kT, B.vaug
                sm_ = B.sm
                cin, cacc, csig = B.cin, B.cacc, B.csig
                for which, col0, dstT in ((0, MQ + h * 128, qT), (1, MK + h * 128, kT)):
                    loadw(B, B.W, col0, 128)
                    blk = which * 4 + h
                    k.memset('pool', cin[:, 0:3], 0.0)
                    yield

                    def evac(g, p, blk=blk, dstT=dstT, which=which):
                        k.cp('act', cin[:, 3:515], p[:])
                        k.ts('dve', cacc[:], cin[:, 3:515], convw[:, blk, 3:4], convb[:, blk:blk + 1], ALU.mult, ALU.add)
                        for j in range(3):
                            k.stt('dve', cacc[:], cin[:, j:j + 512], convw[:, blk, j:j + 1], cacc[:], ALU.mult, ALU.add)
                        k.cp('pool', cin[:, 0:3], cin[:, 512:515])
                        k.act(csig[:], cacc[:], AF.Tanh, scale=0.5)
                        sc = 0.5 if which == 0 else 0.5 * 128 ** -0.5
                        k.stt('dve', csig[:], csig[:], 1.0, cacc[:], ALU.add, ALU.mult)
                        k.ts('pool', dstT[:, g * 512:(g + 1) * 512], csig[:], sc, None, ALU.mult)
                    yield from projfm(B, B.W, evac)
                yield from ktok_from_kT(B)
                loadw(B, B.W, MV + h * 256, 256)
                yield
                yield from v_proj(B, B.W)
                k.memset('pool', vaug[:, :, 256:257], 1.0)
                loadw(B, B.W, MO + h * 256, 256)
                loadw(B, B.W2, MZ + h * 256, 256)
                k.dma(B.gnb[:], A['ml_norm'][h * 256:(h + 1) * 256].partition_broadcast(128))
                k.ts('pool', B.gnb[:], B.gnb[:], 0.5, None, ALU.mult)
                k.memset('pool', B.r32[:], 0.0)
                yield
                r32, rbf, ysb, tmp257 = B.r32, B.rbf, B.ysb, B.tmp257
                for c in range(NT):
                    mprev = mst[:, h:h + 1]
                    bc_t = bcum[:, c, h:h + 1]
                    projtm(B, B.W, 256, c, pc)
                    projtm(B, B.W2, 256, c, pc, c0=256)
                    k.act(B.sigo[:], pc[:, 0:256], AF.Tanh, scale=0.5)
                    k.act(B.gg[:], pc[:, 256:512], AF.Tanh, scale=0.5)
                    k.ts('dve', B.sigo[:], B.sigo[:], 0.5, 0.5, ALU.mult, ALU.add)
                    k.stt('dve', B.gg[:], B.gg[:], 1.0, pc[:, 256:512], ALU.add, ALU.mult)
                    k.tt('pool', B.gg[:], B.gg[:], B.gnb[:], ALU.mult)
                    yield
                    k.ts('dve', B.diagG[:], identf[:], Gtok[:, c, h:h + 1], None, ALU.mult)
                    k.mm(pb[:, 0:128], onesf[:], B.diagG[:])
                    yield
                    k.stt('dve', B.logD[:], pb[:, 0:128], bc_t, negmask[:], ALU.add, ALU.add)
                    mx = sm_[:, 20:21]
                    maxG = sm_[:, 21:22]
                    k.red('dve', maxG, pb[:, 0:128], ALU.max)
                    k.red('dve', mx, B.logD[:], ALU.max)
                    yield
                    inter = sm_[:, 22:23]
                    k.tt('dve', inter, bc_t, mprev, ALU.add)
                    negm = sm_[:, 23:24]
                    k.tt('dve', negm, inter, mx, ALU.max)
                    yield
                    k.ts('dve', negm, negm, -1.0, None, ALU.mult)
                    k.mm(pb[:, 128:256], qT[:, c * 128:(c + 1) * 128], kT[:, c * 128:(c + 1) * 128])
                    yield
                    k.act(B.dmat[:], B.logD[:], AF.Exp, bias=negm)
                    yield
                    k.tt('dve', B.Sbf[:], pb[:, 128:256], B.dmat[:], ALU.mult)
                    yield
                    pt = B.ps['pT']
                    k.tr(pt[:, 0:128], B.Sbf[:], C.identb[:])
                    yield
                    k.cp('act', B.STb[:], pt[:, 0:128])
                    yield
                    k.mm(pa[:, 0:257], B.STb[:], vaug[:, c, 0:257])
                    if c > 0:
                        k.mm(pc[:, 0:257], qT[:, c * 128:(c + 1) * 128], rbf[:, 0:257])
                        wint = sm_[:, 24:25]
                        k.act(wint, inter, AF.Exp, bias=negm)
                        yield
                        k.act(tmp257[:, 0:257], pc[:, 0:257], AF.Copy, scale=wint)
                        yield
                        k.tt('dve', ysb[:, 0:257], tmp257[:, 0:257], pa[:, 0:257], ALU.add)
                    else:
                        yield
                        k.cp('act', ysb[:, 0:257], pa[:, 0:257])
                    enm = sm_[:, 25:26]
                    k.act(enm, negm, AF.Exp)
                    yield
                    den = sm_[:, 26:27]
                    k.act(den, ysb[:, 256:257], AF.Abs)
                    yield
                    k.tt('dve', den, den, enm, ALU.max)
                    yield
                    k.recip(den, den)
                    yield
                    k.stt('dve', ysb[:, 0:256], ysb[:, 0:256], den, B.sigo[:], ALU.mult, ALU.mult)
                    yield
                    if c < NT - 1:
                        nmaxG = sm_[:, 27:28]
                        k.ts('dve', nmaxG, maxG, -1.0, None, ALU.mult)
                        wj = sm_[:, 28:29]
                        k.act(wj, Gtok[:, c, h:h + 1], AF.Exp, bias=nmaxG)
                        yield
                        k.ts('dve', B.vz[:, 0:257], vaug[:, c, 0:257], wj, None, ALU.mult)
                        yield
                        k.mm(pa[:, 0:257], B.ktok[:, c, :], B.vz[:, 0:257])
                        bl = blast[:, c, h:h + 1]
                        mu = sm_[:, 29:30]
                        k.tt('dve', mu, bl, maxG, ALU.add)
                        bm = sm_[:, 30:31]
                        k.tt('dve', bm, bl, mprev, ALU.add)
                        yield
                        nmn = sm_[:, 31:32]
                        k.tt('dve', nmn, bm, mu, ALU.max)
                        yield
                        k.cp('dve', mprev, nmn)
                        k.ts('dve', nmn, nmn, -1.0, None, ALU.mult)
                        yield
                        sp = sm_[:, 32:33]
                        scl = sm_[:, 33:34]
                        k.act(sp, bm, AF.Exp, bias=nmn)
                        k.act(scl, mu, AF.Exp, bias=nmn)
                        yield
                        k.act(tmp257[:, 0:257], pa[:, 0:257], AF.Copy, scale=scl)
                        yield
                        k.stt('dve', r32[:, 0:257], r32[:, 0:257], sp, tmp257[:, 0:257], ALU.mult, ALU.add)
                        yield
                        k.cp('act', rbf[:, 0:257], r32[:, 0:257])
                        yield
                    k.bnstats(sm_[:, 8:14], ysb[:, 0:256])
                    k.bnaggr(sm_[:, 14:16], sm_[:, 8:14])
                    yield
                    rstd = sm_[:, 16:17]
                    k.ts('dve', rstd, sm_[:, 15:16], EPS, None, ALU.add)
                    k.tt('pool', rstd, rstd, C.neghalf[:], ALU.pow)
                    yield
                    k.ts('dve', ysb[:, 0:256], ysb[:, 0:256], sm_[:, 14:15], rstd, ALU.subtract, ALU.mult)
                    yield
                    k.tt('dve', B.mixb[:], ysb[:, 0:256], B.gg[:], ALU.mult)
                    mix_out(B, 8 + h * 2, c)
                    yield

            def ml_head2(h, B):
                pa, pb, pc = B.ps['pa'], B.ps['pb'], B.ps['pc']
                qT, kT, vaug = B.qT, B.kT, B.vaug
                sm_ = B.sm
                cin, cacc, csig = B.cin, B.cacc, B.csig
                loadw(B, B.W, MQ + h * 128, 128)
                loadw(B, B.W2, MK + h * 128, 128)
                loadw(B, B.W3, MV + h * 256, 256)
                loadw(B, B.W4, MO + h * 256, 256)
                for which, col0, dstT in ((0, MQ + h * 128, qT), (1, MK + h * 128, kT)):
                    Wqk = B.W if which == 0 else B.W2
                    blk = which * 4 + h
                    k.memset('pool', cin[:, 0:3], 0.0)
                    yield

                    def evac(g, p, blk=blk, dstT=dstT, which=which):
                        k.cp('act', cin[:, 3:515], p[:])
                        k.ts('dve', cacc[:], cin[:, 3:515], convw[:, blk, 3:4], convb[:, blk:blk + 1], ALU.mult, ALU.add)
                        for j in range(3):
                            k.stt('dve', cacc[:], cin[:, j:j + 512], convw[:, blk, j:j + 1], cacc[:], ALU.mult, ALU.add)
                        k.cp('pool', cin[:, 0:3], cin[:, 512:515])
                        k.act(csig[:], cacc[:], AF.Tanh, scale=0.5)
                        sc = 0.5 if which == 0 else 0.5 * 128 ** -0.5
                        k.stt('dve', csig[:], csig[:], 1.0, cacc[:], ALU.add, ALU.mult)
                        k.ts('pool', dstT[:, g * 512:(g + 1) * 512], csig[:], sc, None, ALU.mult)
                    yield from projfm(B, Wqk, evac)
                yield from ktok_from_kT(B)
                loadw(B, B.W, MZ + h * 256, 256)
                yield
                yield from v_proj(B, B.W3)
                k.memset('pool', vaug[:, :, 256:257], 1.0)
                k.dma(B.gnb[:], A['ml_norm'][h * 256:(h + 1) * 256].partition_broadcast(128))
                k.ts('pool', B.gnb[:], B.gnb[:], 0.5, None, ALU.mult)
                k.memset('pool', B.r32[:], 0.0)
                yield
                inter_all, negm_all, wint_all, enm_all = B.sc16[:, 0, :], B.sc16[:, 1, :], B.sc16[:, 2, :], B.sc16[:, 3, :]
                k.tt('dve', inter_all, bcum[:, :, h], m_all[:, 0:NT, h], ALU.add)
                pt = B.ps['pT']
                for c in range(NT):
                    P = pb if c % 2 == 0 else pa
                    dG, lD, dM, Sb = B.p1[c % 2]
                    k.ts('dve', dG[:], identf[:], Gtok[:, c, h:h + 1], None, ALU.mult)
                    k.mm(P[:, 0:128], onesf[:], dG[:])
                    k.mm(P[:, 128:256], qT[:, c * 128:(c + 1) * 128], kT[:, c * 128:(c + 1) * 128])
                    yield
                    k.stt('dve', lD[:], P[:, 0:128], bcum[:, c, h:h + 1], negmask[:], ALU.add, ALU.add)
                    yield
                    k.red('dve', negm_all[:, c:c + 1], lD[:], ALU.max)
                    yield
                    k.tt('dve', negm_all[:, c:c + 1], negm_all[:, c:c + 1], inter_all[:, c:c + 1], ALU.max)
                    yield
                    k.ts('dve', negm_all[:, c:c + 1], negm_all[:, c:c + 1], -1.0, None, ALU.mult)
                    yield
                    k.act(dM[:], lD[:], AF.Exp, bias=negm_all[:, c:c + 1])
                    yield
                    k.tt('dve', Sb[:], P[:, 128:256], dM[:], ALU.mult)
                    yield
                    k.tr(pt[:, (c % 4) * 128:(c % 4 + 1) * 128], Sb[:], C.identb[:])
                    yield
                    k.cp('act', B.ST[:, c, :], pt[:, (c % 4) * 128:(c % 4 + 1) * 128])
                    yield
                k.tt('dve', wint_all, inter_all, negm_all, ALU.add)
                k.act(wint_all, wint_all, AF.Exp)
                k.act(enm_all, negm_all, AF.Exp)
                yield
                r32, rbf, ysb, tmp257 = B.r32, B.rbf, B.ysb, B.tmp257
                for c in range(NT):
                    projtm(B, B.W4, 256, c, pc)
                    projtm(B, B.W, 256, c, pc, c0=256)
                    k.act(B.sigo[:], pc[:, 0:256], AF.Tanh, scale=0.5)
                    k.act(B.gg[:], pc[:, 256:512], AF.Tanh, scale=0.5)
                    yield
                    k.ts('dve', B.sigo[:], B.sigo[:], 0.5, 0.5, ALU.mult, ALU.add)
                    k.stt('dve', B.gg[:], B.gg[:], 1.0, pc[:, 256:512], ALU.add, ALU.mult)
                    yield
                    k.tt('pool', B.gg[:], B.gg[:], B.gnb[:], ALU.mult)
                    k.mm(pa[:, 0:257], B.ST[:, c, :], vaug[:, c, 0:257])
                    if c > 0:
                        k.mm(pb[:, 0:257], qT[:, c * 128:(c + 1) * 128], rbf[:, 0:257])
                        yield
                        k.act(tmp257[:, 0:257], pb[:, 0:257], AF.Copy, scale=wint_all[:, c:c + 1])
                        yield
                        k.tt('dve', ysb[:, 0:257], tmp257[:, 0:257], pa[:, 0:257], ALU.add)
                    else:
                        yield
                        k.cp('act', ysb[:, 0:257], pa[:, 0:257])
                    yield
                    den = sm_[:, 26:27]
                    k.act(den, ysb[:, 256:257], AF.Abs)
                    yield
                    k.tt('dve', den, den, enm_all[:, c:c + 1], ALU.max)
                    yield
                    k.recip(den, den)
                    yield
                    k.stt('dve', ysb[:, 0:256], ysb[:, 0:256], den, B.sigo[:], ALU.mult, ALU.mult)
                    yield
                    if c < NT - 1:
                        wj = sm_[:, 28:29]
                        k.act(wj, Gtok[:, c, h:h + 1], AF.Exp, bias=nmaxG_all[:, c, h:h + 1])
                        yield
                        k.ts('dve', B.vz[:, 0:257], vaug[:, c, 0:257], wj, None, ALU.mult)
                        yield
                        k.mm(pc[:, 0:257], B.ktok[:, c, :], B.vz[:, 0:257])
                        yield
                        k.act(tmp257[:, 0:257], pc[:, 0:257], AF.Copy, scale=sc_all[:, c, h:h + 1])
                        yield
                        k.stt('dve', r32[:, 0:257], r32[:, 0:257], sp_all[:, c, h:h + 1], tmp257[:, 0:257], ALU.mult, ALU.add)
                        yield
                        k.cp('act', rbf[:, 0:257], r32[:, 0:257])
                        yield
                    k.bnstats(sm_[:, 8:14], ysb[:, 0:256])
                    k.bnaggr(sm_[:, 14:16], sm_[:, 8:14])
                    yield
                    rstd = sm_[:, 16:17]
                    k.ts('dve', rstd, sm_[:, 15:16], EPS, None, ALU.add)
                    k.tt('pool', rstd, rstd, C.neghalf[:], ALU.pow)
                    yield
                    k.ts('dve', ysb[:, 0:256], ysb[:, 0:256], sm_[:, 14:15], rstd, ALU.subtract, ALU.mult)
                    yield
                    k.tt('dve', B.mixb[:], ysb[:, 0:256], B.gg[:], ALU.mult)
                    mix_out(B, 8 + h * 2, c)
                    yield

            print("L0 mixer sbuf remaining", k.nc.sbuf_bytes_remaining, flush=True)
            norm_to_uT(k, C, x_dram, gpreT, uT)
            for pi, pair in enumerate(((0, 1), (2, 3))):
                if L0_STOP <= 1 + pi:
                    break
                run_interleaved([ret_head(h, slots[i]) for i, h in enumerate(pair)][:L0_NG], stagger=STAG_RET)
            B0 = slots[0]
            loadw(B0, B0.W, MI, 8)
            for T in range(NT):
                p = C.pa[T % 2]
                projtm(B0, B0.W, 8, T, p)
                k.tt('dve', ifg[:, T, :], p[:, 0:8], gateb[:], ALU.add)
            k.act(logf[:], ifg[:, :, 4:8], AF.Exp, scale=-1.0)
            k.act(logf[:], logf[:], AF.Ln, bias=1.0)
            k.ts('dve', logf[:], logf[:], -1.0, None, ALU.mult)
            pbc = C.pa[0]
            for T in range(NT):
                k.mm(pbc[:, T * 4:(T + 1) * 4], triu[:], logf[:, T, :])
            k.cp('dve', bcum[:].rearrange("p t h -> p (t h)"), pbc[:, 0:64])
            k.tt('dve', Gtok[:], ifg[:, :, 0:4], bcum[:], ALU.subtract)
            pbl = C.pa[1]
            k.mm(pbl[:, 0:64], onesf[:], logf[:].rearrange("p t h -> p (t h)"))
            k.cp('dve', blast[:].rearrange("p t h -> p (t h)"), pbl[:, 0:64])
            k.memset('pool', mst[:], NEG)
            dgt = [slots[0].p1[0][0], slots[1].p1[0][0]]
            pgt = [PS[0]['pb'], PS[1]['pb']]
            for c in range(NT):
                for hh in range(4):
                    i_ = (c * 4 + hh) % 2
                    k.ts('dve', dgt[i_][:], identf[:], Gtok[:, c, hh:hh + 1], None, ALU.mult)
                    k.mm(pgt[i_][:, 0:128], onesf[:], dgt[i_][:])
                    k.red('dve', maxG_all[:, c, hh:hh + 1], pgt[i_][:, 0:128], ALU.max)
            k.tt('dve', mu_all[:], blast[:], maxG_all[:], ALU.add)
            k.ts('dve', nmaxG_all[:], maxG_all[:], -1.0, None, ALU.mult)
            k.memset('pool', m_all[:, 0, :], NEG)
            for c in range(NT):
                k.tt('dve', bm_all[:, c, :], blast[:, c, :], m_all[:, c, :], ALU.add)
                k.tt('dve', m_all[:, c + 1, :], bm_all[:, c, :], mu_all[:, c, :], ALU.max)
            k.tt('dve', sp_all[:], bm_all[:], m_all[:, 1:NT + 1, :], ALU.subtract)
            k.act(sp_all[:], sp_all[:], AF.Exp)
            k.tt('dve', sc_all[:], mu_all[:], m_all[:, 1:NT + 1, :], ALU.subtract)
            k.act(sc_all[:], sc_all[:], AF.Exp)
            for pi, pair in enumerate(((0, 1), (2, 3))):
                if L0_STOP <= 4 + pi:
                    break
                run_interleaved([ml_head2(h, slots[i]) for i, h in enumerate(pair)][:L0_NG], stagger=STAG_ML)
        k.barrier()
        with ExitStack() as st3:
            sb3 = lambda n, s, d=F32: k.sb(n, s, d, st3)
            gpost_b = sb3("gpostb", [128, 1024])
            k.dma(gpost_b[:], A['gpost0'][:].partition_broadcast(128))
            Wg = sb3("Wg", [128, 8, 1024], BF16)
            Wp = sb3("Wp", [128, 2, 1024], BF16)
            Wo = sb3("Wo0", [128, 16, 1024], BF16)
            for half in range(2):
                load_w(k, C, Wo[:, :, half * 512:(half + 1) * 512], A['w_out0'], 0, 16, half * 512, 512, eng='rr')
            for half in range(2):
                load_w(k, C, Wg[:, :, half * 512:(half + 1) * 512], A['pe_gate0'], 0, 8, half * 512, 512, eng='rr')
                load_w(k, C, Wp[:, :, half * 512:(half + 1) * 512], A['pe_proj0'], 0, 2, half * 512, 512, eng='rr')
            mt = [sb3("mt0", [128, 16, 128], BF16), sb3("mt1", [128, 16, 128], BF16)]
            PSL = [dict(pT=PS[i]['pT'], pg=PS[i]['pa'], pp=PS[i]['pc'], py=PS[i]['pb']) for i in range(2)]

            def gen_y(T, s_):
                m = mt[s_.i]
                k.dma(m[:], mixd[:, :, T * 128:(T + 1) * 128].rearrange("c p t -> p c t"))
                p = s_.ps['py']
                for half in range(2):
                    for fc in range(16):
                        k.mm(p[:], m[:, fc, :], Wo[:, fc, half * 512:(half + 1) * 512], start=(fc == 0), stop=(fc == 15))
                    yield
                    k.cp('act', s_.ytile[:, half * 512:(half + 1) * 512], p[:])
                    yield
            toks = post_block(k, C, sb3, PSL, x_dram, A['p0'], gpost_b, Wg, Wp, out_dram, gen_y)
        k.barrier()
    return toks


def l0_inputs(inp, b):
    d = {}
    d['x'] = np.ascontiguousarray(inp['x'][b])
    d['p0'] = np.ascontiguousarray(inp['p'][0, b])
    d['gpre0T'] = np.ascontiguousarray(inp['norm_pre'][0].reshape(8, 128).T)
    d['gpost0'] = np.ascontiguousarray(inp['norm_post'][0].reshape(1024))
    d['w_in0'] = np.ascontiguousarray(inp['ab_w_in'][0])
    d['ret_norm'] = np.ascontiguousarray(inp['ret_norm'][0].reshape(1024))
    d['ml_norm'] = np.ascontiguousarray(inp['ml_norm'][0].reshape(1024))
    d['convw'] = np.ascontiguousarray(inp['ml_conv_w'][0].T.reshape(8, 128, 4).transpose(1, 0, 2))
    d['convb'] = np.ascontiguousarray(inp['ml_conv_b'][0].reshape(8, 128).T)
    d['gateb'] = np.ascontiguousarray(inp['ml_gate_b'][0].reshape(8))
    d['w_out0'] = np.ascontiguousarray(inp['ab_w_out'][0])
    d['pe_gate0'] = np.ascontiguousarray(inp['pe_gate'][0])
    d['pe_proj0'] = np.ascontiguousarray(inp['pe_proj'][0])
    for kk, v in host_consts_l0().items():
        d['c_' + kk] = v
    return d


def declare(nc, d, skip=()):
    A = {}
    for kk, v in d.items():
        if kk in skip:
            continue
        dt = F32 if v.dtype == np.float32 else BF16
        A[kk] = nc.dram_tensor(kk, list(v.shape), dt, kind="ExternalInput").ap()
    return A


def build_l0(d):
    nc = bass.Bass("TRN2", target_bir_lowering=False)
    A = declare(nc, d)
    out = nc.dram_tensor("out", [S, D], F32, kind="ExternalOutput").ap()
    with ExitStack() as stack:
        k = KB(nc, stack)
        mixd = nc.dram_tensor("mix0T", [16, 128, S], BF16, kind="Internal").ap()
        toks = layer0(k, nc, A, A['x'], out, mixd)
        k.finish(toks)
        print("L0 inst", k.ninst, "waits", k.nwait, flush=True)
    return nc


SLOPES = [2.0 ** (-(h + 1) / 2.0) for h in range(16)]
MASKV = -60000.0
CQ, CKC, CVC, CKS, CVS, CKW, CVW, CG, CNZ, CSU, CSZ = 0, 1024, 1280, 1536, 1792, 2048, 2304, 2560, 2608, 3632, 4144


def _bf(x):
    return np.float64(np.float32(x).astype(ml_dtypes.bfloat16).astype(np.float32))


def _split3(x):
    hi = _bf(x)
    mid = _bf(x - hi)
    lo = _bf(x - hi - mid)
    return hi, mid, lo


def host_consts_l1():
    c = {}
    c['ident'] = np.eye(128, dtype=np.float32)
    kc = np.zeros((4, 32, 2048), np.float32)
    qc = np.zeros((4, 32, 2048), np.float32)
    t = np.arange(2048)
    for g in range(4):
        for j in range(4):
            pcs = _split3(SLOPES[4 * g + j])
            for r in range(3):
                kc[g, 6 * j + r, :] = -8.0 * 128.0 * pcs[r]
                kc[g, 6 * j + 3 + r, :] = -8.0 * pcs[r]
    for j in range(4):
        for r in range(3):
            qc[j, 6 * j + r, :] = t // 128
            qc[j, 6 * j + 3 + r, :] = t % 128
    c['kconst'] = kc
    c['qconst'] = qc
    c['E'] = (t[None, :] // 64 == np.arange(32)[:, None]).astype(np.float32)
    p = np.arange(128)
    c['pos'] = (128.0 * np.arange(16)[None, :] + p[:, None]).astype(np.float32)
    c['posc'] = (16.0 * p + 31.0).astype(np.float32).reshape(128, 1)
    c['cm'] = np.where(p[:, None] <= p[None, :], 0.0, MASKV).astype(np.float32)
    c['am'] = np.where(p[:, None] > p[None, :], 0.0, MASKV).astype(np.float32)
    cmpm = np.where((16 * p[:, None] + 31) <= t[None, :], 0.0, MASKV).astype(np.float32)
    cmpm[127, :] = MASKV
    c['cmpmask'] = cmpm
    d0 = (p[:, None] - 16.0 * p[None, :] - 31.0).astype(np.float32)
    d0[:, 127] = -1e6
    c['dist0'] = d0
    ct = np.zeros((128, 16, 32), np.float32)
    for T in range(16):
        tt = T * 128 + p
        n = np.arange(32)
        forced = (n[None, :] == 0) | (n[None, :] == (tt // 64)[:, None])
        valid = (n[None, :] * 64) <= tt[:, None]
        ct[:, T, :] = np.where(forced, 1e9, np.where(valid, 0.0, -1.0))
    c['ct'] = ct
    c['rowvalid'] = (p >= 31).astype(np.float32).reshape(128, 1)
    c['iota'] = np.broadcast_to(np.arange(512, dtype=np.float32)[None, :], (128, 512)).copy()
    c['halfmask'] = (p[:, None] // 64 == np.arange(2)[None, :]).astype(np.float32)
    return c


def proj_fm64(k, C, W, uT, evac):
    for g in range(4):
        ps = C.pa[C.pai % 2]
        C.pai += 1
        for kc in range(8):
            k.mm(ps[0:64, :], W[:, kc, 0:64], uT[:, kc, g * 512:(g + 1) * 512], start=(kc == 0), stop=(kc == 7))
        evac(g, ps)


def stage_rows(k, C, dst, src, p0, p1, ncols, eng='pool'):
    k.dma(dst[p0:p1, 0:ncols], src[:, 0:ncols], eng='pool')


def norm2max(k, C, srcT, ncols, dst):
    npieces = (ncols + 511) // 512
    scr = C.nmx[C.nmi % 2]
    C.nmi += 1
    for i, c0 in enumerate(range(0, ncols, 512)):
        n = min(512, ncols - c0)
        sq = C.sqbs[C.sqi % 2]
        ps = C.pa[C.sqi % 2]
        C.sqi += 1
        k.act(sq[0:64, 0:n], srcT[0:64, c0:c0 + n], AF.Square)
        k.mm(ps[:, 0:n], C.onesb[0:64, :], sq[0:64, 0:n])
        k.red('dve', scr[:, i:i + 1], ps[:, 0:n], ALU.max)
    if npieces == 1:
        k.cp('dve', dst, scr[:, 0:1])
    else:
        k.red('dve', dst, scr[:, 0:npieces], ALU.max)


L1_STOP = 99
L1_VAR = 0
L1_S5 = True


def layer1(k, nc, A, h_dram, out_dram):
    with ExitStack() as st:
        C = Ctx()
        C.wsi = 0
        C.pai = 0
        C.pti = 0
        sb = lambda n, s, d=F32: k.sb(n, s, d, st)
        ps = lambda n, s, d=F32: k.ps(n, s, d, st)
        C.pa = [ps("pa0", [128, 512]), ps("pa1", [128, 512])]
        C.psc = [ps("psc0", [128, 512]), ps("psc1", [128, 512])]
        C.po = [ps("po0", [128, 512]), ps("po1", [128, 512])]
        C.pm = C.pa[1]
        C.pT = [ps("pT0", [128, 1024], BF16), ps("pT1", [128, 1024], BF16)]
        identf = sb("identf", [128, 128])
        C.identf = identf
        C.identb = sb("identb", [128, 128], BF16)
        k.dma(identf[:], A['d_ident'][:, :])
        k.cp('dve', C.identb[:], identf[:])
        onesf = sb("onesf", [128, 128])
        k.memset('pool', onesf[:], 1.0)
        C.onesb = sb("onesb", [128, 128], BF16)
        k.memset('pool', C.onesb[:], 1.0)
        C.neghalf = sb("neghalf", [128, 1])
        k.memset('pool', C.neghalf[:], -0.5)
        gpreT = sb("gpreT", [128, 8])
        k.dma(gpreT[:], A['gpre1T'][:, :])
        C.xt = [sb("xt0", [128, 1024]), sb("xt1", [128, 1024])]
        C.junk = sb("junk", [128, 1024])
        C.xn = sb("xn", [128, 1024], BF16)
        C.junks = [C.junk, sb("junk2", [128, 1024])]
        C.xns = [C.xn, sb("xn2", [128, 1024], BF16)]
        C.sm = sb("sm", [128, 64])
        sm = C.sm
        mixT = sb("mixT", [128, 12, S], BF16)
        w_in = A['w_in1']
        with ExitStack() as stu:
            uT = k.sb("uT", [128, 8, S], BF16, stu)
            norm_to_uT(k, C, h_dram, gpreT, uT)
            with ExitStack() as st2:
                sb2 = lambda n, s, d=F32: k.sb(n, s, d, st2)
                Wrot = [sb2(f"Wr{i_}", [128, 8, 128], BF16) for i_ in range(3)]
                wri = [0]

                def nextW():
                    wri[0] += 1
                    return Wrot[wri[0] % 3]
                W = Wrot[0]
                qaug = [sb2(f"qaug{j}", [128, S], BF16) for j in range(4)]
                kaug_s = sb2("kaug_s", [128, S], BF16)
                kaug_w = sb2("kaug_w", [128, S], BF16)
                kaug_c = sb2("kaug_c", [128, 128], BF16)
                kc8 = sb2("kc8", [128, 128], BF16)
                vaug_s = sb2("vaug_s", [128, NT, 66], BF16)
                vaug_w = sb2("vaug_w", [128, NT, 66], BF16)
                vcaug = sb2("vcaug", [128, 66], BF16)
                kcmpT = sb2("kcmpT", [128, S], BF16)
                vcmpT = sb2("vcmpT", [128, S], BF16)
                w1k = sb2("w1k", [128, 32, 128], BF16)
                w1v = sb2("w1v", [128, 32, 128], BF16)
                w2k = sb2("w2k", [128, 64], BF16)
                w2v = sb2("w2v", [128, 64], BF16)
                posk = sb2("posk", [128, 32], BF16)
                posv = sb2("posv", [128, 32], BF16)
                gates = sb2("gates", [128, NT, 48])
                ynsa = sb2("ynsa", [128, NT, 128])
                PT = [sb2(f"PT{i}", [128, 512], BF16) for i in range(3)]
                C.sqb = sb2("sqb", [128, 512], BF16)
                C.sqbs = [C.sqb, sb2("sqb2", [128, 512], BF16)]
                C.nmx = [sb2("nmx0", [128, 4]), sb2("nmx1", [128, 4])]
                C.nmi = 0
                C.sqi = 0
                cmb = sb2("cmb", [128, 128], BF16)
                amb = sb2("amb", [128, 128], BF16)
                cmpmask = sb2("cmpmask", [128, S], BF16)
                dist0 = sb2("dist0", [128, 128])
                ctab = sb2("ctab", [128, NT, 32])
                pos = sb2("pos", [128, 16])
                posc = sb2("posc", [128, 1])
                rowvalid = sb2("rowvalid", [128, 1])
                bias_all = sb2("bias_all", [128, 16])
                brow = sb2("brow", [128, 128], BF16)
                nsl = sb2("nsl", [128, 4, 128])
                slt = sb2("slt", [128, 4, 128])
                sc4 = sb2("sc4", [128, 4, 128])
                e4 = sb2("e4", [128, 4, 128])

                class SelSlot:
                    pass
                selslots = []
                for i_ in range(2):
                    s_ = SelSlot()
                    s_.dm = sb2(f"dm{i_}", [128, 128])
                    s_.dmk = sb2(f"dmk{i_}", [128, 128])
                    s_.slt = sb2(f"slt{i_}", [128, 4, 128]) if i_ else slt
                    s_.sc4 = sb2(f"sc4{i_}", [128, 4, 128]) if i_ else sc4
                    s_.e4 = sb2(f"e4{i_}", [128, 4, 128]) if i_ else e4
                    s_.pg = sb2(f"pg{i_}", [128, 128])
                    s_.imp = sb2(f"imp{i_}", [128, 32])
                    s_.top8 = sb2(f"top8{i_}", [128, 8])
                    s_.selb = sb2(f"selb{i_}", [128, 128], BF16)
                    s_.selT = sb2(f"selT{i_}", [128, 128], BF16)
                    s_.sm = sb2(f"ssm{i_}", [128, 8])
                    s_.pq = C.psc[i_]
                    s_.pT = C.pT[i_]
                    k.memset('pool', s_.selb[:], 0.0)
                    selslots.append(s_)
                Mall = sb2("Mall", [128, 4, 3])
                qn2 = sb2("qn2", [128, 8])
                bias_all3 = sb2("bias_all3", [128, 12, 16])
                hbf = sb2("hbf", [128, 128], BF16)
                otmp = sb2("otmp", [128, 4, 64])
                nzs = sb2("nzs", [128, 128])
                mixb = sb2("mixb", [128, 128], BF16)
                print("NSA sbuf remaining", k.nc.sbuf_bytes_remaining, flush=True)
                k.dma(dist0[:], A['d_dist0'][:, :])
                k.dma(ctab[:], A['d_ct'][:, :, :])
                k.dma(pos[:], A['d_pos'][:, :])
                k.dma(posc[:], A['d_posc'][:, :])
                k.dma(rowvalid[:], A['d_rowvalid'][:, :])
                k.dma(cmb[:], A['d_cm'][:, :], eng='pool')
                k.dma(amb[:], A['d_am'][:, :], eng='pool')
                stage_rows(k, C, cmpmask, A['d_cmpmask'], 0, 128, S, eng='rr')
                for j in range(4):
                    stage_rows(k, C, qaug[j], A['d_qconst'][j], 64, 96, S, eng='rr')
                    k.memset('pool', qaug[j][96:128, :], 0.0)
                stage_rows(k, C, kaug_s, A['d_E'], 96, 128, S, eng='rr')
                k.memset('pool', kaug_w[96:128, :], 0.0)
                k.memset('pool', kaug_c[:], 0.0)
                k.memset('pool', kc8[:], 0.0)
                k.memset('pool', vaug_s[:, :, 64:65], 1.0)
                k.memset('pool', vaug_w[:, :, 64:65], 1.0)
                k.memset('pool', vcaug[:], 0.0)
                k.memset('pool', vcaug[0:127, 64:65], 1.0)
                for (w1, src) in ((w1k, A['w1k']), (w1v, A['w1v'])):
                    for l0 in range(0, 32, 8):
                        k.dma(w1[0:64, l0:l0 + 8, :], src[:, l0:l0 + 8, :], eng='pool')
                for (w2, src) in ((w2k, A['w2k']), (w2v, A['w2v'])):
                    k.dma(w2[:], src[:, :], eng='pool')
                for (pp, src) in ((posk, A['poskT']), (posv, A['posvT'])):
                    k.dma(pp[0:64, :], src[:, :], eng='pool')
                load_w(k, C, W, w_in, 0, 8, CG, 48)
                for T in range(NT):
                    p = C.pa[C.pai % 2]
                    C.pai += 1
                    proj_tm(k, C, W, 0, 48, uT, T, p)
                    k.act(gates[:, T, :], p[:, 0:48], AF.Sigmoid)

                for g in range(4):
                    if L1_STOP <= 1 or (L1_STOP <= 6 and g > 0):
                        break
                    stage_rows(k, C, kaug_s, A['d_kconst'][g], 64, 96, S)
                    stage_rows(k, C, kaug_w, A['d_kconst'][g], 64, 96, S)
                    stage_rows(k, C, kaug_c, A['d_kconst'][g][:, 0:128], 64, 96, 128)
                    for col0, dst in ((CKS + g * 64, kaug_s), (CKW + g * 64, kaug_w), (CKC + g * 64, kcmpT), (CVC + g * 64, vcmpT)):
                        W = nextW()
                        load_w(k, C, W, w_in, 0, 8, col0, 64)
                        proj_fm64(k, C, W, uT, lambda gg_, p, dst=dst: k.cp('act', dst[0:64, gg_ * 512:(gg_ + 1) * 512], p[0:64, :]))
                    for col0, dst in ((CVS + g * 64, vaug_s), (CVW + g * 64, vaug_w)):
                        W = nextW()
                        load_w(k, C, W, w_in, 0, 8, col0, 64)
                        for T in range(NT):
                            p = C.pa[C.pai % 2]
                            C.pai += 1
                            proj_tm(k, C, W, 0, 64, uT, T, p)
                            k.cp('act', dst[:, T, 0:64], p[:, 0:64])
                    if L1_STOP <= 2:
                        break
                    for which, (srcT, w1, w2, pp) in enumerate(((kcmpT, w1k, w2k, posk), (vcmpT, w1v, w2v, posv))):
                        ph = C.pm
                        pbrow = C.pa[C.pai % 2]
                        C.pai += 1
                        for l in range(32):
                            k.mm(pbrow[0:1, 0:128], pp[0:64, l:l + 1], w1[0:64, l, :], start=(l == 0), stop=(l == 31))
                        k.cp('act', brow[0:1, :], pbrow[0:1, 0:128])
                        if L1_STOP <= 2.2:
                            break
                        v3 = srcT[0:64, :].rearrange("p (n l) -> p n l", l=16)
                        for l in range(32):
                            rhs = v3[:, 0:127, l] if l < 16 else v3[:, 1:128, l - 16]
                            k.mm(ph[:, 0:127], w1[0:64, l, :], rhs, start=(l == 0), stop=False)
                        k.mm(ph[:, 0:127], brow[0:1, :], C.onesb[0:1, 0:127], start=False, stop=True)
                        if L1_STOP <= 2.4:
                            break
                        k.act(hbf[:, 0:127], ph[:, 0:127], AF.Gelu_apprx_tanh)
                        if L1_STOP <= 2.8:
                            break
                        if which == 0:
                            pk2 = C.pa[C.pai % 2]
                            C.pai += 1
                            k.mm(pk2[0:64, 0:127], w2[:], hbf[:, 0:127])
                            k.cp('act', kaug_c[0:64, 0:127], pk2[0:64, 0:127])
                            k.ts('pool', kc8[0:64, 0:127], kaug_c[0:64, 0:127], 0.125, None, ALU.mult)
                        else:
                            pk2 = C.pa[C.pai % 2]
                            C.pai += 1
                            k.mm(pk2[0:127, 0:64], hbf[:, 0:127], w2[:])
                            k.cp('act', vcaug[0:127, 0:64], pk2[0:127, 0:64])
                    if L1_STOP <= 3:
                        break
                    for j in range(4):
                        h = 4 * g + j
                        W = nextW()
                        load_w(k, C, W, w_in, 0, 8, CQ + h * 64, 64)
                        proj_fm64(k, C, W, uT, lambda gg_, p, j=j: k.cp('act', qaug[j][0:64, gg_ * 512:(gg_ + 1) * 512], p[0:64, :]))
                    if L1_STOP <= 4:
                        break
                    for j in range(4):
                        k.memset('pool', nsl[:, j, :], -SLOPES[4 * g + j])
                    def sel_gen(T, s_):
                        dm, dmk, slt, sc4, e4, pg, imp, top8, selb, selT, sm_ = (s_.dm, s_.dmk, s_.slt, s_.sc4, s_.e4, s_.pg,
                                                                              s_.imp, s_.top8, s_.selb, s_.selT, s_.sm)
                        k.ts('dve', dm[:], dist0[:], float(128 * T), None, ALU.add)
                        k.ts('dve', dmk[:], dm[:], 0.0, 1e32, ALU.is_lt, ALU.mult)
                        yield
                        k.tt('dve', dm[:], dm[:], dmk[:], ALU.add)
                        pq_ = s_.pq
                        pq3 = pq_[:].rearrange("p (h j) -> p h j", h=4)
                        for j in range(4):
                            k.mm(pq3[:, j, :], qaug[j][0:64, T * 128:(T + 1) * 128], kc8[0:64, :])
                        yield
                        k.tt('pool', slt[:], nsl[:], dm[:].unsqueeze(1).to_broadcast([128, 4, 128]), ALU.mult)
                        yield
                        k.tt('dve', sc4[:], pq3, slt[:], ALU.add)
                        yield
                        k.red('dve', sm_[:, 0:4], sc4[:], ALU.max)
                        yield
                        k.tt('dve', sc4[:], sc4[:], sm_[:, 0:4].unsqueeze(2).to_broadcast([128, 4, 128]), ALU.subtract)
                        yield
                        k.act(e4[:], sc4[:], AF.Exp)
                        yield
                        k.red('dve', sm_[:, 4:8], e4[:], ALU.add)
                        yield
                        k.recip(sm_[:, 4:8], sm_[:, 4:8])
                        yield
                        if T == 0:
                            k.tt('dve', sm_[:, 4:8], sm_[:, 4:8], rowvalid[:].to_broadcast([128, 4]), ALU.mult)
                            yield
                        k.tt('pool', e4[:], e4[:], sm_[:, 4:8].unsqueeze(2).to_broadcast([128, 4, 128]), ALU.mult)
                        yield
                        k.red('dve', pg[:], e4[:].rearrange("p h j -> p j h"), ALU.add)
                        yield
                        pg3 = pg[:].rearrange("p (n f) -> p n f", f=4)
                        k.red('dve', imp[:], pg3, ALU.add)
                        yield
                        k.tt('dve', imp[:, 1:32], imp[:, 1:32], pg3[:, 0:31, 3], ALU.add)
                        yield
                        k.tt('dve', imp[:], imp[:], ctab[:, T, :], ALU.add)
                        yield
                        k.max8(top8[:], imp[:])
                        yield
                        k.ts('dve', imp[:], imp[:], top8[:, 3:4], None, ALU.is_ge)
                        yield
                        k.ts('dve', selb[:, 96:128], imp[:], 1.0, -MASKV, ALU.subtract, ALU.mult)
                        yield
                        pt = s_.pT
                        k.tr(pt[:, 0:128], selb[:], C.identb[:])
                        yield
                        k.cp('act', selT[96:128, :], pt[96:128, 0:128])
                        yield
                        for j in range(4):
                            k.cp('pool', qaug[j][96:128, T * 128:(T + 1) * 128], selT[96:128, :])
                        yield
                    for T0 in range(0, NT, 2):
                        run_interleaved([sel_gen(T0 + i, selslots[i]) for i in range(2)], stagger=0)
                    if L1_STOP <= 5:
                        break
                    norm2max(k, C, kaug_c, 127, qn2[:, 4:5])
                    norm2max(k, C, kaug_s, S, qn2[:, 5:6])
                    norm2max(k, C, kaug_w, S, qn2[:, 6:7])
                    for j in range(4):
                        norm2max(k, C, qaug[j], S, qn2[:, j:j + 1])
                    for br in range(3):
                        k.ts('dve', Mall[:, :, br], qn2[:, 0:4], qn2[:, 4 + br:5 + br], None, ALU.mult)
                    k.tt('pool', Mall[:].rearrange("p a b -> p (a b)"), Mall[:].rearrange("p a b -> p (a b)"),
                         C.neghalf[:].to_broadcast([128, 12]), ALU.pow)
                    k.recip(Mall[:].rearrange("p a b -> p (a b)"), Mall[:].rearrange("p a b -> p (a b)"))
                    k.ts('dve', Mall[:], Mall[:], 1.02 / 8.0, None, ALU.mult)
                    for j in range(4):
                        for br in range(3):
                            src = posc[:] if br == 0 else pos[:]
                            dstb = bias_all3[:, j * 3 + br, 0:1] if br == 0 else bias_all3[:, j * 3 + br, :]
                            k.ts('dve', dstb, src, SLOPES[4 * g + j], Mall[:, j, br:br + 1], ALU.mult, ALU.subtract)
                    for j in range(4):
                        h = 4 * g + j
                        for br in range(3):
                            bias_all = bias_all3[:, j * 3 + br, :]
                            kaug = (kaug_c, kaug_s, kaug_w)[br]
                            gcol = h * 3 + br
                            vaug_br = (None, vaug_s, vaug_w)[br]
                            items = []
                            for Q in range(4):
                                its = []
                                if br == 0:
                                    nk = min(127, 32 * (Q + 1))
                                    its.append([Q, 0, nk, 512 * Q, 512 * Q + 512, 'cmp'])
                                elif br == 1:
                                    for S_ in range(0, 4 * Q + 4):
                                        its.append([Q, S_, 128, max(512 * Q, 128 * S_), 512 * Q + 512, 'slc'])
                                else:
                                    for S_ in range(max(0, 4 * Q - 4), 4 * Q + 4):
                                        t0 = 128 * max(S_, 4 * Q)
                                        t1 = 128 * (min(S_ + 4, 4 * Q + 3) + 1)
                                        its.append([Q, S_, 128, t0, t1, 'win'])
                                nmm = sum((it[4] - it[3]) // 128 for it in its)
                                for ii, it in enumerate(its):
                                    it.append(ii == len(its) - 1)
                                    it.append(nmm)
                                items += its
                            immc = [0, 0, 0, 0]

                            def emit_S(it, idx):
                                Q, S_, nk, t0, t1, kind, lastq, nmm = it
                                n = t1 - t0
                                psc = psc3[idx % 3]
                                if kind == 'cmp':
                                    k.mm(psc[0:nk, 0:n], kaug[:, 0:nk], qaug[j][:, t0:t1], start=True, stop=False)
                                    k.mm(psc[0:nk, 0:n], C.identb[0:nk, 0:nk], cmpmask[0:nk, t0:t1], start=False, stop=True)
                                else:
                                    masks = []
                                    if 128 * S_ >= 512 * Q:
                                        masks.append((0, cmb))
                                    if kind == 'win' and S_ + 4 <= 4 * Q + 3:
                                        masks.append((n - 128, amb))
                                    k.mm(psc[:, 0:n], kaug[:, S_ * 128:(S_ + 1) * 128], qaug[j][:, t0:t1], start=True, stop=(len(masks) == 0))
                                    for mi, (c0, mt) in enumerate(masks):
                                        k.mm(psc[:, c0:c0 + 128], C.identb[:], mt[:], start=False, stop=(mi == len(masks) - 1))
                                return psc

                            def emit_E(it, idx, psc):
                                Q, S_, nk, t0, t1, kind, lastq, nmm = it
                                n = t1 - t0
                                bias = bias_all[0:nk, 0:1] if kind == 'cmp' else bias_all[:, S_:S_ + 1]
                                pt_ = PT[idx % 3]
                                k.act(pt_[0:nk, 0:n], psc[0:nk, 0:n], AF.Exp, bias=bias, scale=0.125)
                                return pt_

                            def emit_PV(it, pt_):
                                Q, S_, nk, t0, t1, kind, lastq, nmm = it
                                n = t1 - t0
                                po = C.po[Q % 2]
                                po3 = po[:].rearrange("p (s c) -> p s c", s=4)
                                for sub in range(n // 128):
                                    Tq = t0 // 128 + sub - 4 * Q
                                    rhs = vcaug[0:nk, 0:65] if kind == 'cmp' else vaug_br[:, S_, 0:65]
                                    k.mm(po3[:, Tq, 0:65], pt_[0:nk, sub * 128:(sub + 1) * 128], rhs,
                                         start=(immc[Q] == 0), stop=(immc[Q] == nmm - 1), skip_group_check=True)
                                    immc[Q] += 1
                                if lastq:
                                    rd = sm[:, 50:54]
                                    k.ts('dve', rd, po3[:, :, 64], 1e-30, None, ALU.max)
                                    k.recip(rd, rd)
                                    k.tt('dve', rd, rd, gates[:, 4 * Q:4 * Q + 4, gcol], ALU.mult)
                                    ydst = ynsa[:, 4 * Q:4 * Q + 4, (j % 2) * 64:(j % 2) * 64 + 64]
                                    if br == 0:
                                        k.tt('dve', ydst, po3[:, :, 0:64], rd.unsqueeze(2).to_broadcast([128, 4, 64]), ALU.mult)
                                    else:
                                        k.tt('dve', otmp[:], po3[:, :, 0:64], rd.unsqueeze(2).to_broadcast([128, 4, 64]), ALU.mult)
                                        k.tt('pool', ydst, ydst, otmp[:], ALU.add)
                            psc3 = [C.psc[0], C.psc[1], C.pa[0]]
                            nit = len(items)
                            pscs = {}
                            for idx in range(min(2, nit)):
                                pscs[idx] = emit_S(items[idx], idx)
                            for idx, it in enumerate(items):
                                pt_ = emit_E(it, idx, pscs.pop(idx))
                                if idx + 2 < nit:
                                    pscs[idx + 2] = emit_S(items[idx + 2], idx + 2)
                                emit_PV(it, pt_)
                        if j % 2 == 1:
                            m = h // 2
                            W = nextW()
                            load_w(k, C, W, w_in, 0, 8, CNZ + m * 128, 128)
                            for T in range(NT):
                                p = C.pa[C.pai % 2]
                                C.pai += 1
                                proj_tm(k, C, W, 0, 128, uT, T, p)
                                k.act(nzs[:], p[:, 0:128], AF.Tanh, scale=0.5)
                                k.stt('dve', nzs[:], nzs[:], 1.0, p[:, 0:128], ALU.add, ALU.mult)
                                k.stt('dve', mixb[:], nzs[:], 0.5, ynsa[:, T, :], ALU.mult, ALU.mult)
                                pt = C.pT[C.pti % 2]
                                C.pti += 1
                                k.tr(pt[:, 0:128], mixb[:], C.identb[:])
                                k.cp('act', mixT[:, m, T * 128:(T + 1) * 128], pt[:, 0:128])
            k.barrier()
            with ExitStack() as st2:
                sb2 = lambda n, s, d=F32: k.sb(n, s, d, st2)
                if L1_S5:
                    s5_block(k, C, A, sb2, uT, mixT, identf)
            k.barrier()
        k.barrier()
        with ExitStack() as st3:
            sb3 = lambda n, s, d=F32: k.sb(n, s, d, st3)
            gpost_b = sb3("gpostb", [128, 1024])
            k.dma(gpost_b[:], A['gpost1'][:].partition_broadcast(128))
            Wg = sb3("Wg", [128, 8, 1024], BF16)
            Wp = sb3("Wp", [128, 2, 1024], BF16)
            Wo = sb3("Wo1", [128, 12, 1024], BF16)
            for half in range(2):
                load_w(k, C, Wo[:, :, half * 512:(half + 1) * 512], A['w_out1'], 0, 12, half * 512, 512, eng='rr')
            for half in range(2):
                load_w(k, C, Wg[:, :, half * 512:(half + 1) * 512], A['pe_gate1'], 0, 8, half * 512, 512, eng='rr')
                load_w(k, C, Wp[:, :, half * 512:(half + 1) * 512], A['pe_proj1'], 0, 2, half * 512, 512, eng='rr')
            PSL = [dict(pT=C.pT[i], pg=C.pa[i], pp=C.psc[i], py=C.po[i]) for i in range(2)]

            def gen_y(T, s_):
                p = s_.ps['py']
                for half in range(2):
                    for fc in range(12):
                        k.mm(p[:], mixT[:, fc, T * 128:(T + 1) * 128], Wo[:, fc, half * 512:(half + 1) * 512],
                             start=(fc == 0), stop=(fc == 11))
                    yield
                    k.cp('act', s_.ytile[:, half * 512:(half + 1) * 512], p[:])
                    yield
            toks = post_block(k, C, sb3, PSL, h_dram, A['p1'], gpost_b, Wg, Wp, out_dram, gen_y)
        k.barrier()
    return toks


def s5_block(k, C, A, sb2, uT, mixT, identf):
    sm = C.sm
    w_in = A['w_in1']
    TWO_PI = 2.0 * math.pi
    I32 = mybir.dt.int32
    W = sb2("W5", [128, 8, 512], BF16)
    suT = sb2("suT", [128, 4, S], BF16)
    ys5 = sb2("ys5", [128, NT, 512])
    rr = sb2("rr", [128, 16])
    thr_ = sb2("thr", [128, 16])
    rots = sb2("rots", [128, 16])
    rotc = sb2("rotc", [128, 16])
    BDT = [sb2("BDTr", [128, 4, 128], BF16), sb2("BDTi", [128, 4, 128], BF16)]
    BDTz = [sb2("BDTzr", [128, 4, 128], BF16), sb2("BDTzi", [128, 4, 128], BF16)]
    CBD = [sb2("CBDr", [128, 16, 2, 16], BF16), sb2("CBDi", [128, 16, 2, 16], BF16)]
    diagd = sb2("diagd", [128, 4, 128], BF16)
    iota = sb2("iota", [128, 512])
    k.dma(iota[:], A['d_iota'][:, :])

    def sincos(dst_sin, dst_cos, src, n, ang, tf, ti):
        for dst, off in ((dst_sin, 0.0), (dst_cos, math.pi / 2)):
            k.ts('dve', ang[:, 0:n], src, off, 1.0 / TWO_PI, ALU.add, ALU.mult)
            k.cp('dve', ti[:, 0:n], ang[:, 0:n])
            k.cp('dve', tf[:, 0:n], ti[:, 0:n])
            k.tt('dve', ang[:, 0:n], ang[:, 0:n], tf[:, 0:n], ALU.subtract)
            k.ts('dve', ang[:, 0:n], ang[:, 0:n], -0.49999, 0.49999, ALU.max, ALU.min)
            k.act(dst, ang[:, 0:n], AF.Sin, scale=TWO_PI)

    with ExitStack() as sts:
        sbs = lambda n, s_, d=F32: k.sb(n, s_, d, sts)
        are = sbs("are", [128, 16])
        aim = sbs("aim", [128, 16])
        dt_ = sbs("dt", [128, 16])
        k.dma(are[:], A['s5_are'][:, :])
        k.dma(aim[:], A['s5_aim'][:, :])
        k.dma(dt_[:], A['s5_logdt'][:, :])
        k.act(dt_[:], dt_[:], AF.Exp)
        th = sbs("th", [128, 16])
        k.tt('dve', rr[:], are[:], dt_[:], ALU.mult)
        k.act(rr[:], rr[:], AF.Exp)
        k.tt('dve', th[:], aim[:], dt_[:], ALU.mult)
        ti = sbs("ti", [128, 16], I32)
        tf = sbs("tf", [128, 16])
        ang = sbs("ang", [128, 16])
        sn = sbs("sn", [128, 16])
        cs = sbs("cs", [128, 16])
        sincos(sn[:], cs[:], th[:], 16, ang, tf, ti)
        k.ts('dve', ang[:, 0:16], th[:], 1.0 / TWO_PI, None, ALU.mult)
        k.cp('dve', ti[:, 0:16], ang[:, 0:16])
        k.cp('dve', tf[:, 0:16], ti[:, 0:16])
        k.stt('dve', thr_[:], tf[:, 0:16], -TWO_PI, th[:], ALU.mult, ALU.add)
        th512 = sbs("th512", [128, 16])
        k.ts('dve', th512[:], thr_[:], 512.0, None, ALU.mult)
        sincos(rots[:], rotc[:], th512[:], 16, ang, tf, ti)
        abr = sbs("abr", [128, 16])
        abi = sbs("abi", [128, 16])
        k.tt('dve', abr[:], rr[:], cs[:], ALU.mult)
        k.tt('dve', abi[:], rr[:], sn[:], ALU.mult)
        lam2 = sbs("lam2", [128, 16])
        t1 = sbs("t1", [128, 16])
        t2 = sbs("t2", [128, 16])
        k.tt('dve', lam2[:], are[:], are[:], ALU.mult)
        k.tt('dve', t1[:], aim[:], aim[:], ALU.mult)
        k.tt('dve', lam2[:], lam2[:], t1[:], ALU.add)
        k.recip(lam2[:], lam2[:])
        am1 = sbs("am1", [128, 16])
        k.ts('dve', am1[:], abr[:], -1.0, None, ALU.add)
        cr = sbs("cr", [128, 16])
        ci = sbs("ci", [128, 16])
        k.tt('dve', t1[:], am1[:], are[:], ALU.mult)
        k.tt('dve', t2[:], abi[:], aim[:], ALU.mult)
        k.tt('dve', cr[:], t1[:], t2[:], ALU.add)
        k.tt('dve', cr[:], cr[:], lam2[:], ALU.mult)
        k.tt('dve', t1[:], abi[:], are[:], ALU.mult)
        k.tt('dve', t2[:], am1[:], aim[:], ALU.mult)
        k.tt('dve', ci[:], t1[:], t2[:], ALU.subtract)
        k.tt('dve', ci[:], ci[:], lam2[:], ALU.mult)
        bre = sbs("bre", [128, 16, 16])
        bim = sbs("bim", [128, 16, 16])
        k.dma(bre[:], A['s5_bre'][:, :, :])
        k.dma(bim[:], A['s5_bim'][:, :, :])
        bbr = sbs("bbr", [128, 16, 16])
        bbi = sbs("bbi", [128, 16, 16])
        tb = sbs("tb", [128, 16, 16])
        crb = cr[:].unsqueeze(2).to_broadcast([128, 16, 16])
        cib = ci[:].unsqueeze(2).to_broadcast([128, 16, 16])
        k.tt('dve', bbr[:], bre[:], crb, ALU.mult)
        k.tt('dve', tb[:], bim[:], cib, ALU.mult)
        k.tt('dve', bbr[:], bbr[:], tb[:], ALU.subtract)
        k.tt('dve', bbi[:], bim[:], crb, ALU.mult)
        k.tt('dve', tb[:], bre[:], cib, ALU.mult)
        k.tt('dve', bbi[:], bbi[:], tb[:], ALU.add)
        halfmask = sbs("halfmask", [128, 2])
        k.dma(halfmask[:], A['d_halfmask'][:, :])
        BD = sbs("BD", [128, 16, 2, 16])
        k.memset('pool', BDTz[0][:], 0.0)
        k.memset('pool', BDTz[1][:], 0.0)
        for ri, src in enumerate((bbr, bbi)):
            for g2 in range(2):
                k.ts('dve', BD[:, :, g2, :], src[:], halfmask[:, g2:g2 + 1], None, ALU.mult)
            for fc in range(4):
                pt = C.pm
                k.tr(pt[:, 0:128], BD[:, 4 * fc:4 * fc + 4, :, :].rearrange("p a b c -> p (a b c)"), identf[:])
                k.cp('act', BDT[ri][:, fc, :], pt[:, 0:128])
                k.cp('act', BDTz[ri][96:128, fc, :], pt[96:128, 0:128])
        cre = sbs("cre", [128, 16, 16])
        cim = sbs("cim", [128, 16, 16])
        k.dma(cre[:], A['s5_cre'][:, :, :])
        k.dma(cim[:], A['s5_cim'][:, :, :])
        for g2 in range(2):
            k.ts('dve', CBD[0][:, :, g2, :], cre[:], halfmask[:, g2:g2 + 1], None, ALU.mult)
            k.ts('dve', CBD[1][:, :, g2, :], cim[:], halfmask[:, g2:g2 + 1], -1.0, ALU.mult, ALU.mult)
        dsk = sbs("dsk", [128, 4])
        k.dma(dsk[:], A['s5_dT'][:, :])
        for fc in range(4):
            k.ts('dve', diagd[:, fc, :], identf[:], dsk[:, fc:fc + 1], None, ALU.mult)
    k.barrier()
    load_w(k, C, W, w_in, 0, 8, CSU, 512)
    for fc in range(4):
        proj_fm(k, C, W, fc * 128, uT, lambda g, p, fc=fc: k.cp('act', suT[:, fc, g * 512:(g + 1) * 512], p[:]))
    with ExitStack() as stl:
        sbl = lambda n, s_, d=F32: k.sb(n, s_, d, stl)

        class Sl:
            pass
        slots = []
        for i in range(2):
            s_ = Sl()
            s_.cosT = sbl(f"cosT{i}", [128, 512])
            s_.sinT = sbl(f"sinT{i}", [128, 512])
            s_.zr = sbl(f"zr{i}", [128, 512])
            s_.zi = sbl(f"zi{i}", [128, 512])
            s_.wr = sbl(f"wr{i}", [128, 512])
            s_.wi = sbl(f"wi{i}", [128, 512])
            s_.ta = sbl(f"ta{i}", [128, 512])
            s_.tb = sbl(f"tb{i}", [128, 512])
            s_.ti = sbl(f"ti{i}", [128, 512], I32)
            s_.xr = sbl(f"xr{i}", [128, 512], BF16)
            s_.xi = sbl(f"xi{i}", [128, 512], BF16)
            s_.ini = sbl(f"ini{i}", [128, 4])
            s_.pbr = C.pa[i]
            s_.pbi = C.psc[i]
            s_.pyy = C.po[i]
            slots.append(s_)
        print("S5 sbuf remaining", k.nc.sbuf_bytes_remaining, flush=True)

        def pair_gen(j, s_):
            fc = j // 4
            pb0 = 32 * (j % 4)
            cosT, sinT, zr, zi, wr, wi, ta, tb, xr, xi_, ini = (s_.cosT, s_.sinT, s_.zr, s_.zi, s_.wr, s_.wi, s_.ta,
                                                             s_.tb, s_.xr, s_.xi, s_.ini)
            k.ts('dve', zr[:], iota[:], thr_[:, j:j + 1], None, ALU.mult)
            sincos(sinT[:], cosT[:], zr[:], 512, ta, tb, s_.ti)
            yield
            pyy3 = s_.pyy[:].rearrange("p (t c) -> p t c", t=NT)
            pbr, pbi = s_.pbr, s_.pbi
            for n in range(4):
                if pb0 < 96:
                    k.mm(pbr[:], BDT[0][pb0:pb0 + 32, fc, :], suT[pb0:pb0 + 32, fc, n * 512:(n + 1) * 512])
                    k.mm(pbi[:], BDT[1][pb0:pb0 + 32, fc, :], suT[pb0:pb0 + 32, fc, n * 512:(n + 1) * 512])
                else:
                    k.mm(pbr[:], BDTz[0][64:128, fc, :], suT[64:128, fc, n * 512:(n + 1) * 512])
                    k.mm(pbi[:], BDTz[1][64:128, fc, :], suT[64:128, fc, n * 512:(n + 1) * 512])
                yield
                k.tt('dve', zr[:], pbr[:], cosT[:], ALU.mult)
                k.tt('dve', ta[:], pbi[:], sinT[:], ALU.mult)
                yield
                k.tt('dve', zr[:], zr[:], ta[:], ALU.add)
                k.tt('dve', zi[:], pbi[:], cosT[:], ALU.mult)
                yield
                k.tt('dve', tb[:], pbr[:], sinT[:], ALU.mult)
                yield
                k.tt('pool', zi[:], zi[:], tb[:], ALU.subtract)
                if n == 0:
                    ir, ii = 0.0, 0.0
                else:
                    k.tt('dve', ini[:, 0:1], wr[:, 511:512], rotc[:, j:j + 1], ALU.mult)
                    k.tt('dve', ini[:, 1:2], wi[:, 511:512], rots[:, j:j + 1], ALU.mult)
                    k.tt('dve', ini[:, 2:3], wr[:, 511:512], rots[:, j:j + 1], ALU.mult)
                    k.tt('dve', ini[:, 3:4], wi[:, 511:512], rotc[:, j:j + 1], ALU.mult)
                    yield
                    k.tt('dve', ini[:, 0:1], ini[:, 0:1], ini[:, 1:2], ALU.subtract)
                    k.tt('dve', ini[:, 2:3], ini[:, 2:3], ini[:, 3:4], ALU.add)
                    ir, ii = ini[:, 0:1], ini[:, 2:3]
                yield
                rb = rr[:, j:j + 1].to_broadcast([128, 512])
                k.scan(wr[:], rb, zr[:], ir, ALU.mult, ALU.add)
                yield
                k.scan(wi[:], rb, zi[:], ii, ALU.mult, ALU.add)
                k.tt('pool', ta[:], wr[:], cosT[:], ALU.mult)
                yield
                k.tt('pool', tb[:], wi[:], sinT[:], ALU.mult)
                yield
                k.tt('pool', xr[:], ta[:], tb[:], ALU.subtract)
                yield
                k.tt('pool', ta[:], wr[:], sinT[:], ALU.mult)
                yield
                k.tt('pool', tb[:], wi[:], cosT[:], ALU.mult)
                yield
                k.tt('pool', xi_[:], ta[:], tb[:], ALU.add)
                yield
                for sub in range(4):
                    T = 4 * n + sub
                    k.mm(pyy3[:, T, :], xr[:, sub * 128:(sub + 1) * 128], CBD[0][:, j, :, :].rearrange("p a b -> p (a b)"),
                         start=True, stop=False)
                    k.mm(pyy3[:, T, :], xi_[:, sub * 128:(sub + 1) * 128], CBD[1][:, j, :, :].rearrange("p a b -> p (a b)"),
                         start=False, stop=True)
                yield
            k.cp('act', ys5[:, :, 32 * j:32 * j + 32], pyy3)
            yield
        for j0 in range(0, 16, 2):
            run_interleaved([pair_gen(j0 + i, slots[i]) for i in range(2)], stagger=5)
    k.barrier()
    Wz = W
    load_w(k, C, Wz, w_in, 0, 8, CSZ, 512)
    W = sb2("W5a", [128, 4, 512], BF16)
    load_w(k, C, W, A['w_glu'], 0, 4, 0, 512)
    W2 = sb2("W5b", [128, 4, 512], BF16)
    load_w(k, C, W2, A['w_glu'], 0, 4, 512, 512)
    class FS:
        pass
    fsl = []
    for i in range(2):
        f_ = FS()
        f_.yg = sb2(f"yg{i}", [128, 512], BF16)
        f_.ygT = sb2(f"ygT{i}", [128, 4, 128], BF16)
        f_.mixb = sb2(f"mixb5{i}", [128, 512], BF16)
        f_.za = sb2(f"za5{i}", [128, 512])
        f_.zb = sb2(f"zb5{i}", [128, 512])
        f_.pd, f_.pz1, f_.pz2, f_.pT = C.pa[i], C.psc[i], C.po[i], C.pT[i]
        fsl.append(f_)

    def fin_gen(T, f_):
        pd = f_.pd
        for fc in range(4):
            k.mm(pd[:, fc * 128:(fc + 1) * 128], suT[:, fc, T * 128:(T + 1) * 128], diagd[:, fc, :])
        yield
        k.tt('dve', ys5[:, T, :], ys5[:, T, :], pd[:], ALU.add)
        yield
        k.act(f_.yg[:], ys5[:, T, :], AF.Gelu_apprx_tanh)
        yield
        pt = f_.pT
        for fc in range(4):
            k.tr(pt[:, fc * 128:(fc + 1) * 128], f_.yg[:, fc * 128:(fc + 1) * 128], C.identb[:])
        yield
        k.cp('dve', f_.ygT[:], pt[:, 0:512].rearrange("p (c t) -> p c t", c=4))
        yield
        for fc in range(4):
            k.mm(f_.pz1[:], f_.ygT[:, fc, :], W[:, fc, 0:512], start=(fc == 0), stop=(fc == 3))
        for fc in range(4):
            k.mm(f_.pz2[:], f_.ygT[:, fc, :], W2[:, fc, 0:512], start=(fc == 0), stop=(fc == 3))
        proj_tm(k, C, Wz, 0, 512, uT, T, pd)
        yield
        k.act(f_.za[:], f_.pz2[:], AF.Tanh, scale=0.5)
        yield
        k.stt('dve', f_.za[:], f_.za[:], 1.0, f_.pz1[:], ALU.add, ALU.mult)
        k.act(f_.zb[:], pd[:], AF.Tanh, scale=0.5)
        yield
        k.stt('dve', f_.zb[:], f_.zb[:], 1.0, pd[:], ALU.add, ALU.mult)
        yield
        k.stt('dve', f_.mixb[:], f_.za[:], 0.25, f_.zb[:], ALU.mult, ALU.mult)
        yield
        for fc in range(4):
            k.tr(pt[:, 512 + fc * 128:512 + (fc + 1) * 128], f_.mixb[:, fc * 128:(fc + 1) * 128], C.identb[:])
        yield
        k.cp('act', mixT[:, 8:12, T * 128:(T + 1) * 128], pt[:, 512:1024].rearrange("p (c t) -> p c t", c=4))
        yield
    for T0 in range(0, NT, 2):
        run_interleaved([fin_gen(T0 + i, fsl[i]) for i in range(2)])


def l1_inputs(inp, b, h_in=None):
    d = {}
    if h_in is not None:
        d['h_in'] = np.ascontiguousarray(h_in)
    d['p1'] = np.ascontiguousarray(inp['p'][1, b])
    d['gpre1T'] = np.ascontiguousarray(inp['norm_pre'][1].reshape(8, 128).T)
    d['gpost1'] = np.ascontiguousarray(inp['norm_post'][1].reshape(1024))
    d['w_in1'] = np.ascontiguousarray(inp['cd_w_in'][0])
    d['w1k'] = np.ascontiguousarray(inp['cmp_w1_k'][0].reshape(32, 64, 128).transpose(1, 0, 2))
    d['w1v'] = np.ascontiguousarray(inp['cmp_w1_v'][0].reshape(32, 64, 128).transpose(1, 0, 2))
    d['w2k'] = np.ascontiguousarray(inp['cmp_w2_k'][0])
    d['w2v'] = np.ascontiguousarray(inp['cmp_w2_v'][0])
    d['poskT'] = np.ascontiguousarray(inp['cmp_pos_k'][0].T)
    d['posvT'] = np.ascontiguousarray(inp['cmp_pos_v'][0].T)

    def gp(a):
        return np.ascontiguousarray(a.reshape(16, 2, 64).transpose(1, 2, 0).reshape(128, 16))
    d['s5_are'] = gp(inp['s5_a_re'][0])
    d['s5_aim'] = gp(inp['s5_a_im'][0])
    d['s5_logdt'] = gp(np.repeat(inp['s5_log_dt'][0][:, None], 64, axis=1))
    gb = lambda a: np.ascontiguousarray(a.reshape(16, 2, 64, 16).transpose(1, 2, 0, 3).reshape(128, 16, 16))
    d['s5_bre'] = gb(inp['s5_b_re'][0])
    d['s5_bim'] = gb(inp['s5_b_im'][0])
    gc = lambda a: np.ascontiguousarray(a.reshape(16, 2, 16, 64).transpose(1, 3, 0, 2).reshape(128, 16, 16))
    d['s5_cre'] = gc(inp['s5_c_re'][0])
    d['s5_cim'] = gc(inp['s5_c_im'][0])
    d['s5_dT'] = np.ascontiguousarray(inp['s5_d'][0].reshape(4, 128).T)
    d['w_glu'] = np.ascontiguousarray(inp['s5_w_glu'][0])
    d['w_out1'] = np.ascontiguousarray(inp['cd_w_out'][0])
    d['pe_gate1'] = np.ascontiguousarray(inp['pe_gate'][1])
    d['pe_proj1'] = np.ascontiguousarray(inp['pe_proj'][1])
    for kk, v in host_consts_l1().items():
        d['d_' + kk] = v
    return d


def build_l1(d):
    nc = bass.Bass("TRN2", target_bir_lowering=False)
    A = declare(nc, d)
    out = nc.dram_tensor("out", [S, D], F32, kind="ExternalOutput").ap()
    with ExitStack() as stack:
        k = KB(nc, stack)
        toks = layer1(k, nc, A, A['h_in'], out)
        k.finish(toks)
        print("L1 inst", k.ninst, "waits", k.nwait, flush=True)
        global LASTLOG
        LASTLOG = k.log
    return nc


FUSED = True


def build_fused(d):
    nc = bass.Bass("TRN2", target_bir_lowering=False)
    A = declare(nc, d)
    out = nc.dram_tensor("out", [S, D], F32, kind="ExternalOutput").ap()
    h1 = nc.dram_tensor("h1_scratch", [S, D], F32, kind="Internal").ap()
    with ExitStack() as stack:
        k = KB(nc, stack)
        mixd = nc.dram_tensor("mix0T", [16, 128, S], BF16, kind="Internal").ap()
        layer0(k, nc, A, A['x'], h1, mixd)
        k.barrier()
        toks = layer1(k, nc, A, h1, out)
        k.finish(toks)
    return nc


def kernel(**inp):
    inp = {kk: np.asarray(v) for kk, v in inp.items()}
    cores = list(range(8))
    if FUSED:
        ds = []
        for b in cores:
            d = l0_inputs(inp, b)
            d.update(l1_inputs(inp, b))
            ds.append(d)
        nc = build_fused(ds[0])
        res = run_bass_kernel_spmd(nc, ds, core_ids=cores)
        return np.stack([res.results[b]["out"] for b in cores], 0).astype(np.float32)
    d0 = [l0_inputs(inp, b) for b in cores]
    nc0 = build_l0(d0[0])
    res0 = run_bass_kernel_spmd(nc0, d0, core_ids=cores)
    d1 = [l1_inputs(inp, b, res0.results[b]["out"]) for b in cores]
    nc1 = build_l1(d1[0])
    res1 = run_bass_kernel_spmd(nc1, d1, core_ids=cores)
    return np.stack([res1.results[b]["out"] for b in cores], 0).astype(np.float32)
```
